# Optimizing a Trainium2 kernel written in Bass

```python
import math
import jax, jax.numpy as jnp
from jax import lax
import numpy as np

D_MODEL = 1024
BATCH = 4
SEQ = 8192
DEPTH = 2

CONV_WIDTH = D_MODEL // 4
CONV_KSIZE = 31
ML_HEADS = 4
ML_HEAD_DIM = D_MODEL // 16
ML_WIDTH = ML_HEADS * ML_HEAD_DIM
ML_QK_CONV = 4
ML_CHUNK = 128
DA_HEADS = 4
DA_QK_DIM = D_MODEL // 16
DA_V_DIM = 2 * DA_QK_DIM
DA_WIDTH = DA_HEADS * DA_V_DIM
Q_BLOCK = 128
D_FF = 4 * D_MODEL
SPLIT_SIZES = (CONV_WIDTH, CONV_WIDTH,
               2 * ML_WIDTH, ML_WIDTH, ML_WIDTH,
               ML_HEADS, ML_HEADS,
               DA_HEADS * 2 * DA_QK_DIM, DA_HEADS * 2 * DA_QK_DIM, DA_WIDTH)
SPLIT_POINTS = tuple(int(p) for p in np.cumsum(SPLIT_SIZES)[:-1])
D_IN_PROJ = sum(SPLIT_SIZES)
DEEPNORM_ALPHA = (2 * DEPTH) ** 0.25
DEEPNORM_BETA = (8 * DEPTH) ** -0.25
LN_EPS = 1e-5

kernel_name = 'hymba_conv_mlstm_diffattn_deepnorm'


def layer_norm(x, g, b=None):
    xf = x.astype(jnp.float32)
    mu = jnp.mean(xf, axis=-1, keepdims=True)
    var = jnp.mean(jnp.square(xf - mu), axis=-1, keepdims=True)
    y = ((xf - mu) * lax.rsqrt(var + LN_EPS)).astype(x.dtype) * g
    return y if b is None else y + b


def rms_norm(x, g):
    xf = x.astype(jnp.float32)
    y = xf * lax.rsqrt(jnp.mean(jnp.square(xf), axis=-1, keepdims=True) + LN_EPS)
    return y.astype(x.dtype) * g


def causal_depthwise_conv(x, w, b):
    ksize, ch = w.shape
    y = lax.conv_general_dilated(x, w[:, None, :].astype(x.dtype), window_strides=(1,),
                                 padding=((ksize - 1, 0),),
                                 dimension_numbers=('NWC', 'WIO', 'NWC'),
                                 feature_group_count=ch)
    return y + b


def mlstm_chunkwise(q, k, v, log_i, log_f):
    bsz, nh, s, dh = q.shape
    nc = s // ML_CHUNK
    f32 = jnp.float32
    q = q.astype(f32).reshape(bsz, nh, nc, ML_CHUNK, dh)
    k = (k.astype(f32) * dh ** -0.5).reshape(bsz, nh, nc, ML_CHUNK, dh)
    v = v.astype(f32).reshape(bsz, nh, nc, ML_CHUNK, dh)
    log_i = log_i.astype(f32).reshape(bsz, nh, nc, ML_CHUNK)
    log_f = log_f.astype(f32).reshape(bsz, nh, nc, ML_CHUNK)
    b = jnp.cumsum(log_f, axis=-1)
    g = b[..., -1]
    w_end = g[..., None] - b + log_i
    m_loc = jnp.max(w_end, axis=-1)
    p_end = jnp.exp(w_end - m_loc[..., None])
    c_loc = jnp.einsum('bhcsv,bhcsk->bhcvk', p_end[..., None] * v, k)
    n_loc = jnp.einsum('bhcs,bhcsk->bhck', p_end, k)

    def step(carry, xs):
        c, n, m = carry
        g_c, m_c, c_c, n_c = xs
        m_new = jnp.maximum(g_c + m, m_c)
        a = jnp.exp(g_c + m - m_new)
        e = jnp.exp(m_c - m_new)
        c_new = a[..., None, None] * c + e[..., None, None] * c_c
        n_new = a[..., None] * n + e[..., None] * n_c
        return (c_new, n_new, m_new), (c, n, m)

    init = (jnp.zeros((bsz, nh, dh, dh), f32), jnp.zeros((bsz, nh, dh), f32),
            jnp.zeros((bsz, nh), f32))
    xs = (jnp.moveaxis(g, 2, 0), jnp.moveaxis(m_loc, 2, 0),
          jnp.moveaxis(c_loc, 2, 0), jnp.moveaxis(n_loc, 2, 0))
    _, (c_prev, n_prev, m_prev) = lax.scan(step, init, xs)
    c_prev = jnp.moveaxis(c_prev, 0, 2)
    n_prev = jnp.moveaxis(n_prev, 0, 2)
    m_prev = jnp.moveaxis(m_prev, 0, 2)

    causal = jnp.tril(jnp.ones((ML_CHUNK, ML_CHUNK), dtype=bool))
    d = jnp.where(causal, b[..., :, None] - b[..., None, :] + log_i[..., None, :], -jnp.inf)
    inter = b + m_prev[..., None]
    m_pos = jnp.maximum(inter, jnp.max(d, axis=-1))
    a_inter = jnp.exp(inter - m_pos)
    sc = jnp.einsum('bhcjd,bhcsd->bhcjs', q, k) * jnp.exp(d - m_pos[..., None])
    num = (a_inter[..., None] * jnp.einsum('bhcvk,bhcjk->bhcjv', c_prev, q)
           + jnp.einsum('bhcjs,bhcsv->bhcjv', sc, v))
    den = a_inter * jnp.einsum('bhck,bhcjk->bhcj', n_prev, q) + jnp.sum(sc, axis=-1)
    h = num / jnp.maximum(jnp.abs(den), jnp.exp(-m_pos))[..., None]
    return h.reshape(bsz, nh, s, dh)


def diff_attention(q, k, v, lam, norm_g):
    bsz, s, nh, _, dk = q.shape
    dv = v.shape[-1]
    nqb = s // Q_BLOCK
    scale = dk ** -0.5
    kh = k.transpose(0, 2, 3, 1, 4)
    vh = v.transpose(0, 2, 1, 3)
    qb = q.transpose(0, 2, 3, 1, 4).reshape(bsz, nh, 2, nqb, Q_BLOCK, dk).transpose(3, 0, 1, 2, 4, 5)
    kpos = jnp.arange(s)

    def block(args):
        q_blk, i = args
        sc = jnp.einsum('bhmqd,bhmkd->bhmqk', q_blk, kh).astype(jnp.float32) * scale
        qpos = i * Q_BLOCK + jnp.arange(Q_BLOCK)
        sc = jnp.where(kpos[None, :] <= qpos[:, None], sc, -jnp.inf)
        p = jax.nn.softmax(sc, axis=-1)
        w = p[:, :, 0] - lam * p[:, :, 1]
        return jnp.einsum('bhqk,bhkv->bhqv', w.astype(vh.dtype), vh)

    o = lax.map(block, (qb, jnp.arange(nqb)))
    o = o.transpose(1, 0, 3, 2, 4).reshape(bsz, s, nh, dv)
    return rms_norm(o, norm_g)


def hybrid_mixer(x, w_in, b_igate, b_fgate, conv_dw_w, conv_dw_b, conv_ln_g, conv_ln_b,
                 conv_pw_w, conv_pw_b, ml_conv_w, ml_conv_b, ml_norm_g,
                 lam_q1, lam_k1, lam_q2, lam_k2, da_norm_g, w_out, lambda_init):
    bsz, s, _ = x.shape
    u = x @ w_in
    (c_a, c_g, m_qk, m_v, m_o, m_i, m_f, d_q, d_k, d_v) = jnp.split(u, SPLIT_POINTS, axis=-1)

    y = c_a * jax.nn.sigmoid(c_g)
    y = causal_depthwise_conv(y, conv_dw_w, conv_dw_b)
    y = jax.nn.silu(layer_norm(y, conv_ln_g, conv_ln_b))
    y_conv = y @ conv_pw_w + conv_pw_b

    qk = jax.nn.silu(causal_depthwise_conv(m_qk, ml_conv_w, ml_conv_b))
    q_m, k_m = jnp.split(qk, 2, axis=-1)
    to_heads = lambda t: t.reshape(bsz, s, ML_HEADS, ML_HEAD_DIM).transpose(0, 2, 1, 3)
    log_i = (m_i + b_igate).astype(jnp.float32).transpose(0, 2, 1)
    log_f = jax.nn.log_sigmoid((m_f + b_fgate).astype(jnp.float32)).transpose(0, 2, 1)
    h = mlstm_chunkwise(to_heads(q_m), to_heads(k_m), to_heads(m_v), log_i, log_f)
    h = h.transpose(0, 2, 1, 3).astype(x.dtype) * jax.nn.sigmoid(m_o).reshape(bsz, s, ML_HEADS, ML_HEAD_DIM)
    y_ml = layer_norm(h, ml_norm_g.reshape(ML_HEADS, ML_HEAD_DIM)).reshape(bsz, s, ML_WIDTH)

    lam = (jnp.exp(jnp.sum(lam_q1.astype(jnp.float32) * lam_k1.astype(jnp.float32)))
           - jnp.exp(jnp.sum(lam_q2.astype(jnp.float32) * lam_k2.astype(jnp.float32)))
           + lambda_init)
    y_da = diff_attention(d_q.reshape(bsz, s, DA_HEADS, 2, DA_QK_DIM),
                          d_k.reshape(bsz, s, DA_HEADS, 2, DA_QK_DIM),
                          d_v.reshape(bsz, s, DA_HEADS, DA_V_DIM), lam, da_norm_g)
    y_da = (y_da * (1.0 - lambda_init)).reshape(bsz, s, DA_WIDTH)

    return jnp.concatenate([y_conv, y_ml, y_da], axis=-1) @ w_out


def setup_inputs(seed: int = 0) -> dict:
    key = jax.random.key(seed)
    ks = jax.random.split(key, 26)
    L = DEPTH
    nrm = lambda k, shape, scale: jax.random.normal(k, shape, jnp.float32) * scale
    x = nrm(ks[0], (BATCH, SEQ, D_MODEL), 1.0)
    w_in = nrm(ks[1], (L, D_MODEL, D_IN_PROJ), D_MODEL ** -0.5)
    b_igate = nrm(ks[2], (L, ML_HEADS), 0.1)
    b_fgate = (jnp.broadcast_to(jnp.linspace(3.0, 6.0, ML_HEADS, dtype=jnp.float32), (L, ML_HEADS))
               + nrm(ks[3], (L, ML_HEADS), 0.01))
    conv_dw_w = nrm(ks[4], (L, CONV_KSIZE, CONV_WIDTH), CONV_KSIZE ** -0.5)
    conv_dw_b = nrm(ks[5], (L, CONV_WIDTH), 0.01)
    conv_ln_g = 1.0 + nrm(ks[6], (L, CONV_WIDTH), 0.01)
    conv_ln_b = nrm(ks[7], (L, CONV_WIDTH), 0.01)
    conv_pw_w = nrm(ks[8], (L, CONV_WIDTH, CONV_WIDTH), CONV_WIDTH ** -0.5 * DEEPNORM_BETA)
    conv_pw_b = nrm(ks[9], (L, CONV_WIDTH), 0.01)
    ml_conv_w = nrm(ks[10], (L, ML_QK_CONV, 2 * ML_WIDTH), ML_QK_CONV ** -0.5)
    ml_conv_b = nrm(ks[11], (L, 2 * ML_WIDTH), 0.01)
    ml_norm_g = 1.0 + nrm(ks[12], (L, ML_WIDTH), 0.01)
    lam_q1 = nrm(ks[13], (L, DA_QK_DIM), 0.1)
    lam_k1 = nrm(ks[14], (L, DA_QK_DIM), 0.1)
    lam_q2 = nrm(ks[15], (L, DA_QK_DIM), 0.1)
    lam_k2 = nrm(ks[16], (L, DA_QK_DIM), 0.1)
    da_norm_g = 1.0 + nrm(ks[17], (L, DA_V_DIM), 0.01)
    w_out = nrm(ks[18], (L, D_MODEL, D_MODEL), D_MODEL ** -0.5 * DEEPNORM_BETA)
    ln1_g = 1.0 + nrm(ks[19], (L, D_MODEL), 0.01)
    ln1_b = nrm(ks[20], (L, D_MODEL), 0.01)
    w_up = nrm(ks[21], (L, D_MODEL, D_FF), D_MODEL ** -0.5 * DEEPNORM_BETA)
    w_down = nrm(ks[22], (L, D_FF, D_MODEL), D_FF ** -0.5 * DEEPNORM_BETA)
    ln2_g = 1.0 + nrm(ks[23], (L, D_MODEL), 0.01)
    ln2_b = nrm(ks[24], (L, D_MODEL), 0.01)
    return {'x': x, 'w_in': w_in, 'b_igate': b_igate, 'b_fgate': b_fgate,
            'conv_dw_w': conv_dw_w, 'conv_dw_b': conv_dw_b, 'conv_ln_g': conv_ln_g,
            'conv_ln_b': conv_ln_b, 'conv_pw_w': conv_pw_w, 'conv_pw_b': conv_pw_b,
            'ml_conv_w': ml_conv_w, 'ml_conv_b': ml_conv_b, 'ml_norm_g': ml_norm_g,
            'lam_q1': lam_q1, 'lam_k1': lam_k1, 'lam_q2': lam_q2, 'lam_k2': lam_k2,
            'da_norm_g': da_norm_g, 'w_out': w_out, 'ln1_g': ln1_g, 'ln1_b': ln1_b,
            'w_up': w_up, 'w_down': w_down, 'ln2_g': ln2_g, 'ln2_b': ln2_b}


def reference(x, w_in, b_igate, b_fgate, conv_dw_w, conv_dw_b, conv_ln_g, conv_ln_b,
              conv_pw_w, conv_pw_b, ml_conv_w, ml_conv_b, ml_norm_g,
              lam_q1, lam_k1, lam_q2, lam_k2, da_norm_g, w_out, ln1_g, ln1_b,
              w_up, w_down, ln2_g, ln2_b):
    for l in range(DEPTH):
        lambda_init = 0.8 - 0.6 * math.exp(-0.3 * l)
        h = hybrid_mixer(x, w_in[l], b_igate[l], b_fgate[l], conv_dw_w[l], conv_dw_b[l],
                         conv_ln_g[l], conv_ln_b[l], conv_pw_w[l], conv_pw_b[l],
                         ml_conv_w[l], ml_conv_b[l], ml_norm_g[l],
                         lam_q1[l], lam_k1[l], lam_q2[l], lam_k2[l], da_norm_g[l],
                         w_out[l], lambda_init)
        x = layer_norm(DEEPNORM_ALPHA * x + h, ln1_g[l], ln1_b[l])
        h = jnp.square(jax.nn.relu(x @ w_up[l])) @ w_down[l]
        x = layer_norm(DEEPNORM_ALPHA * x + h, ln2_g[l], ln2_b[l])
    return x
```

```python
import math
from contextlib import ExitStack
import numpy as np
import concourse.bass as bass
import concourse.mybir as mybir
from concourse.bass_utils import run_bass_kernel_spmd

F32 = mybir.dt.float32
BF16 = mybir.dt.bfloat16
AF = mybir.ActivationFunctionType
ALU = mybir.AluOpType

D = 1024
S = 8192
TO = 4096
TA = 8192
DIN = 3080
DFF = 4096
DEPTH = 2
ALPHA = (2 * DEPTH) ** 0.25
EPS = 1e-5
NEG = -30000.0
NDS = 48
FORCE_Q = None


class Buf:
    __slots__ = ("w", "r")

    def __init__(self):
        self.w = None
        self.r = {}


class KB:
    def __init__(self, nc, es):
        self.nc = nc
        self.E = {"pe": nc.tensor, "act": nc.scalar, "dve": nc.vector, "pool": nc.gpsimd, "sp": nc.sync}
        self.sem = {}
        self.cnt = {}
        for e in ("pe", "act", "dve", "pool"):
            self.sem[e] = es.enter_context(nc.semaphore("s_" + e))
            self.cnt[e] = 0
        self.dsem = [es.enter_context(nc.semaphore("d%d" % i)) for i in range(NDS)]
        self.dcnt = [0] * NDS
        self.dnext = 0
        self.seen = {}
        self.ninst = 0

    def _toks(self, reads, writes):
        toks = []
        for b in reads:
            if b.w is not None:
                toks.append(b.w)
        for b in writes:
            if b.w is not None:
                toks.append(b.w)
            toks.extend(b.r.items())
        return toks

    def _wait(self, e, toks):
        need = {}
        for k, v in toks:
            if k == "pe" and e == "pe":
                continue
            if self.seen.get((e, k), 0) >= v:
                continue
            if need.get(k, 0) < v:
                need[k] = v
        for k, v in need.items():
            sem = self.sem[k] if isinstance(k, str) else self.dsem[k]
            self.E[e].wait_ge(sem, v)
            self.seen[(e, k)] = v
            self.ninst += 1

    def _commit(self, tok, reads, writes):
        k, v = tok
        for b in reads:
            if b.r.get(k, 0) < v:
                b.r[k] = v
        for b in writes:
            b.w = tok
            b.r = {}

    def op(self, e, fn, reads=(), writes=(), sig=True):
        self._wait(e, self._toks(reads, writes))
        inst = fn(self.E[e])
        self.ninst += 1
        if sig:
            self.cnt[e] += 1
            inst.then_inc(self.sem[e], 1)
            tok = (e, self.cnt[e])
        else:
            tok = (e, self.cnt[e] + 1)
        self._commit(tok, reads, writes)
        return tok

    def dma(self, out, in_, reads=(), writes=(), q="sp"):
        if FORCE_Q is not None:
            q = FORCE_Q
        self._wait(q, self._toks(reads, writes))
        j = self.dnext
        self.dnext = (j + 1) % NDS
        self.dcnt[j] += 16
        self.E[q].dma_start(out=out, in_=in_).then_inc(self.dsem[j], 16)
        self.ninst += 1
        tok = (j, self.dcnt[j])
        self._commit(tok, reads, writes)
        return tok

    def barrier(self):
        toks = [(e, self.cnt[e]) for e in ("pe", "act", "dve", "pool") if self.cnt[e] > 0]
        toks += [(j, self.dcnt[j]) for j in range(NDS) if self.dcnt[j] > 0]
        for e in ("pe", "act", "dve", "pool", "sp"):
            need = []
            for k, v in toks:
                if self.seen.get((e, k), 0) < v:
                    need.append((k, v))
            for k, v in need:
                sem = self.sem[k] if isinstance(k, str) else self.dsem[k]
                self.E[e].wait_ge(sem, v)
                self.seen[(e, k)] = v


class Ctx:
    pass


class RowChunks:
    def __init__(self, handles, ch):
        self.h = handles
        self.ch = ch

    def __getitem__(self, idx):
        rs, cs = idx
        j = rs.start // self.ch
        a = rs.start - j * self.ch
        b = rs.stop - j * self.ch
        assert 0 <= a < b <= self.ch
        return self.h[j].ap()[a:b, cs]


_UID = [0]


def sb(es, nc, name, shape, dt):
    _UID[0] += 1
    return es.enter_context(nc.sbuf_tensor("%s_u%d" % (name, _UID[0]), shape, dt))


def psb(es, nc, name, shape, dt):
    _UID[0] += 1
    return es.enter_context(nc.psum_tensor("%s_u%d" % (name, _UID[0]), shape, dt))


def load_cast(k, es, nc, name, dst, dstbuf, src3, ncols, piece, engines=("dve", "pool")):
    KC = dst.shape[1]
    st = [sb(es, nc, "%s_st%d" % (name, i), [128, KC, piece], F32) for i in range(2)]
    stb = [Buf(), Buf()]
    i = 0
    c0 = 0
    while c0 < ncols:
        w = min(piece, ncols - c0)
        s = st[i % 2]
        k.dma(s[:, :, 0:w], src3[:, :, c0:c0 + w], writes=[stb[i % 2]])
        e = engines[i % len(engines)]
        k.op(e, lambda eng, s=s, c0=c0, w=w: eng.tensor_copy(dst[:, :, c0:c0 + w], s[:, :, 0:w]),
             reads=[stb[i % 2]], writes=[dstbuf])
        c0 += w
        i += 1


class Prefetch:
    def __init__(self, k, es, nc, name, src3, KC, ncols, piece, engines):
        self.k = k
        self.src3 = src3
        self.piece = piece
        self.ncols = ncols
        self.engines = engines
        self.dst = sb(es, nc, name, [128, KC, ncols], BF16)
        self.st = [sb(es, nc, "%s_st%d" % (name, i), [128, KC, piece], F32) for i in range(2)]
        self.stb = [Buf(), Buf()]
        self.npieces = (ncols + piece - 1) // piece
        self.bufs = [Buf() for _ in range(self.npieces)]
        self.i = 0

    def step(self, n=1):
        k = self.k
        for _ in range(n):
            if self.i >= self.npieces:
                return
            i = self.i
            self.i += 1
            c0 = i * self.piece
            w = min(self.piece, self.ncols - c0)
            s_ = self.st[i % 2]
            k.dma(s_[:, :, 0:w], self.src3[:, :, c0:c0 + w], writes=[self.stb[i % 2]])
            e = self.engines[i % len(self.engines)]
            k.op(e, lambda eng: eng.tensor_copy(self.dst[:, :, c0:c0 + w], s_[:, :, 0:w]),
                 reads=[self.stb[i % 2]], writes=[self.bufs[i]])

    def finish(self):
        self.step(self.npieces)


def phase0(k, nc, C, src_list, hook=None):
    with ExitStack() as es:
        NBF = 4
        xin = [sb(es, nc, "p0_x%d" % i, [128, D], F32) for i in range(NBF)]
        xinb = [Buf() for _ in range(NBF)]
        xo = [sb(es, nc, "p0_o%d" % i, [128, 8, 512], BF16) for i in range(2)]
        xob = [Buf(), Buf()]
        ps = [psb(es, nc, "p0_ps%d" % i, [128, 512], F32) for i in range(8)]
        psbuf = [Buf() for _ in range(8)]
        it = 0
        for (src, nrows, toff) in src_list:
            for blk in range(nrows // 128):
                xi = xin[it % NBF]
                g = (it // 4) % 2
                b4 = it % 4
                k.dma(xi[:, :], src[blk * 128:(blk + 1) * 128, :], writes=[xinb[it % NBF]])
                for hf in range(2):
                    pi = (it * 2 + hf) % 8
                    for j in range(4):
                        kc = hf * 4 + j
                        k.op("pe", lambda e: e.transpose(ps[pi][:, j * 128:(j + 1) * 128], xi[:, kc * 128:(kc + 1) * 128], C.ident[:, :]),
                             reads=[xinb[it % NBF], C.identb], writes=[psbuf[pi]], sig=(j == 3))
                    dst = xo[g][:, hf * 4:(hf + 1) * 4, b4 * 128:(b4 + 1) * 128]
                    srcp = ps[pi][:, :].rearrange("p (a b) -> p a b", a=4)
                    if hf == 0:
                        k.op("act", lambda e: e.copy(dst, srcp), reads=[psbuf[pi]], writes=[xob[g]])
                    else:
                        k.op("dve", lambda e: e.tensor_copy(dst, srcp), reads=[psbuf[pi]], writes=[xob[g]])
                if b4 == 3:
                    t0 = toff + (blk - 3) * 128
                    k.dma(C.xT_all[:, :, t0:t0 + 512], xo[g][:, :, :], reads=[xob[g]], writes=[C.xT_buf], q="pool")
                it += 1
                if hook is not None:
                    hook(it)
    k.barrier()


def phase1(k, nc, C, L, PW):
    with ExitStack() as es:
        PW.finish()
        win = PW.dst
        k._wait("pe", [b.w for b in PW.bufs if b.w is not None])
        winb = Buf()
        gb = sb(es, nc, "p1_gb", [128, 8], F32)
        gbb = Buf()
        k.dma(gb[:, 0:4], L.b_igate.partition_broadcast(128), writes=[gbb])
        k.dma(gb[:, 4:8], L.b_fgate.partition_broadcast(128), writes=[gbb])
        xt = [sb(es, nc, "p1_xt%d" % i, [128, 8, 512], BF16) for i in range(2)]
        xtb = [Buf(), Buf()]
        NST = 4
        stg = [sb(es, nc, "p1_stg%d" % i, [128, 512], BF16) for i in range(NST)]
        stgb = [Buf() for _ in range(NST)]
        sg = [sb(es, nc, "p1_sg%d" % i, [128, 512], BF16) for i in range(2)]
        sgb = [Buf(), Buf()]
        tst = [sb(es, nc, "p1_tst%d" % i, [128, 512], BF16) for i in range(NST)]
        tstb = [Buf() for _ in range(NST)]
        gst = [sb(es, nc, "p1_gst%d" % i, [128, 8], F32) for i in range(2)]
        gstb = [Buf(), Buf()]
        ps = [psb(es, nc, "p1_ps%d" % i, [128, 512], F32) for i in range(8)]
        psbuf = [Buf() for _ in range(8)]
        st = {"ps": 0, "stg": 0, "tst": 0, "ev": 0, "sg": 0, "gst": 0}

        def mm_feat(xtile, xbuf, col0):
            pi = st["ps"] % 8
            st["ps"] += 1
            for kc in range(8):
                k.op("pe", lambda e, kc=kc, pi=pi: e.matmul(ps[pi][:, :], win[:, kc, col0:col0 + 128],
                                                           xtile[:, kc, :], start=(kc == 0), stop=(kc == 7)),
                     reads=[winb, xbuf], writes=[psbuf[pi]], sig=(kc == 7))
            return pi

        def evac_copy(pi, dst_sb, dst_buf):
            eng = "act" if st["ev"] % 2 == 0 else "dve"
            st["ev"] += 1
            if eng == "act":
                k.op("act", lambda e: e.copy(dst_sb, ps[pi][:, 0:dst_sb.shape[1]]), reads=[psbuf[pi]], writes=[dst_buf])
            else:
                k.op("dve", lambda e: e.tensor_copy(dst_sb, ps[pi][:, 0:dst_sb.shape[1]]), reads=[psbuf[pi]],
                     writes=[dst_buf])

        for ti in range(16):
            own = ti >= 8
            lastprev = ti == 7
            t0 = ti * 512
            xtile = xt[ti % 2]
            xbuf = xtb[ti % 2]
            k.dma(xtile[:, :, :], C.xT_all[:, :, t0:t0 + 512], reads=[C.xT_buf], writes=[xbuf])
            if own or lastprev:
                gcol = t0 - 7 * 512
                for j in range(2):
                    pa = mm_feat(xtile, xbuf, j * 128)
                    pg = mm_feat(xtile, xbuf, 256 + j * 128)
                    si = st["sg"] % 2
                    st["sg"] += 1
                    k.op("act", lambda e, si=si, pg=pg: e.activation(out=sg[si][:, :], in_=ps[pg][:, :], func=AF.Sigmoid),
                         reads=[psbuf[pg]], writes=[sgb[si]])
                    oi = st["stg"] % NST
                    st["stg"] += 1
                    k.op("dve", lambda e, si=si, pa=pa, oi=oi: e.tensor_tensor(out=stg[oi][:, :], in0=ps[pa][:, :],
                                                                             in1=sg[si][:, :], op=ALU.mult),
                         reads=[psbuf[pa], sgb[si]], writes=[stgb[oi]])
                    k.dma(C.glu[:, j, gcol:gcol + 512], stg[oi][:, :], reads=[stgb[oi]], writes=[C.glu_buf], q="pool")
            feats = []
            for j in range(4):
                if own or lastprev or j >= 2:
                    feats.append((512 + j * 128, C.mqk[:, j, t0:t0 + 512], C.mqk_buf))
            if own:
                for j in range(4):
                    feats.append((1544 + j * 128, C.dq[:, j, t0 - TO:t0 - TO + 512], C.dq_buf))
            for j in range(4):
                feats.append((2056 + j * 128, C.dk[:, j, t0:t0 + 512], C.dk_buf))
            for (col0, dst, dbuf) in feats:
                pi = mm_feat(xtile, xbuf, col0)
                oi = st["stg"] % NST
                st["stg"] += 1
                evac_copy(pi, stg[oi][:, :], stgb[oi])
                k.dma(dst, stg[oi][:, :], reads=[stgb[oi]], writes=[dbuf], q="pool")
            for blk in range(4):
                tb = t0 + blk * 128
                pi = st["ps"] % 8
                st["ps"] += 1
                for kc in range(8):
                    k.op("pe", lambda e, kc=kc, pi=pi: e.matmul(ps[pi][:, :], xtile[:, kc, blk * 128:(blk + 1) * 128],
                                                               win[:, kc, 1024:1536], start=(kc == 0), stop=(kc == 7)),
                         reads=[winb, xbuf], writes=[psbuf[pi]], sig=(kc == 7))
                oi = st["tst"] % NST
                st["tst"] += 1
                k.op("dve", lambda e, pi=pi, oi=oi: e.tensor_copy(tst[oi][:, 0:256], ps[pi][:, 0:256]),
                     reads=[psbuf[pi]], writes=[tstb[oi]])
                if own:
                    k.op("act", lambda e, pi=pi, oi=oi: e.activation(out=tst[oi][:, 256:512], in_=ps[pi][:, 256:512],
                                                                    func=AF.Sigmoid),
                         reads=[psbuf[pi]], writes=[tstb[oi]])
                k.dma(C.mv[tb:tb + 128, :], tst[oi][:, 0:256], reads=[tstb[oi]], writes=[C.mv_buf], q="pool")
                if own:
                    k.dma(C.mo[tb - TO:tb - TO + 128, :], tst[oi][:, 256:512], reads=[tstb[oi]], writes=[C.mo_buf],
                          q="pool")
                pi = st["ps"] % 8
                st["ps"] += 1
                for kc in range(8):
                    k.op("pe", lambda e, kc=kc, pi=pi: e.matmul(ps[pi][:, 0:8], xtile[:, kc, blk * 128:(blk + 1) * 128],
                                                               win[:, kc, 1536:1544], start=(kc == 0), stop=(kc == 7)),
                         reads=[winb, xbuf], writes=[psbuf[pi]], sig=(kc == 7))
                gi = st["gst"] % 2
                st["gst"] += 1
                k.op("dve", lambda e, pi=pi, gi=gi: e.tensor_tensor(out=gst[gi][:, :], in0=ps[pi][:, 0:8], in1=gb[:, :],
                                                                   op=ALU.add),
                     reads=[psbuf[pi], gbb], writes=[gstb[gi]])
                k.dma(C.gates[tb:tb + 128, :], gst[gi][:, :], reads=[gstb[gi]], writes=[C.gates_buf], q="pool")
                pi = st["ps"] % 8
                st["ps"] += 1
                for kc in range(8):
                    k.op("pe", lambda e, kc=kc, pi=pi: e.matmul(ps[pi][:, :], xtile[:, kc, blk * 128:(blk + 1) * 128],
                                                               win[:, kc, 2568:3080], start=(kc == 0), stop=(kc == 7)),
                         reads=[winb, xbuf], writes=[psbuf[pi]], sig=(kc == 7))
                oi = st["tst"] % NST
                st["tst"] += 1
                evac_copy(pi, tst[oi][:, :], tstb[oi])
                k.dma(C.dv[tb:tb + 128, :], tst[oi][:, :], reads=[tstb[oi]], writes=[C.dv_buf], q="pool")
    k.barrier()


def rstd_from(k, C, out_ap, in_ap, bufs_r, bufs_w, tmp_ap, tmpbuf):
    k.op("act", lambda e: e.activation(out=tmp_ap, in_=in_ap, func=AF.Ln), reads=bufs_r, writes=[tmpbuf])
    k.op("act", lambda e: e.activation(out=out_ap, in_=tmp_ap, func=AF.Exp, scale=-0.5), reads=[tmpbuf], writes=bufs_w)


CP_DWB, CP_LNG, CP_LNB, CP_PWB, CP_MLB, CP_DWW, CP_MLW = 0, 2, 4, 6, 8, 12, 74
NCP = 90


def phase2(k, nc, C, L):
    with ExitStack() as es:
        glu = sb(es, nc, "p2_glu", [128, 2, 512 + TO], BF16)
        glub = Buf()
        for j in range(2):
            k.dma(glu[:, j, :], C.glu[:, j, :], reads=[C.glu_buf], writes=[glub])
        k.op("dve", lambda e: e.tensor_scalar(glu[:, :, 482:512], glu[:, :, 482:512], C.flag[:, 0:1], None, ALU.mult),
             reads=[glub, C.flagb], writes=[glub])
        cp = sb(es, nc, "p2_cp", [128, NCP], F32)
        cpb = Buf()
        k.dma(cp[:, :], L.colpack[:, :], writes=[cpb])
        dg = sb(es, nc, "p2_dg", [128, 62, 128], BF16)
        dgb = Buf()
        for i in range(62):
            k.op("dve" if i % 2 == 0 else "pool",
                 lambda e: e.tensor_scalar(dg[:, i, :], C.identbf[:, :], cp[:, CP_DWW + i:CP_DWW + i + 1], None, ALU.mult),
                 reads=[C.identbfb, cpb], writes=[dgb])
        pw = sb(es, nc, "p2_pw", [128, 2, 256], BF16)
        pwb = Buf()
        load_cast(k, es, nc, "p2pw", pw, pwb, L.conv_pw_w.rearrange("(kc p) n -> p kc n", p=128), 256, 256)
        ones = sb(es, nc, "p2_ones", [128, 128], F32)
        onesb = Buf()
        k.op("dve", lambda e: e.memset(ones[:, :], 1.0), writes=[onesb])
        yb = [[sb(es, nc, "p2_y%d_%d" % (p, j), [128, 512], F32) for j in range(2)] for p in range(2)]
        ybb = [[Buf(), Buf()] for _ in range(2)]
        sq = [[sb(es, nc, "p2_sq%d_%d" % (p, j), [128, 512], F32) for j in range(2)] for p in range(2)]
        sqb = [[Buf(), Buf()] for _ in range(2)]
        msb = sb(es, nc, "p2_ms", [128, 512], F32); msbb = Buf()
        tt = sb(es, nc, "p2_tt", [128, 512], F32); ttb = Buf()
        var = sb(es, nc, "p2_var", [128, 512], F32); varb = Buf()
        lnt = sb(es, nc, "p2_lnt", [128, 512], F32); lntb = Buf()
        rstd = sb(es, nc, "p2_rstd", [128, 512], F32); rstdb = Buf()
        dd = [sb(es, nc, "p2_d%d" % j, [128, 512], F32) for j in range(2)]
        ddb = [Buf(), Buf()]
        act = [sb(es, nc, "p2_a%d" % j, [128, 512], BF16) for j in range(2)]
        actb = [Buf(), Buf()]
        og = [sb(es, nc, "p2_o%d" % j, [128, 512], BF16) for j in range(2)]
        ogb = [Buf(), Buf()]
        ps = [psb(es, nc, "p2_ps%d" % i, [128, 512], F32) for i in range(6)]
        pb = [Buf() for _ in range(6)]

        def emit_conv(ti):
            p = ti % 2
            c0 = 512 + ti * 512
            for j in range(2):
                for t in range(31):
                    k.op("pe", lambda e: e.matmul(ps[j][:, :], dg[:, j * 31 + t, :], glu[:, j, c0 - 30 + t:c0 - 30 + t + 512],
                                                  start=(t == 0), stop=(t == 30)),
                         reads=[dgb, glub], writes=[pb[j]], sig=(t == 30))
                k.op("act", lambda e: e.activation(out=yb[p][j][:, :], in_=ps[j][:, :], func=AF.Identity,
                                                   bias=cp[:, CP_DWB + j:CP_DWB + j + 1]),
                     reads=[pb[j], cpb], writes=[ybb[p][j]])
                k.op("act", lambda e: e.activation(out=sq[p][j][:, :], in_=ps[j][:, :], func=AF.Square,
                                                   bias=cp[:, CP_DWB + j:CP_DWB + j + 1]),
                     reads=[pb[j], cpb], writes=[sqb[p][j]])

        def emit_stats_mm(ti):
            p = ti % 2
            for j in range(2):
                k.op("pe", lambda e: e.matmul(ps[2][:, :], ones[:, :], yb[p][j][:, :], start=(j == 0), stop=(j == 1)),
                     reads=[onesb, ybb[p][j]], writes=[pb[2]], sig=(j == 1))
            for j in range(2):
                k.op("pe", lambda e: e.matmul(ps[3][:, :], ones[:, :], sq[p][j][:, :], start=(j == 0), stop=(j == 1)),
                     reads=[onesb, sqb[p][j]], writes=[pb[3]], sig=(j == 1))

        def emit_ln(ti):
            p = ti % 2
            k.op("act", lambda e: e.activation(out=msb[:, :], in_=ps[2][:, :], func=AF.Copy, scale=1.0 / 256),
                 reads=[pb[2]], writes=[msbb])
            k.op("dve", lambda e: e.tensor_tensor(out=tt[:, :], in0=msb[:, :], in1=msb[:, :], op=ALU.mult),
                 reads=[msbb], writes=[ttb])
            k.op("dve", lambda e: e.scalar_tensor_tensor(out=var[:, :], in0=ps[3][:, :], scalar=1.0 / 256, in1=tt[:, :],
                                                         op0=ALU.mult, op1=ALU.subtract),
                 reads=[pb[3], ttb], writes=[varb])
            k.op("dve", lambda e: e.tensor_scalar(var[:, :], var[:, :], EPS, None, ALU.add), reads=[varb], writes=[varb])
            rstd_from(k, C, rstd[:, :], var[:, :], [varb], [rstdb], lnt[:, :], lntb)
            for j in range(2):
                k.op("dve", lambda e: e.tensor_tensor(out=dd[j][:, :], in0=yb[p][j][:, :], in1=msb[:, :], op=ALU.subtract),
                     reads=[ybb[p][j], msbb], writes=[ddb[j]])
                k.op("pool", lambda e: e.tensor_tensor(out=dd[j][:, :], in0=dd[j][:, :], in1=rstd[:, :], op=ALU.mult),
                     reads=[ddb[j], rstdb], writes=[ddb[j]])
                k.op("act", lambda e: e.activation(out=act[j][:, :], in_=dd[j][:, :], func=AF.Silu,
                                                   scale=cp[:, CP_LNG + j:CP_LNG + j + 1],
                                                   bias=cp[:, CP_LNB + j:CP_LNB + j + 1]),
                     reads=[ddb[j], cpb], writes=[actb[j]])

        def emit_pw(ti):
            for co in range(2):
                for ci in range(2):
                    k.op("pe", lambda e: e.matmul(ps[4 + co][:, :], pw[:, ci, co * 128:(co + 1) * 128], act[ci][:, :],
                                                  start=(ci == 0), stop=(ci == 1)),
                         reads=[pwb, actb[ci]], writes=[pb[4 + co]], sig=(ci == 1))
                k.op("act", lambda e: e.activation(out=og[co][:, :], in_=ps[4 + co][:, :], func=AF.Identity,
                                                   bias=cp[:, CP_PWB + co:CP_PWB + co + 1]),
                     reads=[pb[4 + co], cpb], writes=[ogb[co]])
                k.dma(C.ymixT[:, co, ti * 512:(ti + 1) * 512], og[co][:, :], reads=[ogb[co]], writes=[C.ymix_buf], q="pool")

        emit_conv(0)
        for ti in range(8):
            emit_stats_mm(ti)
            if ti + 1 < 8:
                emit_conv(ti + 1)
            emit_ln(ti)
            emit_pw(ti)
    k.barrier()


def phase4(k, nc, C, L, hook=None):
    lam_init = 0.8 - 0.6 * math.exp(-0.3 * L.idx)
    X = mybir.AxisListType.X
    with ExitStack() as es:
        lamv = sb(es, nc, "p4_lamv", [128, 4, 64], F32); lamb = Buf()
        for i, a in enumerate((L.lam_q1, L.lam_k1, L.lam_q2, L.lam_k2)):
            k.dma(lamv[:, i, :], a.partition_broadcast(128), writes=[lamb])
        prod = sb(es, nc, "p4_prod", [128, 2, 64], F32); prodb = Buf()
        dots = sb(es, nc, "p4_dots", [128, 4], F32); dotsb = Buf()
        for j in range(2):
            k.op("dve", lambda e: e.tensor_tensor(out=prod[:, j, :], in0=lamv[:, 2 * j, :], in1=lamv[:, 2 * j + 1, :], op=ALU.mult),
                 reads=[lamb], writes=[prodb])
            k.op("dve", lambda e: e.reduce_sum(dots[:, j:j + 1], prod[:, j, :], X), reads=[prodb], writes=[dotsb])
        k.op("act", lambda e: e.activation(out=dots[:, 2:4], in_=dots[:, 0:2], func=AF.Exp), reads=[dotsb], writes=[dotsb])
        neglam = sb(es, nc, "p4_nl", [128, 1], F32); nlb = Buf()
        k.op("dve", lambda e: e.tensor_tensor(out=neglam[:, :], in0=dots[:, 3:4], in1=dots[:, 2:3], op=ALU.subtract),
             reads=[dotsb], writes=[nlb])
        k.op("dve", lambda e: e.tensor_scalar(neglam[:, :], neglam[:, :], -lam_init, None, ALU.add), reads=[nlb], writes=[nlb])
        grow = sb(es, nc, "p4_grow", [128, 128], F32); growb = Buf()
        k.dma(grow[:, :], L.da_norm_g.partition_broadcast(128), writes=[growb])
        k.op("dve", lambda e: e.tensor_scalar(grow[:, :], grow[:, :], 1.0 - lam_init, None, ALU.mult), reads=[growb], writes=[growb])
        cmf = sb(es, nc, "p4_cmf", [128, 4, 512], F32); cmfb = Buf()
        cmask = sb(es, nc, "p4_cm", [128, 4, 512], BF16); cmb = Buf()
        k.dma(cmf[:, :, :], C.cmask_d[:, :, :], writes=[cmfb])
        k.op("dve", lambda e: e.tensor_copy(cmask[:, :, :], cmf[:, :, :]), reads=[cmfb], writes=[cmb])
        KT = [sb(es, nc, "p4_k%d" % i, [128, TA], BF16) for i in range(2)]; KTb = [Buf(), Buf()]
        QT = [sb(es, nc, "p4_q%d" % i, [128, TO], BF16) for i in range(2)]; QTb = [Buf(), Buf()]
        V = [sb(es, nc, "p4_v%d" % i, [128, 64, 129], BF16) for i in range(2)]; Vb = [Buf(), Buf()]
        for i in range(2):
            k.op("pool", lambda e: e.memset(V[i][:, :, 128:129], 1.0), writes=[Vb[i]])
        PT = [sb(es, nc, "p4_p%d" % i, [128, 1024], BF16) for i in range(2)]; PTb = [Buf() for _ in range(2)]
        o0 = [sb(es, nc, "p4_o0%d" % i, [128, 128], F32) for i in range(2)]; o0b = [Buf(), Buf()]
        oo = [sb(es, nc, "p4_oo%d" % i, [128, 128], F32) for i in range(2)]; oob = [Buf(), Buf()]
        yy = [sb(es, nc, "p4_yy%d" % i, [128, 128], BF16) for i in range(2)]; yyb = [Buf(), Buf()]
        sm = [sb(es, nc, "p4_sm%d" % i, [128, 8], F32) for i in range(2)]; smb = [Buf(), Buf()]
        st6 = [sb(es, nc, "p4_st%d" % i, [128, 6], F32) for i in range(2)]; st6b = [Buf(), Buf()]
        ys = [sb(es, nc, "p4_ys%d" % i, [128, 512], BF16) for i in range(2)]; ysb = [Buf(), Buf()]
        Sps = [psb(es, nc, "p4_S%d" % i, [128, 1024], F32) for i in range(2)]; Sb = [Buf() for _ in range(2)]
        Aps = [psb(es, nc, "p4_A%d" % i, [128, 512], F32) for i in range(3)]; Ab = [Buf() for _ in range(3)]
        tp = psb(es, nc, "p4_tp", [128, 1024], BF16); tpb = Buf()

        def acc(a):
            return Aps[a // 3][:, (a % 3) * 129:(a % 3) * 129 + 129], Ab[a // 3]

        nq = 0
        for h in range(4):
            hi = h % 2
            k.dma(KT[hi][:, :], C.dk[:, h, :], reads=[C.dk_buf], writes=[KTb[hi]])
            k.dma(QT[hi][:, :], C.dq[:, h, :], reads=[C.dq_buf], writes=[QTb[hi]])
            k.dma(V[hi][:, :, 0:128], C.dv[:, h * 128:(h + 1) * 128].rearrange("(n p) f -> p n f", p=128),
                  reads=[C.dv_buf], writes=[Vb[hi]])
            for qt in range(8):
                if hook is not None:
                    hook()
                nkb = 32 + 4 * qt + 4
                d0 = 32 + 4 * qt

                def emit_S(kb):
                    dj = kb - d0
                    si = kb % 2
                    c0 = 128 * dj if dj > 0 else 0
                    for m in range(2):
                        k.op("pe", lambda e: e.matmul(Sps[si][:, m * 512 + c0:(m + 1) * 512], KT[hi][m * 64:(m + 1) * 64, kb * 128:(kb + 1) * 128],
                                                      QT[hi][m * 64:(m + 1) * 64, qt * 512 + c0:(qt + 1) * 512],
                                                      start=True, stop=(dj < 0)),
                             reads=[KTb[hi], QTb[hi]], writes=[Sb[si]], sig=(dj < 0 and m == 1))
                        if dj >= 0:
                            k.op("pe", lambda e: e.matmul(Sps[si][:, m * 512 + c0:(m + 1) * 512], C.identbf[:, :], cmask[:, dj, c0:512],
                                                          start=False, stop=True),
                                 reads=[C.identbfb, cmb], writes=[Sb[si]], sig=(m == 1))
                    bias = C.flag[:, 1:2] if kb < 32 else 0.0
                    if c0 == 0:
                        k.op("act", lambda e: e.activation(out=PT[si][:, :], in_=Sps[si][:, :], func=AF.Exp, bias=bias, scale=0.125),
                             reads=[Sb[si], C.flagb], writes=[PTb[si]])
                    else:
                        k.op("act", lambda e: e.activation(out=PT[si][:, :].rearrange("p (m c) -> p m c", m=2)[:, :, c0:512],
                                                           in_=Sps[si][:, :].rearrange("p (m c) -> p m c", m=2)[:, :, c0:512],
                                                           func=AF.Exp, bias=bias, scale=0.125),
                             reads=[Sb[si], C.flagb], writes=[PTb[si]])

                def emit_PV(kb):
                    dj = kb - d0
                    si = kb % 2
                    lst = []
                    for qb in range(4):
                        if dj >= 0 and qb < dj:
                            continue
                        for m in range(2):
                            lst.append((qb, m))
                    for idx, (qb, m) in enumerate(lst):
                        ap, ab = acc(qb * 2 + m)
                        k.op("pe", lambda e: e.matmul(ap, PT[si][:, m * 512 + qb * 128:m * 512 + (qb + 1) * 128], V[hi][:, kb, :],
                                                      start=(kb == 0 and (qb * 2 + m) % 3 == 0), stop=(kb == d0 + qb), skip_group_check=True),
                             reads=[PTb[si], Vb[hi]], writes=[ab], sig=(idx == len(lst) - 1))

                emit_S(0)
                for kb in range(nkb):
                    if kb + 1 < nkb:
                        emit_S(kb + 1)
                    emit_PV(kb)
                for qb in range(4):
                    i = nq % 2
                    nq += 1
                    a0, ab0 = acc(qb * 2)
                    a1, ab1 = acc(qb * 2 + 1)
                    k.op("dve", lambda e: e.reciprocal(sm[i][:, 0:1], a0[:, 128:129]), reads=[ab0], writes=[smb[i]])
                    k.op("dve", lambda e: e.reciprocal(sm[i][:, 1:2], a1[:, 128:129]), reads=[ab1], writes=[smb[i]])
                    k.op("dve", lambda e: e.tensor_tensor(out=sm[i][:, 2:3], in0=sm[i][:, 1:2], in1=neglam[:, 0:1], op=ALU.mult),
                         reads=[smb[i], nlb], writes=[smb[i]])
                    k.op("dve", lambda e: e.tensor_scalar(o0[i][:, :], a0[:, 0:128], sm[i][:, 0:1], None, ALU.mult),
                         reads=[ab0, smb[i]], writes=[o0b[i]])
                    k.op("dve", lambda e: e.scalar_tensor_tensor(out=oo[i][:, :], in0=a1[:, 0:128], scalar=sm[i][:, 2:3],
                                                                 in1=o0[i][:, :], op0=ALU.mult, op1=ALU.add),
                         reads=[ab1, smb[i], o0b[i]], writes=[oob[i]])
                    k.op("dve", lambda e: e.bn_stats(st6[i][:, :], oo[i][:, :]), reads=[oob[i]], writes=[st6b[i]])
                    k.op("dve", lambda e: e.bn_aggr(sm[i][:, 3:5], st6[i][:, :]), reads=[st6b[i]], writes=[smb[i]])
                    k.op("dve", lambda e: e.scalar_tensor_tensor(out=sm[i][:, 5:6], in0=sm[i][:, 3:4], scalar=sm[i][:, 3:4],
                                                                 in1=sm[i][:, 4:5], op0=ALU.mult, op1=ALU.add),
                         reads=[smb[i]], writes=[smb[i]])
                    k.op("dve", lambda e: e.tensor_scalar(sm[i][:, 5:6], sm[i][:, 5:6], EPS, None, ALU.add), reads=[smb[i]], writes=[smb[i]])
                    rstd_from(k, C, sm[i][:, 7:8], sm[i][:, 5:6], [smb[i]], [smb[i]], sm[i][:, 6:7], smb[i])
                    k.op("dve", lambda e: e.scalar_tensor_tensor(out=yy[i][:, :], in0=oo[i][:, :], scalar=sm[i][:, 7:8],
                                                                 in1=grow[:, :], op0=ALU.mult, op1=ALU.mult),
                         reads=[oob[i], smb[i], growb], writes=[yyb[i]])
                    k.op("pe", lambda e: e.transpose(tp[:, qb * 128:(qb + 1) * 128], yy[i][:, :], C.identbf[:, :]),
                         reads=[yyb[i], C.identbfb], writes=[tpb], sig=True)
                yi = (h * 8 + qt) % 2
                k.op("dve", lambda e: e.tensor_copy(ys[yi][:, :], tp[:, 0:512]), reads=[tpb], writes=[ysb[yi]])
                k.dma(C.ymixT[:, 4 + h, qt * 512:(qt + 1) * 512], ys[yi][:, :], reads=[ysb[yi]], writes=[C.ymix_buf], q="pool")
    k.barrier()


def phase3(k, nc, C, L):
    NB = TA // 128
    NO = TO // 128
    NG = NB * 4
    with ExitStack() as es:
        cp = sb(es, nc, "p3_cp", [128, NCP], F32); cpb = Buf()
        k.dma(cp[:, :], L.colpack[:, :], writes=[cpb])
        dgm = sb(es, nc, "p3_dg", [128, 16, 128], BF16); dgb = Buf()
        for i in range(16):
            k.op("dve", lambda e: e.tensor_scalar(dgm[:, i, :], C.identbf[:, :], cp[:, CP_MLW + i:CP_MLW + i + 1], None, ALU.mult),
                 reads=[C.identbfb, cpb], writes=[dgb])
        utri = sb(es, nc, "p3_u", [128, 128], F32); ub = Buf()
        k.dma(utri[:, :], C.utri_d[:, :], writes=[ub])
        ones = sb(es, nc, "p3_ones", [128, 128], F32); onesb = Buf()
        k.op("dve", lambda e: e.memset(ones[:, :], 1.0), writes=[onesb])
        mlg = sb(es, nc, "p3_mlg", [128, 256], F32); mlgb = Buf()
        k.dma(mlg[:, :], L.ml_norm_g.partition_broadcast(128), writes=[mlgb])
        qT = sb(es, nc, "p3_qT", [128, 2, TO], BF16); qTb = Buf()
        kT = sb(es, nc, "p3_kT", [128, 2, TA], BF16); kTb = Buf()
        gts = sb(es, nc, "p3_g", [128, NB, 8], F32); gtb = Buf()
        k.dma(gts[:, :, :], C.gates.rearrange("(n p) g -> p n g", p=128), reads=[C.gates_buf], writes=[gtb])
        Call = sb(es, nc, "p3_Call", [128, NO, 2, 65], BF16); Callb = Buf()
        vaug = sb(es, nc, "p3_v", [128, NB, 4, 65], BF16); vb = Buf()
        A = [psb(es, nc, "p3_A%d" % i, [128, 512], F32) for i in range(2)]; Apb = [Buf(), Buf()]
        Np = [psb(es, nc, "p3_N%d" % i, [128, 512], F32) for i in range(2)]; Npb = [Buf(), Buf()]
        Dp = [psb(es, nc, "p3_D%d" % i, [128, 512], F32) for i in range(2)]; Dpb = [Buf(), Buf()]
        tp = psb(es, nc, "p3_tp", [128, 1024], BF16); tpb = Buf()
        tp2 = psb(es, nc, "p3_tp2", [128, 1024], BF16)
        tpx = [tp, tp2]
        with ExitStack() as esA:
            inp = [sb(esA, nc, "p3_in%d" % i, [128, 3 + TO], BF16) for i in range(2)]; inpb = [Buf(), Buf()]
            na = 0
            for j in range(4):
                k.op("pool", lambda e: e.memset(inp[0][:, 0:3], 0.0), writes=[inpb[0]])
                k.dma(inp[0][:, 3:], C.mqk[:, j, 0:TO], reads=[C.mqk_buf], writes=[inpb[0]])
                k.dma(inp[1][:, 3:], C.mqk[:, j, TO:TA], reads=[C.mqk_buf], writes=[inpb[1]])
                k.op("dve", lambda e: e.tensor_scalar(inp[1][:, 0:3], inp[0][:, TO:TO + 3], C.flag[:, 0:1], None, ALU.mult),
                     reads=[inpb[0], C.flagb], writes=[inpb[1]])
                for part in range(2):
                    if part == 0 and j < 2:
                        continue
                    for ti in range(8):
                        ai = na % 2
                        na += 1
                        for t in range(4):
                            k.op("pe", lambda e: e.matmul(A[ai][:, :], dgm[:, j * 4 + t, :], inp[part][:, ti * 512 + t:ti * 512 + t + 512],
                                                          start=(t == 0), stop=(t == 3)),
                                 reads=[dgb, inpb[part]], writes=[Apb[ai]], sig=(t == 3))
                        if j < 2:
                            dst, dbuf = qT[:, j, ti * 512:(ti + 1) * 512], qTb
                        else:
                            c0 = part * TO + ti * 512
                            dst, dbuf = kT[:, j - 2, c0:c0 + 512], kTb
                        k.op("act", lambda e: e.activation(out=dst, in_=A[ai][:, :], func=AF.Silu, bias=cp[:, CP_MLB + j:CP_MLB + j + 1]),
                             reads=[Apb[ai], cpb], writes=[dbuf])
            k.barrier()
        lf = sb(es, nc, "p3_lf", [128, NB, 4], F32); lfb = Buf()
        li = sb(es, nc, "p3_li", [128, NB, 4], F32); lib = Buf()
        bcol = sb(es, nc, "p3_bc", [128, NG], F32); bcb = Buf()
        apr = sb(es, nc, "p3_ap", [128, NG], F32); aprb = Buf()
        eb = sb(es, nc, "p3_eb", [128, NG], F32); ebb = Buf()
        eg = sb(es, nc, "p3_eg", [128, NB, 4], F32); egb = Buf()
        egc = sb(es, nc, "p3_egc", [128, NB, 2], F32); egcb = Buf()
        k.op("act", lambda e: e.activation(out=lf[:, :, :], in_=gts[:, :, 4:8], func=AF.Exp, scale=-1.0), reads=[gtb], writes=[lfb])
        k.op("act", lambda e: e.activation(out=lf[:, :, :], in_=lf[:, :, :], func=AF.Ln, bias=1.0), reads=[lfb], writes=[lfb])
        k.op("dve", lambda e: e.tensor_scalar(lf[:, :, :], lf[:, :, :], -1.0, None, ALU.mult), reads=[lfb], writes=[lfb])
        k.op("dve", lambda e: e.tensor_copy(li[:, :, :], gts[:, :, 0:4]), reads=[gtb], writes=[lib])
        lf2 = lf[:, :, :].rearrange("p a b -> p (a b)")
        li2 = li[:, :, :].rearrange("p a b -> p (a b)")
        eg2 = eg[:, :, :].rearrange("p a b -> p (a b)")
        k.op("pe", lambda e: e.matmul(A[0][:, 0:NG], utri[:, :], lf2, start=True, stop=True), reads=[ub, lfb], writes=[Apb[0]])
        k.op("pe", lambda e: e.matmul(A[1][:, 0:NG], ones[:, :], lf2, start=True, stop=True), reads=[onesb, lfb], writes=[Apb[1]])
        k.op("dve", lambda e: e.tensor_copy(bcol[:, :], A[0][:, 0:NG]), reads=[Apb[0]], writes=[bcb])
        k.op("dve", lambda e: e.tensor_tensor(out=apr[:, :], in0=li2, in1=bcol[:, :], op=ALU.subtract), reads=[lib, bcb], writes=[aprb])
        k.op("act", lambda e: e.activation(out=apr[:, :], in_=apr[:, :], func=AF.Exp), reads=[aprb], writes=[aprb])
        k.op("act", lambda e: e.activation(out=eb[:, :], in_=bcol[:, :], func=AF.Exp), reads=[bcb], writes=[ebb])
        k.op("dve", lambda e: e.tensor_scalar(eb[:, :], eb[:, :], 0.125, None, ALU.mult), reads=[ebb], writes=[ebb])
        k.op("act", lambda e: e.activation(out=eg2, in_=A[1][:, 0:NG], func=AF.Exp), reads=[Apb[1]], writes=[egb])
        for jc in range(2):
            for hp in range(2):
                ps_ = slice(hp * 64, (hp + 1) * 64)
                k.op("dve", lambda e: e.tensor_copy(egc[ps_, :, jc], eg[ps_, :, 2 * jc + hp]), reads=[egb], writes=[egcb])
        with ExitStack() as esB:
            ktok = sb(esB, nc, "p3_ktok", [128, NB, 256], BF16); ktokb = [Buf() for _ in range(NB)]
            dC = sb(esB, nc, "p3_dC", [128, NB, 2, 65], F32); dCb = [[[Buf(), Buf()] for _ in range(2)] for _ in range(NB)]
            tps = [Buf(), Buf()]
            for h in range(4):
                k.dma(vaug[:, :, h, 0:64], C.mv[:, h * 64:(h + 1) * 64].rearrange("(n p) d -> p n d", p=128), reads=[C.mv_buf], writes=[vb])
            k.op("pool", lambda e: e.memset(vaug[:, :, :, 64:65], 1.0), writes=[vb])
            for blk in range(NB):
                for jc in range(2):
                    k.op("pe", lambda e: e.transpose(tpx[blk % 2][:, jc * 128:(jc + 1) * 128],
                                                     kT[:, jc, blk * 128:(blk + 1) * 128], C.identbf[:, :]),
                         reads=[kTb, C.identbfb], writes=[tps[blk % 2]], sig=(jc == 1))
                if blk % 2 == 0:
                    k.op("dve", lambda e: e.tensor_copy(ktok[:, blk, :], tp[:, 0:256]), reads=[tps[0]], writes=[ktokb[blk]])
                else:
                    k.op("act", lambda e: e.copy(ktok[:, blk, :], tp2[:, 0:256]), reads=[tps[1]], writes=[ktokb[blk]])
            pvp = [sb(esB, nc, "p3_pv%d" % i, [128, 130], BF16) for i in range(8)]; pvpb = [Buf() for _ in range(8)]
            nv = 0
            for blk in range(NB):
                own = blk >= NB // 2
                ob = blk - NB // 2
                for jc in range(2):
                    pv, pvb_ = pvp[nv % 8][:, :], pvpb[nv % 8]
                    for hp in range(2):
                        h = 2 * jc + hp
                        k.op("pool" if hp == 0 else "dve",
                             lambda e: e.tensor_scalar(pv[:, hp * 65:(hp + 1) * 65], vaug[:, blk, h, :],
                                                       apr[:, blk * 4 + h:blk * 4 + h + 1], None, ALU.mult),
                             reads=[vb, aprb], writes=[pvb_])
                    di = nv % 2
                    nv += 1
                    k.op("pe", lambda e: e.matmul(Dp[di][:, 0:130], ktok[:, blk, jc * 128:(jc + 1) * 128], pv, start=True, stop=True),
                         reads=[ktokb[blk], pvb_], writes=[Dpb[di]], sig=True)
                    for hp in range(2):
                        h = 2 * jc + hp
                        ps_ = slice(hp * 64, (hp + 1) * 64)
                        if hp == 0:
                            k.op("dve", lambda e: e.tensor_scalar(dC[ps_, blk, jc, :], Dp[di][ps_, hp * 65:(hp + 1) * 65],
                                                                  eg[ps_, blk, h:h + 1], None, ALU.mult),
                                 reads=[Dpb[di], egb], writes=[dCb[blk][jc][hp]])
                        else:
                            k.op("act", lambda e: e.activation(out=dC[ps_, blk, jc, :], in_=Dp[di][ps_, hp * 65:(hp + 1) * 65],
                                                               func=AF.Copy, scale=eg[ps_, blk, h:h + 1]),
                                 reads=[Dpb[di], egb], writes=[dCb[blk][jc][hp]])
            Cst = sb(esB, nc, "p3_C", [128, 2, 65], F32)
            Ch = sb(esB, nc, "p3_Ch", [128, NO, 2, 65], F32)
            Cb2 = [Buf(), Buf()]
            k.op("dve", lambda e: e.memset(Cst[:, :, :], 0.0), writes=Cb2)
            for blk in range(NB - 1):
                own_next = blk + 1 >= NB // 2
                for jc in range(2):
                    if blk + 1 == NB // 2:
                        k.op("dve", lambda e: e.scalar_tensor_tensor(out=Cst[:, jc, :], in0=Cst[:, jc, :], scalar=egc[:, blk, jc:jc + 1],
                                                                     in1=dC[:, blk, jc, :], op0=ALU.mult, op1=ALU.add),
                             reads=[Cb2[jc], egcb] + dCb[blk][jc], writes=[Cb2[jc]])
                        k.op("dve", lambda e: e.tensor_scalar(Ch[:, 0, jc, :], Cst[:, jc, :], C.flag[:, 0:1], None, ALU.mult),
                             reads=[C.flagb, Cb2[jc]], writes=[Cb2[jc]])
                        continue
                    if blk + 1 < NB // 2:
                        src, dst = Cst[:, jc, :], Cst[:, jc, :]
                    else:
                        ob = blk - NB // 2
                        src, dst = Ch[:, ob, jc, :], Ch[:, ob + 1, jc, :]
                    k.op("dve", lambda e: e.scalar_tensor_tensor(out=dst, in0=src, scalar=egc[:, blk, jc:jc + 1],
                                                                 in1=dC[:, blk, jc, :], op0=ALU.mult, op1=ALU.add),
                         reads=[Cb2[jc], egcb] + dCb[blk][jc], writes=[Cb2[jc]])
            for q in range(4):
                k.op("act" if q % 2 == 0 else "pool",
                     (lambda e: e.copy(Call[:, q * 8:(q + 1) * 8, :, :], Ch[:, q * 8:(q + 1) * 8, :, :])) if q % 2 == 0 else
                     (lambda e: e.tensor_copy(Call[:, q * 8:(q + 1) * 8, :, :], Ch[:, q * 8:(q + 1) * 8, :, :])),
                     reads=Cb2, writes=[Callb])
            k.barrier()
        mo = sb(es, nc, "p3_mo", [128, NO, 256], BF16); mob = Buf()
        k.dma(mo[:, :, :], C.mo.rearrange("(n p) f -> p n f", p=128), reads=[C.mo_buf], writes=[mob])
        Cbf = [sb(es, nc, "p3_Cb%d" % i, [128, 2, 65], BF16) for i in range(2)]; Cbfb = [Buf(), Buf()]
        pvp = [sb(es, nc, "p3_pw%d" % i, [128, 130], BF16) for i in range(4)]; pvpb = [Buf() for _ in range(4)]
        scT = [sb(es, nc, "p3_sc%d" % i, [128, 128], BF16) for i in range(8)]; scTb = [Buf() for _ in range(8)]
        hbuf = [sb(es, nc, "p3_h%d" % i, [128, 256], F32) for i in range(2)]; hbufb = [Buf(), Buf()]
        xn = [sb(es, nc, "p3_xn%d" % i, [128, 256], F32) for i in range(2)]; xnb = [Buf(), Buf()]
        yml = [sb(es, nc, "p3_y%d" % i, [128, 256], BF16) for i in range(2)]; ymlb = [Buf(), Buf()]
        ysg = [sb(es, nc, "p3_ys%d" % i, [128, 2, 128], BF16) for i in range(2)]; ysgb = [Buf(), Buf()]
        sm = [sb(es, nc, "p3_sm%d" % i, [128, 16], F32) for i in range(2)]; smb = [Buf(), Buf()]
        st6 = [sb(es, nc, "p3_st%d" % i, [128, 4, 6], F32) for i in range(2)]; st6b = [Buf(), Buf()]
        mvs = [sb(es, nc, "p3_mv%d" % i, [128, 4, 2], F32) for i in range(2)]; mvsb = [Buf(), Buf()]
        rs = [sb(es, nc, "p3_rs%d" % i, [128, 8], F32) for i in range(2)]; rsb = [Buf(), Buf()]
        smh = [[Buf() for _ in range(4)] for _ in range(2)]
        hbh = [[Buf() for _ in range(4)] for _ in range(2)]
        sth = [[Buf() for _ in range(4)] for _ in range(2)]
        mvh = [[Buf() for _ in range(4)] for _ in range(2)]
        xnh = [[Buf() for _ in range(4)] for _ in range(2)]
        tpq = [Buf(), Buf()]

        def emit_A(ob):
            blk = NB // 2 + ob
            bi = ob % 2
            k.op("act", lambda e: e.copy(Cbf[bi][:, :, :], Call[:, ob, :, :]), reads=[Callb], writes=[Cbfb[bi]])
            for jc in range(2):
                pi = bi * 2 + jc
                for hp in range(2):
                    h = 2 * jc + hp
                    k.op("pool" if hp == 0 else "dve",
                         lambda e: e.tensor_scalar(pvp[pi][:, hp * 65:(hp + 1) * 65], vaug[:, blk, h, :],
                                                   apr[:, blk * 4 + h:blk * 4 + h + 1], None, ALU.mult),
                         reads=[vb, aprb], writes=[pvpb[pi]])
            for h in range(4):
                jc, hp = h // 2, h % 2
                ps_ = slice(hp * 64, (hp + 1) * 64)
                k.op("pe", lambda e: e.matmul(A[bi][:, h * 128:(h + 1) * 128], kT[ps_, jc, blk * 128:(blk + 1) * 128],
                                              qT[ps_, jc, ob * 128:(ob + 1) * 128], start=True, stop=True),
                     reads=[kTb, qTb], writes=[Apb[bi]], sig=True)
                k.op("dve", lambda e: e.tensor_tensor(out=scT[bi * 4 + h][:, :], in0=A[bi][:, h * 128:(h + 1) * 128], in1=utri[:, :],
                                                      op=ALU.mult),
                     reads=[Apb[bi], ub], writes=[scTb[bi * 4 + h]])

        def emit_B(ob):
            bi = ob % 2
            for h in range(4):
                jc, hp = h // 2, h % 2
                ps_ = slice(hp * 64, (hp + 1) * 64)
                pi = bi * 2 + jc
                k.op("pe", lambda e: e.matmul(Np[bi][:, h * 65:(h + 1) * 65], scT[bi * 4 + h][:, :], pvp[pi][:, hp * 65:(hp + 1) * 65],
                                              start=True, stop=False),
                     reads=[scTb[bi * 4 + h], pvpb[pi]], writes=[Npb[bi]], sig=False)
                k.op("pe", lambda e: e.matmul(Np[bi][:, h * 65:(h + 1) * 65], qT[ps_, jc, ob * 128:(ob + 1) * 128],
                                              Cbf[bi][ps_, jc, :], start=False, stop=True),
                     reads=[qTb, Cbfb[bi]], writes=[Npb[bi]], sig=True)

        def emit_C(ob):
            blk = NB // 2 + ob
            bi = ob % 2
            i = ob % 2
            for step in range(6):
                for h in range(4):
                    c = blk * 4 + h
                    nh = Np[bi][:, h * 65:(h + 1) * 65]
                    sb_ = smh[i][h]
                    if step == 0:
                        k.op("dve", lambda e: e.tensor_scalar(sm[i][:, h:h + 1], nh[:, 64:65], eb[:, c:c + 1], None, ALU.mult),
                             reads=[Npb[bi], ebb], writes=[sb_])
                    elif step == 1:
                        k.op("dve", lambda e: e.tensor_scalar(sm[i][:, 4 + h:5 + h], sm[i][:, h:h + 1], 1.0, None, ALU.max),
                             reads=[sb_], writes=[sb_])
                    elif step == 2:
                        k.op("dve", lambda e: e.scalar_tensor_tensor(out=sm[i][:, h:h + 1], in0=sm[i][:, h:h + 1], scalar=-1.0,
                                                                     in1=sm[i][:, 4 + h:5 + h], op0=ALU.mult, op1=ALU.max),
                             reads=[sb_], writes=[sb_])
                    elif step == 3:
                        k.op("dve", lambda e: e.reciprocal(sm[i][:, 4 + h:5 + h], sm[i][:, h:h + 1]), reads=[sb_], writes=[sb_])
                    elif step == 4:
                        k.op("dve", lambda e: e.tensor_tensor(out=sm[i][:, 8 + h:9 + h], in0=sm[i][:, 4 + h:5 + h], in1=eb[:, c:c + 1], op=ALU.mult),
                             reads=[sb_, ebb], writes=[sb_])
                    else:
                        k.op("dve", lambda e: e.scalar_tensor_tensor(out=hbuf[i][:, h * 64:(h + 1) * 64], in0=nh[:, 0:64], scalar=sm[i][:, 8 + h:9 + h],
                                                                     in1=mo[:, ob, h * 64:(h + 1) * 64], op0=ALU.mult, op1=ALU.mult),
                             reads=[Npb[bi], sb_, mob], writes=[hbh[i][h]])
            for h in range(4):
                k.op("dve", lambda e: e.bn_stats(st6[i][:, h, :], hbuf[i][:, h * 64:(h + 1) * 64]), reads=[hbh[i][h]], writes=[sth[i][h]])
            for h in range(4):
                k.op("dve", lambda e: e.bn_aggr(mvs[i][:, h, :], st6[i][:, h, :]), reads=[sth[i][h]], writes=[mvh[i][h]])
            k.op("dve", lambda e: e.tensor_scalar(rs[i][:, 0:4], mvs[i][:, :, 1], EPS, None, ALU.add), reads=mvh[i], writes=[rsb[i]])
            rstd_from(k, C, rs[i][:, 0:4], rs[i][:, 0:4], [rsb[i]], [rsb[i]], rs[i][:, 4:8], rsb[i])
            for h in range(4):
                k.op("dve" if h % 2 == 0 else "pool",
                     lambda e: e.tensor_scalar(xn[i][:, h * 64:(h + 1) * 64], hbuf[i][:, h * 64:(h + 1) * 64], mvs[i][:, h, 0:1],
                                               rs[i][:, h:h + 1], ALU.subtract, ALU.mult),
                     reads=[hbh[i][h], mvh[i][h], rsb[i]], writes=[xnh[i][h]])
            k.op("pool", lambda e: e.tensor_tensor(out=yml[i][:, :], in0=xn[i][:, :], in1=mlg[:, :], op=ALU.mult),
                 reads=xnh[i] + [mlgb], writes=[ymlb[i]])
            for jc in range(2):
                k.op("pe", lambda e: e.transpose(tpx[i][:, 512 + jc * 128:512 + (jc + 1) * 128],
                                                 yml[i][:, jc * 128:(jc + 1) * 128], C.identbf[:, :]),
                     reads=[ymlb[i], C.identbfb], writes=[tpq[i]], sig=(jc == 1))
            k.op("act", lambda e: e.copy(ysg[i][:, :, :], tpx[i][:, 512:768].rearrange("p (a b) -> p a b", a=2)),
                 reads=[tpq[i]], writes=[ysgb[i]])
            k.dma(C.ymixT[:, 2:4, ob * 128:(ob + 1) * 128], ysg[i][:, :, :], reads=[ysgb[i]], writes=[C.ymix_buf], q="pool")

        emit_A(0)
        for ob in range(NO):
            if ob + 1 < NO:
                emit_A(ob + 1)
            emit_B(ob)
            emit_C(ob)
    k.barrier()


def layer_norm_rows(k, C, z, zb, st, stb, mv, mvb, tmp, tmpb, grow, brow, rowb, outt, outb, eng2="pool"):
    for hf in range(2):
        k.op("dve", lambda e: e.bn_stats(st[:, hf, :], z[:, hf * 512:(hf + 1) * 512]), reads=[zb], writes=[stb])
    k.op("dve", lambda e: e.bn_aggr(mv[:, 0:2], st[:, :, :].rearrange("p a b -> p (a b)")), reads=[stb], writes=[mvb])
    k.op("dve", lambda e: e.tensor_scalar(mv[:, 2:3], mv[:, 1:2], EPS, None, ALU.add), reads=[mvb], writes=[mvb])
    rstd_from(k, C, mv[:, 3:4], mv[:, 2:3], [mvb], [mvb], tmp[:, 0:1], tmpb)
    k.op("dve", lambda e: e.tensor_scalar(z[:, :], z[:, :], mv[:, 0:1], mv[:, 3:4], ALU.subtract, ALU.mult),
         reads=[zb, mvb], writes=[zb])
    k.op(eng2, lambda e: e.tensor_tensor(out=outt[:, :], in0=z[:, :], in1=grow[:, :], op=ALU.mult),
         reads=[zb, rowb], writes=[outb])
    k.op(eng2, lambda e: e.tensor_tensor(out=outt[:, :], in0=outt[:, :], in1=brow[:, :], op=ALU.add),
         reads=[outb, rowb], writes=[outb])


def phase5(k, nc, C, L, xres_src):
    with ExitStack() as es:
        wout = sb(es, nc, "p5_w", [128, 8, D], BF16); woutb = Buf()
        load_cast(k, es, nc, "p5w", wout, woutb, L.w_out.rearrange("(kc p) n -> p kc n", p=128), D, 256)
        grow = sb(es, nc, "p5_g", [128, D], F32); brow = sb(es, nc, "p5_b", [128, D], F32); rowb = Buf()
        k.dma(grow[:, :], L.ln1_g.partition_broadcast(128), writes=[rowb])
        k.dma(brow[:, :], L.ln1_b.partition_broadcast(128), writes=[rowb])
        ymt = [sb(es, nc, "p5_y%d" % i, [128, 8, 128], BF16) for i in range(4)]; ymtb = [Buf() for _ in range(4)]
        xr = [sb(es, nc, "p5_x%d" % i, [128, D], F32) for i in range(4)]; xrb = [Buf() for _ in range(4)]
        z = [sb(es, nc, "p5_z%d" % i, [128, D], F32) for i in range(4)]; zb = [Buf() for _ in range(4)]
        xt = [sb(es, nc, "p5_t%d" % i, [128, 8, 128], BF16) for i in range(4)]; xtb = [Buf() for _ in range(4)]
        st = [sb(es, nc, "p5_st%d" % i, [128, 2, 6], F32) for i in range(4)]; stb = [Buf() for _ in range(4)]
        mvt = [sb(es, nc, "p5_mv%d" % i, [128, 2, 6], F32) for i in range(2)]; mvb = [Buf(), Buf()]
        ps = [psb(es, nc, "p5_ps%d" % i, [128, 512], F32) for i in range(8)]
        pb = [Buf() for _ in range(8)]
        NGR = TO // 256

        def emit_MM(g):
            for j in range(2):
                bi = (g % 2) * 2 + j
                r0 = (g * 2 + j) * 128
                k.dma(ymt[bi][:, :, :], C.ymixT[:, :, r0:r0 + 128], reads=[C.ymix_buf], writes=[ymtb[bi]])
                k.dma(xr[bi][:, :], xres_src[r0:r0 + 128, :], writes=[xrb[bi]])
                for hf in range(2):
                    pi = j * 2 + hf
                    for kc in range(8):
                        k.op("pe", lambda e: e.matmul(ps[pi][:, :], ymt[bi][:, kc, :], wout[:, kc, hf * 512:(hf + 1) * 512],
                                                      start=(kc == 0), stop=(kc == 7)),
                             reads=[ymtb[bi], woutb], writes=[pb[pi]], sig=(kc == 7))

        def emit_stt(g):
            gp = g % 2
            for j in range(2):
                bi = gp * 2 + j
                for hf in range(2):
                    pi = j * 2 + hf
                    k.op("dve", lambda e: e.scalar_tensor_tensor(out=z[bi][:, hf * 512:(hf + 1) * 512], in0=xr[bi][:, hf * 512:(hf + 1) * 512],
                                                                 scalar=ALPHA, in1=ps[pi][:, :], op0=ALU.mult, op1=ALU.add),
                         reads=[xrb[bi], pb[pi]], writes=[zb[bi]])
                    k.op("dve", lambda e: e.bn_stats(st[bi][:, hf, :], z[bi][:, hf * 512:(hf + 1) * 512]), reads=[zb[bi]], writes=[stb[bi]])
                k.op("dve", lambda e: e.bn_aggr(mvt[gp][:, j, 0:2], st[bi][:, :, :].rearrange("p a b -> p (a b)")),
                     reads=[stb[bi]], writes=[mvb[gp]])

        def emit_LN(g):
            gp = g % 2
            k.op("dve", lambda e: e.tensor_scalar(mvt[gp][:, :, 2], mvt[gp][:, :, 1], EPS, None, ALU.add), reads=[mvb[gp]], writes=[mvb[gp]])
            rstd_from(k, C, mvt[gp][:, :, 4], mvt[gp][:, :, 2], [mvb[gp]], [mvb[gp]], mvt[gp][:, :, 3], mvb[gp])
            for j in range(2):
                bi = gp * 2 + j
                r0 = (g * 2 + j) * 128
                k.op("dve", lambda e: e.tensor_scalar(z[bi][:, :], z[bi][:, :], mvt[gp][:, j, 0:1], mvt[gp][:, j, 4:5], ALU.subtract, ALU.mult),
                     reads=[zb[bi], mvb[gp]], writes=[zb[bi]])
                k.op("dve", lambda e: e.tensor_tensor(out=z[bi][:, :], in0=z[bi][:, :], in1=grow[:, :], op=ALU.mult),
                     reads=[zb[bi], rowb], writes=[zb[bi]])
                k.op("pool", lambda e: e.tensor_tensor(out=z[bi][:, :], in0=z[bi][:, :], in1=brow[:, :], op=ALU.add),
                     reads=[zb[bi], rowb], writes=[zb[bi]])
                k.dma(C.x1[r0:r0 + 128, :], z[bi][:, :], reads=[zb[bi]], writes=[C.x1_buf], q="pool")

        def emit_T(g):
            gp = g % 2
            for j in range(2):
                bi = gp * 2 + j
                r0 = (g * 2 + j) * 128
                for hf in range(2):
                    pi = 4 + j * 2 + hf
                    for jj in range(4):
                        kc = hf * 4 + jj
                        k.op("pe", lambda e: e.transpose(ps[pi][:, jj * 128:(jj + 1) * 128], z[bi][:, kc * 128:(kc + 1) * 128], C.ident[:, :]),
                             reads=[zb[bi], C.identb], writes=[pb[pi]], sig=(jj == 3))
                    k.op("act", lambda e: e.copy(xt[bi][:, hf * 4:(hf + 1) * 4, :], ps[pi][:, :].rearrange("p (a b) -> p a b", a=4)),
                         reads=[pb[pi]], writes=[xtb[bi]])
                k.dma(C.x1T[:, :, r0:r0 + 128], xt[bi][:, :, :], reads=[xtb[bi]], writes=[C.x1T_buf], q="pool")

        emit_MM(0)
        emit_stt(0)
        for g in range(NGR):
            if g + 1 < NGR:
                emit_MM(g + 1)
            emit_LN(g)
            emit_T(g)
            if g + 1 < NGR:
                emit_stt(g + 1)
    k.barrier()


def phase6(k, nc, C, L, xdst, xdst_buf, PU):
    with ExitStack() as es:
        PU.finish()
        wup = PU.dst
        wupb = PU.bufs
        wdn = sb(es, nc, "p6_wd", [128, 32, D], BF16)
        wdnb = [[Buf(), Buf()] for _ in range(8)]
        std = [sb(es, nc, "p6_sd%d" % i, [128, 16, 128], F32) for i in range(2)]; stdb = [Buf(), Buf()]
        wd3 = L.w_down.rearrange("(kc p) n -> p kc n", p=128)
        n = 0
        for cb in range(8):
            for fh in range(2):
                i = n % 2
                n += 1
                k.dma(std[i][:, :, :], wd3[:, fh * 16:(fh + 1) * 16, cb * 128:(cb + 1) * 128], writes=[stdb[i]])
                k.op("dve" if n % 2 == 0 else "pool",
                     lambda e: e.tensor_copy(wdn[:, fh * 16:(fh + 1) * 16, cb * 128:(cb + 1) * 128], std[i][:, :, :]),
                     reads=[stdb[i]], writes=[wdnb[cb][fh]])
        grow = sb(es, nc, "p6_g", [128, D], F32); brow = sb(es, nc, "p6_b", [128, D], F32); rowb = Buf()
        k.dma(grow[:, :], L.ln2_g.partition_broadcast(128), writes=[rowb])
        k.dma(brow[:, :], L.ln2_b.partition_broadcast(128), writes=[rowb])
        xt = [sb(es, nc, "p6_t%d" % i, [128, 8, 256], BF16) for i in range(2)]; xtb = [Buf(), Buf()]
        hT = sb(es, nc, "p6_h", [128, 32, 256], BF16); hTb = [Buf() for _ in range(32)]
        rl = [sb(es, nc, "p6_r%d" % i, [128, 256], BF16) for i in range(4)]; rlb = [Buf() for _ in range(4)]
        xr = [sb(es, nc, "p6_x%d" % i, [128, D], F32) for i in range(2)]; xrb = [Buf(), Buf()]
        z = [sb(es, nc, "p6_z%d" % i, [128, D], F32) for i in range(2)]; zb = [Buf(), Buf()]
        st = sb(es, nc, "p6_st", [128, 2, 6], F32); stb = Buf()
        mv = sb(es, nc, "p6_mv", [128, 4], F32); mvb = Buf()
        tmp = sb(es, nc, "p6_tmp", [128, 1], F32); tmpb = Buf()
        ps = [psb(es, nc, "p6_ps%d" % i, [128, 512], F32) for i in range(8)]
        pb = [Buf() for _ in range(8)]
        nb = 0
        for ti in range(TO // 256):
            t0 = ti * 256
            x_t = xt[ti % 2]
            k.dma(x_t[:, :, :], C.x1T[:, :, t0:t0 + 256], reads=[C.x1T_buf], writes=[xtb[ti % 2]])
            for fc in range(32):
                pi = fc % 4
                for kc in range(8):
                    k.op("pe", lambda e: e.matmul(ps[pi][:, 0:256], wup[:, kc, fc * 128:(fc + 1) * 128], x_t[:, kc, :],
                                                  start=(kc == 0), stop=(kc == 7)),
                         reads=[wupb[fc], xtb[ti % 2]], writes=[pb[pi]], sig=(kc == 7))
                ri = fc % 4
                k.op("act", lambda e: e.activation(out=rl[ri][:, :], in_=ps[pi][:, 0:256], func=AF.Relu),
                     reads=[pb[pi]], writes=[rlb[ri]])
                k.op("dve" if fc % 2 == 0 else "pool",
                     lambda e: e.tensor_tensor(out=hT[:, fc, :], in0=rl[ri][:, :], in1=rl[ri][:, :], op=ALU.mult),
                     reads=[rlb[ri]], writes=[hTb[fc]])
            for blk in range(2):
                i = nb % 2
                nb += 1
                r0 = t0 + blk * 128
                k.dma(xr[i][:, :], C.x1[r0:r0 + 128, :], reads=[C.x1_buf], writes=[xrb[i]])
                for hf in range(2):
                    pi = 4 + i * 2 + hf
                    for fc in range(32):
                        wr = [wdnb[cb][fc // 16] for cb in range(hf * 4, hf * 4 + 4)]
                        k.op("pe", lambda e: e.matmul(ps[pi][:, :], hT[:, fc, blk * 128:(blk + 1) * 128],
                                                      wdn[:, fc, hf * 512:(hf + 1) * 512], start=(fc == 0), stop=(fc == 31)),
                             reads=[hTb[fc]] + wr, writes=[pb[pi]], sig=(fc == 31))
                    k.op("dve", lambda e: e.scalar_tensor_tensor(out=z[i][:, hf * 512:(hf + 1) * 512],
                                                                 in0=xr[i][:, hf * 512:(hf + 1) * 512], scalar=ALPHA,
                                                                 in1=ps[pi][:, :], op0=ALU.mult, op1=ALU.add),
                         reads=[xrb[i], pb[pi]], writes=[zb[i]])
                layer_norm_rows(k, C, z[i], zb[i], st, stb, mv, mvb, tmp, tmpb, grow, brow, rowb, z[i], zb[i])
                k.dma(xdst[r0:r0 + 128, :], z[i][:, :], reads=[zb[i]], writes=[xdst_buf], q="pool")
    k.barrier()


LAYER_W = [("w_in", [D, DIN]), ("b_igate", [4]), ("b_fgate", [4]), ("conv_pw_w", [256, 256]),
           ("colpack", [128, NCP]), ("ml_norm_g", [256]), ("lam_q1", [64]), ("lam_k1", [64]),
           ("lam_q2", [64]), ("lam_k2", [64]), ("da_norm_g", [128]), ("w_out", [D, D]), ("ln1_g", [D]), ("ln1_b", [D]),
           ("w_up", [D, DFF]), ("w_down", [DFF, D]), ("ln2_g", [D]), ("ln2_b", [D])]


def build(layers, phases=None, debug=False, inject=()):
    nc = bass.Bass("TRN2", target_bir_lowering=False)
    C = Ctx()
    skind = "ExternalOutput" if debug else "Internal"
    x_own = nc.dram_tensor("x_own", [TO, D], F32, kind="ExternalInput").ap()
    x_prev = nc.dram_tensor("x_prev", [TO, D], F32, kind="ExternalInput").ap()
    flag_d = nc.dram_tensor("flag", [128, 2], F32, kind="ExternalInput").ap()
    ident_d = nc.dram_tensor("ident", [128, 128], F32, kind="ExternalInput").ap()
    cmask_d = nc.dram_tensor("cmask", [128, 4, 512], F32, kind="ExternalInput").ap()
    C.utri_d = nc.dram_tensor("utri", [128, 128], F32, kind="ExternalInput").ap()
    Ls = []
    for l in layers:
        L = Ctx()
        L.idx = l
        for (nm, shp) in LAYER_W:
            setattr(L, nm, nc.dram_tensor("%s_%d" % (nm, l), shp, F32, kind="ExternalInput").ap())
        Ls.append(L)
    out = nc.dram_tensor("out", [TO, D], F32, kind="ExternalOutput").ap()

    def scratch(name, shape, dt):
        return nc.dram_tensor(name, shape, dt, kind=("ExternalInput" if name in inject else skind)).ap()
    C.xT_all = scratch("xT_all", [128, 8, TA], BF16); C.xT_buf = Buf()
    C.glu = scratch("glu", [128, 2, 512 + TO], BF16); C.glu_buf = Buf()
    C.mqk = scratch("mqk", [128, 4, TA], BF16); C.mqk_buf = Buf()
    C.mv = scratch("mv", [TA, 256], BF16); C.mv_buf = Buf()
    C.mo = scratch("mo", [TO, 256], BF16); C.mo_buf = Buf()
    C.gates = scratch("gates", [TA, 8], F32); C.gates_buf = Buf()
    C.dq = scratch("dq", [128, 4, TO], BF16); C.dq_buf = Buf()
    C.dk = scratch("dk", [128, 4, TA], BF16); C.dk_buf = Buf()
    C.dv = scratch("dv", [TA, 512], BF16); C.dv_buf = Buf()
    C.ymixT = scratch("ymixT", [128, 8, TO], BF16); C.ymix_buf = Buf()
    C.x1 = scratch("x1", [TO, D], F32); C.x1_buf = Buf()
    C.x1T = scratch("x1T", [128, 8, TO], BF16); C.x1T_buf = Buf()
    NCH, CH = 8, 512
    bnc = [nc.dram_tensor("xbnc%d" % j, [CH, D], F32) for j in range(NCH)]
    gth = [nc.dram_tensor("xgth%d" % j, [2 * CH, D], F32) for j in range(NCH)]
    bncb = Buf()
    C.xmid = RowChunks(bnc, CH); C.xmid_buf = Buf()
    C.out = out; C.out_buf = Buf()
    with ExitStack() as es:
        k = KB(nc, es)
        cc_sem = es.enter_context(nc.semaphore("cc_sem"))
        C.ident = sb(es, nc, "ident_sb", [128, 128], F32); C.identb = Buf()
        C.identbf = sb(es, nc, "identbf", [128, 128], BF16); C.identbfb = Buf()
        C.flag = sb(es, nc, "flagsb", [128, 2], F32); C.flagb = Buf()
        k.dma(C.ident[:, :], ident_d[:, :], writes=[C.identb])
        k.dma(C.flag[:, :], flag_d[:, :], writes=[C.flagb])
        k.op("dve", lambda e: e.tensor_copy(C.identbf[:, :], C.ident[:, :]), reads=[C.identb], writes=[C.identbfb])
        C.cmask_d = cmask_d
        ncc = 0
        for li, L in enumerate(Ls):
            if li == 0:
                xo = x_own
                src_prev = [(x_prev, TO, 0)]
            else:
                k.barrier()
                for j in range(NCH):
                    nc.gpsimd.collective_compute("AllGather", ALU.bypass, replica_groups=[[0, 1], [2, 3], [4, 5], [6, 7]],
                                                 ins=[bnc[j].ap().opt()], outs=[gth[j].ap().opt()]).then_inc(cc_sem)
                    ncc += 1
                for e in ("pe", "act", "dve", "pool", "sp"):
                    k.E[e].wait_ge(cc_sem, ncc)
                xo = C.xmid
                src_prev = [(gth[j].ap()[0:CH, :], CH, j * CH) for j in range(NCH)]
            last = li == len(Ls) - 1
            xdst = out if last else C.xmid
            xdst_buf = C.out_buf if last else C.xmid_buf
            ph = phases if phases is not None else [0, 1, 2, 3, 4, 5, 6]
            with ExitStack() as esw:
                PW = Prefetch(k, esw, nc, "pw_in", L.w_in.rearrange("(kc p) n -> p kc n", p=128), 8, DIN, 280, ("dve", "pool"))
                if 0 in ph:
                    phase0(k, nc, C, src_prev + [(xo, TO, TO)], hook=lambda it: PW.step(1) if it % 5 == 0 else None)
                if 1 in ph:
                    phase1(k, nc, C, L, PW)
                k.barrier()
            if 2 in ph:
                phase2(k, nc, C, L)
            if 3 in ph:
                phase3(k, nc, C, L)
            with ExitStack() as esw:
                PU = Prefetch(k, esw, nc, "pw_up", L.w_up.rearrange("(kc p) n -> p kc n", p=128), 8, DFF, 128, ("pool",))
                if 4 in ph:
                    phase4(k, nc, C, L, hook=lambda: PU.step(1))
                if 5 in ph:
                    phase5(k, nc, C, L, xo)
                if 6 in ph:
                    phase6(k, nc, C, L, xdst, xdst_buf, PU)
                k.barrier()
        k.barrier()
        C.ninst = k.ninst
    return nc, C


def make_cmask():
    kk = np.arange(128)[:, None, None]
    jj = np.arange(4)[None, :, None]
    qq = np.arange(512)[None, None, :]
    return np.where(qq >= jj * 128 + kk, 0.0, NEG).astype(np.float32)


def make_colpack(inp, l):
    cp = np.zeros((128, NCP), np.float32)
    def cols(v, n):
        return np.asarray(v, np.float32).reshape(n, 128).T
    cp[:, CP_DWB:CP_DWB + 2] = cols(inp["conv_dw_b"][l], 2)
    cp[:, CP_LNG:CP_LNG + 2] = cols(inp["conv_ln_g"][l], 2)
    cp[:, CP_LNB:CP_LNB + 2] = cols(inp["conv_ln_b"][l], 2)
    cp[:, CP_PWB:CP_PWB + 2] = cols(inp["conv_pw_b"][l], 2)
    cp[:, CP_MLB:CP_MLB + 4] = cols(inp["ml_conv_b"][l], 4)
    w = np.asarray(inp["conv_dw_w"][l], np.float32)
    for j in range(2):
        cp[:, CP_DWW + j * 31:CP_DWW + (j + 1) * 31] = w[:, j * 128:(j + 1) * 128].T
    w = np.asarray(inp["ml_conv_w"][l], np.float32)
    for j in range(4):
        cp[:, CP_MLW + j * 4:CP_MLW + (j + 1) * 4] = w[:, j * 128:(j + 1) * 128].T
    return cp


def core_inputs(inp, x_full, c, layers):
    b, p = c // 2, c % 2
    m = {"x_own": np.ascontiguousarray(x_full[b, p * TO:(p + 1) * TO]),
         "x_prev": np.ascontiguousarray(x_full[b, 0:TO]),
         "flag": np.tile(np.array([[float(p), 0.0 if p == 1 else NEG]], np.float32), (128, 1)),
         "ident": np.eye(128, dtype=np.float32), "cmask": make_cmask(),
         "utri": np.triu(np.ones((128, 128), np.float32))}
    for l in layers:
        for (nm, shp) in LAYER_W:
            if nm == "colpack":
                m["colpack_%d" % l] = make_colpack(inp, l)
            else:
                m["%s_%d" % (nm, l)] = np.ascontiguousarray(inp[nm][l])
    return m


def _run(inp, x_full, layers):
    nc, _ = build(layers)
    in_maps = [core_inputs(inp, x_full, c, layers) for c in range(8)]
    res = run_bass_kernel_spmd(nc, in_maps, core_ids=list(range(8)))
    out = np.empty((4, S, D), np.float32)
    for c in range(8):
        b, p = c // 2, c % 2
        out[b, p * TO:(p + 1) * TO] = np.asarray(res.results[c]["out"], dtype=np.float32)
    return out


def kernel(**inputs):
    inp = {k_: np.asarray(v) for k_, v in inputs.items()}
    x = np.asarray(inp["x"], np.float32)
    return _run(inp, x, list(range(DEPTH)))
```

```python
import math
from contextlib import ExitStack
import numpy as np
import concourse.bass as bass
import concourse.mybir as mybir
from concourse.bass_utils import run_bass_kernel_spmd

F32 = mybir.dt.float32
BF16 = mybir.dt.bfloat16
AF = mybir.ActivationFunctionType
ALU = mybir.AluOpType

D = 1024
S = 8192
TO = 4096
TA = 8192
DIN = 3080
DFF = 4096
DEPTH = 2
ALPHA = (2 * DEPTH) ** 0.25
EPS = 1e-5
NEG = -30000.0
NDS = 48
FORCE_Q = None


class Buf:
    __slots__ = ("w", "r")

    def __init__(self):
        self.w = None
        self.r = {}


class KB:
    def __init__(self, nc, es):
        self.nc = nc
        self.E = {"pe": nc.tensor, "act": nc.scalar, "dve": nc.vector, "pool": nc.gpsimd, "sp": nc.sync}
        self.sem = {}
        self.cnt = {}
        for e in ("pe", "act", "dve", "pool"):
            self.sem[e] = es.enter_context(nc.semaphore("s_" + e))
            self.cnt[e] = 0
        self.dsem = [es.enter_context(nc.semaphore("d%d" % i)) for i in range(NDS)]
        self.dcnt = [0] * NDS
        self.dnext = 0
        self.seen = {}
        self.ninst = 0

    def _toks(self, reads, writes):
        toks = []
        for b in reads:
            if b.w is not None:
                toks.append(b.w)
        for b in writes:
            if b.w is not None:
                toks.append(b.w)
            toks.extend(b.r.items())
        return toks

    def _wait(self, e, toks):
        need = {}
        for k, v in toks:
            if k == "pe" and e == "pe":
                continue
            if self.seen.get((e, k), 0) >= v:
                continue
            if need.get(k, 0) < v:
                need[k] = v
        for k, v in need.items():
            sem = self.sem[k] if isinstance(k, str) else self.dsem[k]
            self.E[e].wait_ge(sem, v)
            self.seen[(e, k)] = v
            self.ninst += 1

    def _commit(self, tok, reads, writes):
        k, v = tok
        for b in reads:
            if b.r.get(k, 0) < v:
                b.r[k] = v
        for b in writes:
            b.w = tok
            b.r = {}

    def op(self, e, fn, reads=(), writes=(), sig=True):
        self._wait(e, self._toks(reads, writes))
        inst = fn(self.E[e])
        self.ninst += 1
        if sig:
            self.cnt[e] += 1
            inst.then_inc(self.sem[e], 1)
            tok = (e, self.cnt[e])
        else:
            tok = (e, self.cnt[e] + 1)
        self._commit(tok, reads, writes)
        return tok

    def dma(self, out, in_, reads=(), writes=(), q="sp"):
        if FORCE_Q is not None:
            q = FORCE_Q
        self._wait(q, self._toks(reads, writes))
        j = self.dnext
        self.dnext = (j + 1) % NDS
        self.dcnt[j] += 16
        self.E[q].dma_start(out=out, in_=in_).then_inc(self.dsem[j], 16)
        self.ninst += 1
        tok = (j, self.dcnt[j])
        self._commit(tok, reads, writes)
        return tok

    def barrier(self):
        toks = [(e, self.cnt[e]) for e in ("pe", "act", "dve", "pool") if self.cnt[e] > 0]
        toks += [(j, self.dcnt[j]) for j in range(NDS) if self.dcnt[j] > 0]
        for e in ("pe", "act", "dve", "pool", "sp"):
            need = []
            for k, v in toks:
                if self.seen.get((e, k), 0) < v:
                    need.append((k, v))
            for k, v in need:
                sem = self.sem[k] if isinstance(k, str) else self.dsem[k]
                self.E[e].wait_ge(sem, v)
                self.seen[(e, k)] = v


class Ctx:
    pass


class RowChunks:
    def __init__(self, handles, ch):
        self.h = handles
        self.ch = ch

    def __getitem__(self, idx):
        rs, cs = idx
        j = rs.start // self.ch
        a = rs.start - j * self.ch
        b = rs.stop - j * self.ch
        assert 0 <= a < b <= self.ch
        return self.h[j].ap()[a:b, cs]


_UID = [0]


def sb(es, nc, name, shape, dt):
    _UID[0] += 1
    return es.enter_context(nc.sbuf_tensor("%s_u%d" % (name, _UID[0]), shape, dt))


def psb(es, nc, name, shape, dt):
    _UID[0] += 1
    return es.enter_context(nc.psum_tensor("%s_u%d" % (name, _UID[0]), shape, dt))


def load_cast(k, es, nc, name, dst, dstbuf, src3, ncols, piece, engines=("dve", "pool")):
    KC = dst.shape[1]
    st = [sb(es, nc, "%s_st%d" % (name, i), [128, KC, piece], F32) for i in range(2)]
    stb = [Buf(), Buf()]
    i = 0
    c0 = 0
    while c0 < ncols:
        w = min(piece, ncols - c0)
        s = st[i % 2]
        k.dma(s[:, :, 0:w], src3[:, :, c0:c0 + w], writes=[stb[i % 2]])
        e = engines[i % len(engines)]
        k.op(e, lambda eng, s=s, c0=c0, w=w: eng.tensor_copy(dst[:, :, c0:c0 + w], s[:, :, 0:w]),
             reads=[stb[i % 2]], writes=[dstbuf])
        c0 += w
        i += 1


class Prefetch:
    def __init__(self, k, es, nc, name, src3, KC, ncols, piece, engines):
        self.k = k
        self.src3 = src3
        self.piece = piece
        self.ncols = ncols
        self.engines = engines
        self.dst = sb(es, nc, name, [128, KC, ncols], BF16)
        self.st = [sb(es, nc, "%s_st%d" % (name, i), [128, KC, piece], F32) for i in range(2)]
        self.stb = [Buf(), Buf()]
        self.npieces = (ncols + piece - 1) // piece
        self.bufs = [Buf() for _ in range(self.npieces)]
        self.i = 0

    def step(self, n=1):
        k = self.k
        for _ in range(n):
            if self.i >= self.npieces:
                return
            i = self.i
            self.i += 1
            c0 = i * self.piece
            w = min(self.piece, self.ncols - c0)
            s_ = self.st[i % 2]
            k.dma(s_[:, :, 0:w], self.src3[:, :, c0:c0 + w], writes=[self.stb[i % 2]])
            e = self.engines[i % len(self.engines)]
            k.op(e, lambda eng: eng.tensor_copy(self.dst[:, :, c0:c0 + w], s_[:, :, 0:w]),
                 reads=[self.stb[i % 2]], writes=[self.bufs[i]])

    def finish(self):
        self.step(self.npieces)


def phase0(k, nc, C, src_list, hook=None):
    with ExitStack() as es:
        NBF = 4
        xin = [sb(es, nc, "p0_x%d" % i, [128, D], F32) for i in range(NBF)]
        xinb = [Buf() for _ in range(NBF)]
        xo = [sb(es, nc, "p0_o%d" % i, [128, 8, 512], BF16) for i in range(2)]
        xob = [Buf(), Buf()]
        ps = [psb(es, nc, "p0_ps%d" % i, [128, 512], F32) for i in range(8)]
        psbuf = [Buf() for _ in range(8)]
        it = 0
        for (src, nrows, toff) in src_list:
            for blk in range(nrows // 128):
                xi = xin[it % NBF]
                g = (it // 4) % 2
                b4 = it % 4
                k.dma(xi[:, :], src[blk * 128:(blk + 1) * 128, :], writes=[xinb[it % NBF]])
                for hf in range(2):
                    pi = (it * 2 + hf) % 8
                    for j in range(4):
                        kc = hf * 4 + j
                        k.op("pe", lambda e: e.transpose(ps[pi][:, j * 128:(j + 1) * 128], xi[:, kc * 128:(kc + 1) * 128], C.ident[:, :]),
                             reads=[xinb[it % NBF], C.identb], writes=[psbuf[pi]], sig=(j == 3))
                    dst = xo[g][:, hf * 4:(hf + 1) * 4, b4 * 128:(b4 + 1) * 128]
                    srcp = ps[pi][:, :].rearrange("p (a b) -> p a b", a=4)
                    if hf == 0:
                        k.op("act", lambda e: e.copy(dst, srcp), reads=[psbuf[pi]], writes=[xob[g]])
                    else:
                        k.op("dve", lambda e: e.tensor_copy(dst, srcp), reads=[psbuf[pi]], writes=[xob[g]])
                if b4 == 3:
                    t0 = toff + (blk - 3) * 128
                    k.dma(C.xT_all[:, :, t0:t0 + 512], xo[g][:, :, :], reads=[xob[g]], writes=[C.xT_buf], q="pool")
                it += 1
                if hook is not None:
                    hook(it)
    k.barrier()


def phase1(k, nc, C, L, PW):
    with ExitStack() as es:
        PW.finish()
        win = PW.dst
        k._wait("pe", [b.w for b in PW.bufs if b.w is not None])
        winb = Buf()
        gb = sb(es, nc, "p1_gb", [128, 8], F32)
        gbb = Buf()
        k.dma(gb[:, 0:4], L.b_igate.partition_broadcast(128), writes=[gbb])
        k.dma(gb[:, 4:8], L.b_fgate.partition_broadcast(128), writes=[gbb])
        xt = [sb(es, nc, "p1_xt%d" % i, [128, 8, 512], BF16) for i in range(2)]
        xtb = [Buf(), Buf()]
        NST = 4
        stg = [sb(es, nc, "p1_stg%d" % i, [128, 512], BF16) for i in range(NST)]
        stgb = [Buf() for _ in range(NST)]
        sg = [sb(es, nc, "p1_sg%d" % i, [128, 512], BF16) for i in range(2)]
        sgb = [Buf(), Buf()]
        tst = [sb(es, nc, "p1_tst%d" % i, [128, 512], BF16) for i in range(NST)]
        tstb = [Buf() for _ in range(NST)]
        gst = [sb(es, nc, "p1_gst%d" % i, [128, 8], F32) for i in range(2)]
        gstb = [Buf(), Buf()]
        ps = [psb(es, nc, "p1_ps%d" % i, [128, 512], F32) for i in range(8)]
        psbuf = [Buf() for _ in range(8)]
        st = {"ps": 0, "stg": 0, "tst": 0, "ev": 0, "sg": 0, "gst": 0}

        def mm_feat(xtile, xbuf, col0):
            pi = st["ps"] % 8
            st["ps"] += 1
            for kc in range(8):
                k.op("pe", lambda e, kc=kc, pi=pi: e.matmul(ps[pi][:, :], win[:, kc, col0:col0 + 128],
                                                           xtile[:, kc, :], start=(kc == 0), stop=(kc == 7)),
                     reads=[winb, xbuf], writes=[psbuf[pi]], sig=(kc == 7))
            return pi

        def evac_copy(pi, dst_sb, dst_buf):
            eng = "act" if st["ev"] % 2 == 0 else "dve"
            st["ev"] += 1
            if eng == "act":
                k.op("act", lambda e: e.copy(dst_sb, ps[pi][:, 0:dst_sb.shape[1]]), reads=[psbuf[pi]], writes=[dst_buf])
            else:
                k.op("dve", lambda e: e.tensor_copy(dst_sb, ps[pi][:, 0:dst_sb.shape[1]]), reads=[psbuf[pi]],
                     writes=[dst_buf])

        for ti in range(16):
            own = ti >= 8
            lastprev = ti == 7
            t0 = ti * 512
            xtile = xt[ti % 2]
            xbuf = xtb[ti % 2]
            k.dma(xtile[:, :, :], C.xT_all[:, :, t0:t0 + 512], reads=[C.xT_buf], writes=[xbuf])
            if own or lastprev:
                gcol = t0 - 7 * 512
                for j in range(2):
                    pa = mm_feat(xtile, xbuf, j * 128)
                    pg = mm_feat(xtile, xbuf, 256 + j * 128)
                    si = st["sg"] % 2
                    st["sg"] += 1
                    k.op("act", lambda e, si=si, pg=pg: e.activation(out=sg[si][:, :], in_=ps[pg][:, :], func=AF.Sigmoid),
                         reads=[psbuf[pg]], writes=[sgb[si]])
                    oi = st["stg"] % NST
                    st["stg"] += 1
                    k.op("dve", lambda e, si=si, pa=pa, oi=oi: e.tensor_tensor(out=stg[oi][:, :], in0=ps[pa][:, :],
                                                                             in1=sg[si][:, :], op=ALU.mult),
                         reads=[psbuf[pa], sgb[si]], writes=[stgb[oi]])
                    k.dma(C.glu[:, j, gcol:gcol + 512], stg[oi][:, :], reads=[stgb[oi]], writes=[C.glu_buf], q="pool")
            feats = []
            for j in range(4):
                if own or lastprev or j >= 2:
                    feats.append((512 + j * 128, C.mqk[:, j, t0:t0 + 512], C.mqk_buf))
            if own:
                for j in range(4):
                    feats.append((1544 + j * 128, C.dq[:, j, t0 - TO:t0 - TO + 512], C.dq_buf))
            for j in range(4):
                feats.append((2056 + j * 128, C.dk[:, j, t0:t0 + 512], C.dk_buf))
            for (col0, dst, dbuf) in feats:
                pi = mm_feat(xtile, xbuf, col0)
                oi = st["stg"] % NST
                st["stg"] += 1
                evac_copy(pi, stg[oi][:, :], stgb[oi])
                k.dma(dst, stg[oi][:, :], reads=[stgb[oi]], writes=[dbuf], q="pool")
            for blk in range(4):
                tb = t0 + blk * 128
                pi = st["ps"] % 8
                st["ps"] += 1
                for kc in range(8):
                    k.op("pe", lambda e, kc=kc, pi=pi: e.matmul(ps[pi][:, :], xtile[:, kc, blk * 128:(blk + 1) * 128],
                                                               win[:, kc, 1024:1536], start=(kc == 0), stop=(kc == 7)),
                         reads=[winb, xbuf], writes=[psbuf[pi]], sig=(kc == 7))
                oi = st["tst"] % NST
                st["tst"] += 1
                k.op("dve", lambda e, pi=pi, oi=oi: e.tensor_copy(tst[oi][:, 0:256], ps[pi][:, 0:256]),
                     reads=[psbuf[pi]], writes=[tstb[oi]])
                if own:
                    k.op("act", lambda e, pi=pi, oi=oi: e.activation(out=tst[oi][:, 256:512], in_=ps[pi][:, 256:512],
                                                                    func=AF.Sigmoid),
                         reads=[psbuf[pi]], writes=[tstb[oi]])
                k.dma(C.mv[tb:tb + 128, :], tst[oi][:, 0:256], reads=[tstb[oi]], writes=[C.mv_buf], q="pool")
                if own:
                    k.dma(C.mo[tb - TO:tb - TO + 128, :], tst[oi][:, 256:512], reads=[tstb[oi]], writes=[C.mo_buf],
                          q="pool")
                pi = st["ps"] % 8
                st["ps"] += 1
                for kc in range(8):
                    k.op("pe", lambda e, kc=kc, pi=pi: e.matmul(ps[pi][:, 0:8], xtile[:, kc, blk * 128:(blk + 1) * 128],
                                                               win[:, kc, 1536:1544], start=(kc == 0), stop=(kc == 7)),
                         reads=[winb, xbuf], writes=[psbuf[pi]], sig=(kc == 7))
                gi = st["gst"] % 2
                st["gst"] += 1
                k.op("dve", lambda e, pi=pi, gi=gi: e.tensor_tensor(out=gst[gi][:, :], in0=ps[pi][:, 0:8], in1=gb[:, :],
                                                                   op=ALU.add),
                     reads=[psbuf[pi], gbb], writes=[gstb[gi]])
                k.dma(C.gates[tb:tb + 128, :], gst[gi][:, :], reads=[gstb[gi]], writes=[C.gates_buf], q="pool")
                pi = st["ps"] % 8
                st["ps"] += 1
                for kc in range(8):
                    k.op("pe", lambda e, kc=kc, pi=pi: e.matmul(ps[pi][:, :], xtile[:, kc, blk * 128:(blk + 1) * 128],
                                                               win[:, kc, 2568:3080], start=(kc == 0), stop=(kc == 7)),
                         reads=[winb, xbuf], writes=[psbuf[pi]], sig=(kc == 7))
                oi = st["tst"] % NST
                st["tst"] += 1
                evac_copy(pi, tst[oi][:, :], tstb[oi])
                k.dma(C.dv[tb:tb + 128, :], tst[oi][:, :], reads=[tstb[oi]], writes=[C.dv_buf], q="pool")
    k.barrier()


def rstd_from(k, C, out_ap, in_ap, bufs_r, bufs_w, tmp_ap, tmpbuf):
    k.op("act", lambda e: e.activation(out=tmp_ap, in_=in_ap, func=AF.Ln), reads=bufs_r, writes=[tmpbuf])
    k.op("act", lambda e: e.activation(out=out_ap, in_=tmp_ap, func=AF.Exp, scale=-0.5), reads=[tmpbuf], writes=bufs_w)


CP_DWB, CP_LNG, CP_LNB, CP_PWB, CP_MLB, CP_DWW, CP_MLW = 0, 2, 4, 6, 8, 12, 74
NCP = 90


def phase2(k, nc, C, L):
    with ExitStack() as es:
        glu = sb(es, nc, "p2_glu", [128, 2, 512 + TO], BF16)
        glub = Buf()
        for j in range(2):
            k.dma(glu[:, j, :], C.glu[:, j, :], reads=[C.glu_buf], writes=[glub])
        k.op("dve", lambda e: e.tensor_scalar(glu[:, :, 482:512], glu[:, :, 482:512], C.flag[:, 0:1], None, ALU.mult),
             reads=[glub, C.flagb], writes=[glub])
        cp = sb(es, nc, "p2_cp", [128, NCP], F32)
        cpb = Buf()
        k.dma(cp[:, :], L.colpack[:, :], writes=[cpb])
        dg = sb(es, nc, "p2_dg", [128, 62, 128], BF16)
        dgb = Buf()
        for i in range(62):
            k.op("dve" if i % 2 == 0 else "pool",
                 lambda e: e.tensor_scalar(dg[:, i, :], C.identbf[:, :], cp[:, CP_DWW + i:CP_DWW + i + 1], None, ALU.mult),
                 reads=[C.identbfb, cpb], writes=[dgb])
        pw = sb(es, nc, "p2_pw", [128, 2, 256], BF16)
        pwb = Buf()
        load_cast(k, es, nc, "p2pw", pw, pwb, L.conv_pw_w.rearrange("(kc p) n -> p kc n", p=128), 256, 256)
        ones = sb(es, nc, "p2_ones", [128, 128], F32)
        onesb = Buf()
        k.op("dve", lambda e: e.memset(ones[:, :], 1.0), writes=[onesb])
        yb = [[sb(es, nc, "p2_y%d_%d" % (p, j), [128, 512], F32) for j in range(2)] for p in range(2)]
        ybb = [[Buf(), Buf()] for _ in range(2)]
        sq = [[sb(es, nc, "p2_sq%d_%d" % (p, j), [128, 512], F32) for j in range(2)] for p in range(2)]
        sqb = [[Buf(), Buf()] for _ in range(2)]
        msb = sb(es, nc, "p2_ms", [128, 512], F32); msbb = Buf()
        tt = sb(es, nc, "p2_tt", [128, 512], F32); ttb = Buf()
        var = sb(es, nc, "p2_var", [128, 512], F32); varb = Buf()
        lnt = sb(es, nc, "p2_lnt", [128, 512], F32); lntb = Buf()
        rstd = sb(es, nc, "p2_rstd", [128, 512], F32); rstdb = Buf()
        dd = [sb(es, nc, "p2_d%d" % j, [128, 512], F32) for j in range(2)]
        ddb = [Buf(), Buf()]
        act = [sb(es, nc, "p2_a%d" % j, [128, 512], BF16) for j in range(2)]
        actb = [Buf(), Buf()]
        og = [sb(es, nc, "p2_o%d" % j, [128, 512], BF16) for j in range(2)]
        ogb = [Buf(), Buf()]
        ps = [psb(es, nc, "p2_ps%d" % i, [128, 512], F32) for i in range(6)]
        pb = [Buf() for _ in range(6)]

        def emit_conv(ti):
            p = ti % 2
            c0 = 512 + ti * 512
            for j in range(2):
                for t in range(31):
                    k.op("pe", lambda e: e.matmul(ps[j][:, :], dg[:, j * 31 + t, :], glu[:, j, c0 - 30 + t:c0 - 30 + t + 512],
                                                  start=(t == 0), stop=(t == 30)),
                         reads=[dgb, glub], writes=[pb[j]], sig=(t == 30))
                k.op("act", lambda e: e.activation(out=yb[p][j][:, :], in_=ps[j][:, :], func=AF.Identity,
                                                   bias=cp[:, CP_DWB + j:CP_DWB + j + 1]),
                     reads=[pb[j], cpb], writes=[ybb[p][j]])
                k.op("act", lambda e: e.activation(out=sq[p][j][:, :], in_=ps[j][:, :], func=AF.Square,
                                                   bias=cp[:, CP_DWB + j:CP_DWB + j + 1]),
                     reads=[pb[j], cpb], writes=[sqb[p][j]])

        def emit_stats_mm(ti):
            p = ti % 2
            for j in range(2):
                k.op("pe", lambda e: e.matmul(ps[2][:, :], ones[:, :], yb[p][j][:, :], start=(j == 0), stop=(j == 1)),
                     reads=[onesb, ybb[p][j]], writes=[pb[2]], sig=(j == 1))
            for j in range(2):
                k.op("pe", lambda e: e.matmul(ps[3][:, :], ones[:, :], sq[p][j][:, :], start=(j == 0), stop=(j == 1)),
                     reads=[onesb, sqb[p][j]], writes=[pb[3]], sig=(j == 1))

        def emit_ln(ti):
            p = ti % 2
            k.op("act", lambda e: e.activation(out=msb[:, :], in_=ps[2][:, :], func=AF.Copy, scale=1.0 / 256),
                 reads=[pb[2]], writes=[msbb])
            k.op("dve", lambda e: e.tensor_tensor(out=tt[:, :], in0=msb[:, :], in1=msb[:, :], op=ALU.mult),
                 reads=[msbb], writes=[ttb])
            k.op("dve", lambda e: e.scalar_tensor_tensor(out=var[:, :], in0=ps[3][:, :], scalar=1.0 / 256, in1=tt[:, :],
                                                         op0=ALU.mult, op1=ALU.subtract),
                 reads=[pb[3], ttb], writes=[varb])
            k.op("dve", lambda e: e.tensor_scalar(var[:, :], var[:, :], EPS, None, ALU.add), reads=[varb], writes=[varb])
            rstd_from(k, C, rstd[:, :], var[:, :], [varb], [rstdb], lnt[:, :], lntb)
            for j in range(2):
                k.op("dve", lambda e: e.tensor_tensor(out=dd[j][:, :], in0=yb[p][j][:, :], in1=msb[:, :], op=ALU.subtract),
                     reads=[ybb[p][j], msbb], writes=[ddb[j]])
                k.op("pool", lambda e: e.tensor_tensor(out=dd[j][:, :], in0=dd[j][:, :], in1=rstd[:, :], op=ALU.mult),
                     reads=[ddb[j], rstdb], writes=[ddb[j]])
                k.op("act", lambda e: e.activation(out=act[j][:, :], in_=dd[j][:, :], func=AF.Silu,
                                                   scale=cp[:, CP_LNG + j:CP_LNG + j + 1],
                                                   bias=cp[:, CP_LNB + j:CP_LNB + j + 1]),
                     reads=[ddb[j], cpb], writes=[actb[j]])

        def emit_pw(ti):
            for co in range(2):
                for ci in range(2):
                    k.op("pe", lambda e: e.matmul(ps[4 + co][:, :], pw[:, ci, co * 128:(co + 1) * 128], act[ci][:, :],
                                                  start=(ci == 0), stop=(ci == 1)),
                         reads=[pwb, actb[ci]], writes=[pb[4 + co]], sig=(ci == 1))
                k.op("act", lambda e: e.activation(out=og[co][:, :], in_=ps[4 + co][:, :], func=AF.Identity,
                                                   bias=cp[:, CP_PWB + co:CP_PWB + co + 1]),
                     reads=[pb[4 + co], cpb], writes=[ogb[co]])
                k.dma(C.ymixT[:, co, ti * 512:(ti + 1) * 512], og[co][:, :], reads=[ogb[co]], writes=[C.ymix_buf], q="pool")

        emit_conv(0)
        for ti in range(8):
            emit_stats_mm(ti)
            if ti + 1 < 8:
                emit_conv(ti + 1)
            emit_ln(ti)
            emit_pw(ti)
    k.barrier()


def phase4(k, nc, C, L, hook=None):
    lam_init = 0.8 - 0.6 * math.exp(-0.3 * L.idx)
    X = mybir.AxisListType.X
    with ExitStack() as es:
        lamv = sb(es, nc, "p4_lamv", [128, 4, 64], F32); lamb = Buf()
        for i, a in enumerate((L.lam_q1, L.lam_k1, L.lam_q2, L.lam_k2)):
            k.dma(lamv[:, i, :], a.partition_broadcast(128), writes=[lamb])
        prod = sb(es, nc, "p4_prod", [128, 2, 64], F32); prodb = Buf()
        dots = sb(es, nc, "p4_dots", [128, 4], F32); dotsb = Buf()
        for j in range(2):
            k.op("dve", lambda e: e.tensor_tensor(out=prod[:, j, :], in0=lamv[:, 2 * j, :], in1=lamv[:, 2 * j + 1, :], op=ALU.mult),
                 reads=[lamb], writes=[prodb])
            k.op("dve", lambda e: e.reduce_sum(dots[:, j:j + 1], prod[:, j, :], X), reads=[prodb], writes=[dotsb])
        k.op("act", lambda e: e.activation(out=dots[:, 2:4], in_=dots[:, 0:2], func=AF.Exp), reads=[dotsb], writes=[dotsb])
        neglam = sb(es, nc, "p4_nl", [128, 1], F32); nlb = Buf()
        k.op("dve", lambda e: e.tensor_tensor(out=neglam[:, :], in0=dots[:, 3:4], in1=dots[:, 2:3], op=ALU.subtract),
             reads=[dotsb], writes=[nlb])
        k.op("dve", lambda e: e.tensor_scalar(neglam[:, :], neglam[:, :], -lam_init, None, ALU.add), reads=[nlb], writes=[nlb])
        grow = sb(es, nc, "p4_grow", [128, 128], F32); growb = Buf()
        k.dma(grow[:, :], L.da_norm_g.partition_broadcast(128), writes=[growb])
        k.op("dve", lambda e: e.tensor_scalar(grow[:, :], grow[:, :], 1.0 - lam_init, None, ALU.mult), reads=[growb], writes=[growb])
        cmf = sb(es, nc, "p4_cmf", [128, 4, 512], F32); cmfb = Buf()
        cmask = sb(es, nc, "p4_cm", [128, 4, 512], BF16); cmb = Buf()
        k.dma(cmf[:, :, :], C.cmask_d[:, :, :], writes=[cmfb])
        k.op("dve", lambda e: e.tensor_copy(cmask[:, :, :], cmf[:, :, :]), reads=[cmfb], writes=[cmb])
        KT = [sb(es, nc, "p4_k%d" % i, [128, TA], BF16) for i in range(2)]; KTb = [Buf(), Buf()]
        QT = [sb(es, nc, "p4_q%d" % i, [128, TO], BF16) for i in range(2)]; QTb = [Buf(), Buf()]
        V = [sb(es, nc, "p4_v%d" % i, [128, 64, 129], BF16) for i in range(2)]; Vb = [Buf(), Buf()]
        for i in range(2):
            k.op("pool", lambda e: e.memset(V[i][:, :, 128:129], 1.0), writes=[Vb[i]])
        PT = [sb(es, nc, "p4_p%d" % i, [128, 1024], BF16) for i in range(2)]; PTb = [Buf() for _ in range(2)]
        o0 = [sb(es, nc, "p4_o0%d" % i, [128, 128], F32) for i in range(4)]; o0b = [Buf() for _ in range(4)]
        oo = [sb(es, nc, "p4_oo%d" % i, [128, 128], F32) for i in range(4)]; oob = [Buf() for _ in range(4)]
        yy = [sb(es, nc, "p4_yy%d" % i, [128, 128], BF16) for i in range(4)]; yyb = [Buf() for _ in range(4)]
        sm = [sb(es, nc, "p4_sm%d" % i, [128, 8], F32) for i in range(4)]; smb = [Buf() for _ in range(4)]
        st6 = [sb(es, nc, "p4_st%d" % i, [128, 6], F32) for i in range(4)]; st6b = [Buf() for _ in range(4)]
        ys = [sb(es, nc, "p4_ys%d" % i, [128, 512], BF16) for i in range(2)]; ysb = [Buf(), Buf()]
        Sps = [psb(es, nc, "p4_S%d" % i, [128, 1024], F32) for i in range(2)]; Sb = [Buf() for _ in range(2)]
        Aps = [psb(es, nc, "p4_A%d" % i, [128, 512], F32) for i in range(3)]; Ab = [Buf() for _ in range(3)]
        tp = psb(es, nc, "p4_tp", [128, 1024], BF16); tpb = Buf()

        def acc(a):
            return Aps[a // 3][:, (a % 3) * 129:(a % 3) * 129 + 129], Ab[a // 3]

        for h in range(4):
            hi = h % 2
            k.dma(KT[hi][:, :], C.dk[:, h, :], reads=[C.dk_buf], writes=[KTb[hi]])
            k.dma(QT[hi][:, :], C.dq[:, h, :], reads=[C.dq_buf], writes=[QTb[hi]])
            k.dma(V[hi][:, :, 0:128], C.dv[:, h * 128:(h + 1) * 128].rearrange("(n p) f -> p n f", p=128),
                  reads=[C.dv_buf], writes=[Vb[hi]])
            for qt in range(8):
                if hook is not None:
                    hook()
                nkb = 32 + 4 * qt + 4
                d0 = 32 + 4 * qt

                def emit_S(kb):
                    dj = kb - d0
                    si = kb % 2
                    c0 = 128 * dj if dj > 0 else 0
                    for m in range(2):
                        k.op("pe", lambda e: e.matmul(Sps[si][:, m * 512 + c0:(m + 1) * 512], KT[hi][m * 64:(m + 1) * 64, kb * 128:(kb + 1) * 128],
                                                      QT[hi][m * 64:(m + 1) * 64, qt * 512 + c0:(qt + 1) * 512],
                                                      start=True, stop=(dj < 0)),
                             reads=[KTb[hi], QTb[hi]], writes=[Sb[si]], sig=(dj < 0 and m == 1))
                        if dj >= 0:
                            k.op("pe", lambda e: e.matmul(Sps[si][:, m * 512 + c0:(m + 1) * 512], C.identbf[:, :], cmask[:, dj, c0:512],
                                                          start=False, stop=True),
                                 reads=[C.identbfb, cmb], writes=[Sb[si]], sig=(m == 1))
                    bias = C.flag[:, 1:2] if kb < 32 else 0.0
                    if c0 == 0:
                        k.op("act", lambda e: e.activation(out=PT[si][:, :], in_=Sps[si][:, :], func=AF.Exp, bias=bias, scale=0.125),
                             reads=[Sb[si], C.flagb], writes=[PTb[si]])
                    else:
                        k.op("act", lambda e: e.activation(out=PT[si][:, :].rearrange("p (m c) -> p m c", m=2)[:, :, c0:512],
                                                           in_=Sps[si][:, :].rearrange("p (m c) -> p m c", m=2)[:, :, c0:512],
                                                           func=AF.Exp, bias=bias, scale=0.125),
                             reads=[Sb[si], C.flagb], writes=[PTb[si]])

                def emit_PV(kb):
                    dj = kb - d0
                    si = kb % 2
                    lst = []
                    for qb in range(4):
                        if dj >= 0 and qb < dj:
                            continue
                        for m in range(2):
                            lst.append((qb, m))
                    for idx, (qb, m) in enumerate(lst):
                        ap, ab = acc(qb * 2 + m)
                        k.op("pe", lambda e: e.matmul(ap, PT[si][:, m * 512 + qb * 128:m * 512 + (qb + 1) * 128], V[hi][:, kb, :],
                                                      start=(kb == 0 and (qb * 2 + m) % 3 == 0), stop=(kb == d0 + qb), skip_group_check=True),
                             reads=[PTb[si], Vb[hi]], writes=[ab], sig=(idx == len(lst) - 1))

                emit_S(0)
                for kb in range(nkb):
                    if kb + 1 < nkb:
                        emit_S(kb + 1)
                    emit_PV(kb)
                for step in range(9):
                    for qb in range(4):
                        i = qb
                        a0, ab0 = acc(qb * 2)
                        a1, ab1 = acc(qb * 2 + 1)
                        if step == 0:
                            k.op("dve", lambda e: e.reciprocal(sm[i][:, 0:1], a0[:, 128:129]), reads=[ab0], writes=[smb[i]])
                            k.op("dve", lambda e: e.reciprocal(sm[i][:, 1:2], a1[:, 128:129]), reads=[ab1], writes=[smb[i]])
                        elif step == 1:
                            k.op("dve", lambda e: e.tensor_tensor(out=sm[i][:, 2:3], in0=sm[i][:, 1:2], in1=neglam[:, 0:1], op=ALU.mult),
                                 reads=[smb[i], nlb], writes=[smb[i]])
                            k.op("dve", lambda e: e.tensor_scalar(o0[i][:, :], a0[:, 0:128], sm[i][:, 0:1], None, ALU.mult),
                                 reads=[ab0, smb[i]], writes=[o0b[i]])
                        elif step == 2:
                            k.op("dve", lambda e: e.scalar_tensor_tensor(out=oo[i][:, :], in0=a1[:, 0:128], scalar=sm[i][:, 2:3],
                                                                         in1=o0[i][:, :], op0=ALU.mult, op1=ALU.add),
                                 reads=[ab1, smb[i], o0b[i]], writes=[oob[i]])
                        elif step == 3:
                            k.op("dve", lambda e: e.bn_stats(st6[i][:, :], oo[i][:, :]), reads=[oob[i]], writes=[st6b[i]])
                        elif step == 4:
                            k.op("dve", lambda e: e.bn_aggr(sm[i][:, 3:5], st6[i][:, :]), reads=[st6b[i]], writes=[smb[i]])
                        elif step == 5:
                            k.op("dve", lambda e: e.scalar_tensor_tensor(out=sm[i][:, 5:6], in0=sm[i][:, 3:4], scalar=sm[i][:, 3:4],
                                                                         in1=sm[i][:, 4:5], op0=ALU.mult, op1=ALU.add),
                                 reads=[smb[i]], writes=[smb[i]])
                        elif step == 6:
                            k.op("dve", lambda e: e.tensor_scalar(sm[i][:, 5:6], sm[i][:, 5:6], EPS, None, ALU.add), reads=[smb[i]], writes=[smb[i]])
                        elif step == 7:
                            rstd_from(k, C, sm[i][:, 7:8], sm[i][:, 5:6], [smb[i]], [smb[i]], sm[i][:, 6:7], smb[i])
                        else:
                            k.op("dve", lambda e: e.scalar_tensor_tensor(out=yy[i][:, :], in0=oo[i][:, :], scalar=sm[i][:, 7:8],
                                                                         in1=grow[:, :], op0=ALU.mult, op1=ALU.mult),
                                 reads=[oob[i], smb[i], growb], writes=[yyb[i]])
                            k.op("pe", lambda e: e.transpose(tp[:, qb * 128:(qb + 1) * 128], yy[i][:, :], C.identbf[:, :]),
                                 reads=[yyb[i], C.identbfb], writes=[tpb], sig=True)
                yi = (h * 8 + qt) % 2
                k.op("dve", lambda e: e.tensor_copy(ys[yi][:, :], tp[:, 0:512]), reads=[tpb], writes=[ysb[yi]])
                k.dma(C.ymixT[:, 4 + h, qt * 512:(qt + 1) * 512], ys[yi][:, :], reads=[ysb[yi]], writes=[C.ymix_buf], q="pool")
    k.barrier()


def phase3(k, nc, C, L):
    NB = TA // 128
    NO = TO // 128
    NG = NB * 4
    with ExitStack() as es:
        cp = sb(es, nc, "p3_cp", [128, NCP], F32); cpb = Buf()
        k.dma(cp[:, :], L.colpack[:, :], writes=[cpb])
        dgm = sb(es, nc, "p3_dg", [128, 16, 128], BF16); dgb = Buf()
        for i in range(16):
            k.op("dve", lambda e: e.tensor_scalar(dgm[:, i, :], C.identbf[:, :], cp[:, CP_MLW + i:CP_MLW + i + 1], None, ALU.mult),
                 reads=[C.identbfb, cpb], writes=[dgb])
        utri = sb(es, nc, "p3_u", [128, 128], F32); ub = Buf()
        k.dma(utri[:, :], C.utri_d[:, :], writes=[ub])
        ones = sb(es, nc, "p3_ones", [128, 128], F32); onesb = Buf()
        k.op("dve", lambda e: e.memset(ones[:, :], 1.0), writes=[onesb])
        mlg = sb(es, nc, "p3_mlg", [128, 256], F32); mlgb = Buf()
        k.dma(mlg[:, :], L.ml_norm_g.partition_broadcast(128), writes=[mlgb])
        qT = sb(es, nc, "p3_qT", [128, 2, TO], BF16); qTb = Buf()
        kT = sb(es, nc, "p3_kT", [128, 2, TA], BF16); kTb = Buf()
        gts = sb(es, nc, "p3_g", [128, NB, 8], F32); gtb = Buf()
        k.dma(gts[:, :, :], C.gates.rearrange("(n p) g -> p n g", p=128), reads=[C.gates_buf], writes=[gtb])
        Call = sb(es, nc, "p3_Call", [128, NO, 2, 65], BF16); Callb = Buf()
        vaug = sb(es, nc, "p3_v", [128, NB, 4, 65], BF16); vb = Buf()
        A = [psb(es, nc, "p3_A%d" % i, [128, 512], F32) for i in range(2)]; Apb = [Buf(), Buf()]
        Np = [psb(es, nc, "p3_N%d" % i, [128, 512], F32) for i in range(2)]; Npb = [Buf(), Buf()]
        Dp = [psb(es, nc, "p3_D%d" % i, [128, 512], F32) for i in range(2)]; Dpb = [Buf(), Buf()]
        tp = psb(es, nc, "p3_tp", [128, 1024], BF16); tpb = Buf()
        tp2 = psb(es, nc, "p3_tp2", [128, 1024], BF16)
        tpx = [tp, tp2]
        with ExitStack() as esA:
            inp = [sb(esA, nc, "p3_in%d" % i, [128, 3 + TO], BF16) for i in range(2)]; inpb = [Buf(), Buf()]
            na = 0
            for j in range(4):
                k.op("pool", lambda e: e.memset(inp[0][:, 0:3], 0.0), writes=[inpb[0]])
                k.dma(inp[0][:, 3:], C.mqk[:, j, 0:TO], reads=[C.mqk_buf], writes=[inpb[0]])
                k.dma(inp[1][:, 3:], C.mqk[:, j, TO:TA], reads=[C.mqk_buf], writes=[inpb[1]])
                k.op("dve", lambda e: e.tensor_scalar(inp[1][:, 0:3], inp[0][:, TO:TO + 3], C.flag[:, 0:1], None, ALU.mult),
                     reads=[inpb[0], C.flagb], writes=[inpb[1]])
                for part in range(2):
                    if part == 0 and j < 2:
                        continue
                    for ti in range(8):
                        ai = na % 2
                        na += 1
                        for t in range(4):
                            k.op("pe", lambda e: e.matmul(A[ai][:, :], dgm[:, j * 4 + t, :], inp[part][:, ti * 512 + t:ti * 512 + t + 512],
                                                          start=(t == 0), stop=(t == 3)),
                                 reads=[dgb, inpb[part]], writes=[Apb[ai]], sig=(t == 3))
                        if j < 2:
                            dst, dbuf = qT[:, j, ti * 512:(ti + 1) * 512], qTb
                        else:
                            c0 = part * TO + ti * 512
                            dst, dbuf = kT[:, j - 2, c0:c0 + 512], kTb
                        k.op("act", lambda e: e.activation(out=dst, in_=A[ai][:, :], func=AF.Silu, bias=cp[:, CP_MLB + j:CP_MLB + j + 1]),
                             reads=[Apb[ai], cpb], writes=[dbuf])
            k.barrier()
        lf = sb(es, nc, "p3_lf", [128, NB, 4], F32); lfb = Buf()
        li = sb(es, nc, "p3_li", [128, NB, 4], F32); lib = Buf()
        bcol = sb(es, nc, "p3_bc", [128, NG], F32); bcb = Buf()
        apr = sb(es, nc, "p3_ap", [128, NG], F32); aprb = Buf()
        eb = sb(es, nc, "p3_eb", [128, NG], F32); ebb = Buf()
        eg = sb(es, nc, "p3_eg", [128, NB, 4], F32); egb = Buf()
        egc = sb(es, nc, "p3_egc", [128, NB, 2], F32); egcb = Buf()
        k.op("act", lambda e: e.activation(out=lf[:, :, :], in_=gts[:, :, 4:8], func=AF.Exp, scale=-1.0), reads=[gtb], writes=[lfb])
        k.op("act", lambda e: e.activation(out=lf[:, :, :], in_=lf[:, :, :], func=AF.Ln, bias=1.0), reads=[lfb], writes=[lfb])
        k.op("dve", lambda e: e.tensor_scalar(lf[:, :, :], lf[:, :, :], -1.0, None, ALU.mult), reads=[lfb], writes=[lfb])
        k.op("dve", lambda e: e.tensor_copy(li[:, :, :], gts[:, :, 0:4]), reads=[gtb], writes=[lib])
        lf2 = lf[:, :, :].rearrange("p a b -> p (a b)")
        li2 = li[:, :, :].rearrange("p a b -> p (a b)")
        eg2 = eg[:, :, :].rearrange("p a b -> p (a b)")
        k.op("pe", lambda e: e.matmul(A[0][:, 0:NG], utri[:, :], lf2, start=True, stop=True), reads=[ub, lfb], writes=[Apb[0]])
        k.op("pe", lambda e: e.matmul(A[1][:, 0:NG], ones[:, :], lf2, start=True, stop=True), reads=[onesb, lfb], writes=[Apb[1]])
        k.op("dve", lambda e: e.tensor_copy(bcol[:, :], A[0][:, 0:NG]), reads=[Apb[0]], writes=[bcb])
        k.op("dve", lambda e: e.tensor_tensor(out=apr[:, :], in0=li2, in1=bcol[:, :], op=ALU.subtract), reads=[lib, bcb], writes=[aprb])
        k.op("act", lambda e: e.activation(out=apr[:, :], in_=apr[:, :], func=AF.Exp), reads=[aprb], writes=[aprb])
        k.op("act", lambda e: e.activation(out=eb[:, :], in_=bcol[:, :], func=AF.Exp), reads=[bcb], writes=[ebb])
        k.op("dve", lambda e: e.tensor_scalar(eb[:, :], eb[:, :], 0.125, None, ALU.mult), reads=[ebb], writes=[ebb])
        k.op("act", lambda e: e.activation(out=eg2, in_=A[1][:, 0:NG], func=AF.Exp), reads=[Apb[1]], writes=[egb])
        for jc in range(2):
            for hp in range(2):
                ps_ = slice(hp * 64, (hp + 1) * 64)
                k.op("dve", lambda e: e.tensor_copy(egc[ps_, :, jc], eg[ps_, :, 2 * jc + hp]), reads=[egb], writes=[egcb])
        with ExitStack() as esB:
            ktok = sb(esB, nc, "p3_ktok", [128, NB, 256], BF16); ktokb = [Buf() for _ in range(NB)]
            dC = sb(esB, nc, "p3_dC", [128, NB, 2, 65], F32); dCb = [[[Buf(), Buf()] for _ in range(2)] for _ in range(NB)]
            tps = [Buf(), Buf()]
            for h in range(4):
                k.dma(vaug[:, :, h, 0:64], C.mv[:, h * 64:(h + 1) * 64].rearrange("(n p) d -> p n d", p=128), reads=[C.mv_buf], writes=[vb])
            k.op("pool", lambda e: e.memset(vaug[:, :, :, 64:65], 1.0), writes=[vb])
            for blk in range(NB):
                for jc in range(2):
                    k.op("pe", lambda e: e.transpose(tpx[blk % 2][:, jc * 128:(jc + 1) * 128],
                                                     kT[:, jc, blk * 128:(blk + 1) * 128], C.identbf[:, :]),
                         reads=[kTb, C.identbfb], writes=[tps[blk % 2]], sig=(jc == 1))
                if blk % 2 == 0:
                    k.op("dve", lambda e: e.tensor_copy(ktok[:, blk, :], tp[:, 0:256]), reads=[tps[0]], writes=[ktokb[blk]])
                else:
                    k.op("act", lambda e: e.copy(ktok[:, blk, :], tp2[:, 0:256]), reads=[tps[1]], writes=[ktokb[blk]])
            pvp = [sb(esB, nc, "p3_pv%d" % i, [128, 130], BF16) for i in range(8)]; pvpb = [Buf() for _ in range(8)]
            nv = 0
            for blk in range(NB):
                own = blk >= NB // 2
                ob = blk - NB // 2
                for jc in range(2):
                    pv, pvb_ = pvp[nv % 8][:, :], pvpb[nv % 8]
                    for hp in range(2):
                        h = 2 * jc + hp
                        k.op("pool" if hp == 0 else "dve",
                             lambda e: e.tensor_scalar(pv[:, hp * 65:(hp + 1) * 65], vaug[:, blk, h, :],
                                                       apr[:, blk * 4 + h:blk * 4 + h + 1], None, ALU.mult),
                             reads=[vb, aprb], writes=[pvb_])
                    di = nv % 2
                    nv += 1
                    k.op("pe", lambda e: e.matmul(Dp[di][:, 0:130], ktok[:, blk, jc * 128:(jc + 1) * 128], pv, start=True, stop=True),
                         reads=[ktokb[blk], pvb_], writes=[Dpb[di]], sig=True)
                    for hp in range(2):
                        h = 2 * jc + hp
                        ps_ = slice(hp * 64, (hp + 1) * 64)
                        if hp == 0:
                            k.op("dve", lambda e: e.tensor_scalar(dC[ps_, blk, jc, :], Dp[di][ps_, hp * 65:(hp + 1) * 65],
                                                                  eg[ps_, blk, h:h + 1], None, ALU.mult),
                                 reads=[Dpb[di], egb], writes=[dCb[blk][jc][hp]])
                        else:
                            k.op("act", lambda e: e.activation(out=dC[ps_, blk, jc, :], in_=Dp[di][ps_, hp * 65:(hp + 1) * 65],
                                                               func=AF.Copy, scale=eg[ps_, blk, h:h + 1]),
                                 reads=[Dpb[di], egb], writes=[dCb[blk][jc][hp]])
            Cst = sb(esB, nc, "p3_C", [128, 2, 65], F32)
            Ch = sb(esB, nc, "p3_Ch", [128, NO, 2, 65], F32)
            Cb2 = [Buf(), Buf()]
            k.op("dve", lambda e: e.memset(Cst[:, :, :], 0.0), writes=Cb2)
            for blk in range(NB - 1):
                own_next = blk + 1 >= NB // 2
                for jc in range(2):
                    if blk + 1 == NB // 2:
                        k.op("dve", lambda e: e.scalar_tensor_tensor(out=Cst[:, jc, :], in0=Cst[:, jc, :], scalar=egc[:, blk, jc:jc + 1],
                                                                     in1=dC[:, blk, jc, :], op0=ALU.mult, op1=ALU.add),
                             reads=[Cb2[jc], egcb] + dCb[blk][jc], writes=[Cb2[jc]])
                        k.op("dve", lambda e: e.tensor_scalar(Ch[:, 0, jc, :], Cst[:, jc, :], C.flag[:, 0:1], None, ALU.mult),
                             reads=[C.flagb, Cb2[jc]], writes=[Cb2[jc]])
                        continue
                    if blk + 1 < NB // 2:
                        src, dst = Cst[:, jc, :], Cst[:, jc, :]
                    else:
                        ob = blk - NB // 2
                        src, dst = Ch[:, ob, jc, :], Ch[:, ob + 1, jc, :]
                    k.op("dve", lambda e: e.scalar_tensor_tensor(out=dst, in0=src, scalar=egc[:, blk, jc:jc + 1],
                                                                 in1=dC[:, blk, jc, :], op0=ALU.mult, op1=ALU.add),
                         reads=[Cb2[jc], egcb] + dCb[blk][jc], writes=[Cb2[jc]])
            for q in range(4):
                k.op("act" if q % 2 == 0 else "pool",
                     (lambda e: e.copy(Call[:, q * 8:(q + 1) * 8, :, :], Ch[:, q * 8:(q + 1) * 8, :, :])) if q % 2 == 0 else
                     (lambda e: e.tensor_copy(Call[:, q * 8:(q + 1) * 8, :, :], Ch[:, q * 8:(q + 1) * 8, :, :])),
                     reads=Cb2, writes=[Callb])
            k.barrier()
        mo = sb(es, nc, "p3_mo", [128, NO, 256], BF16); mob = Buf()
        k.dma(mo[:, :, :], C.mo.rearrange("(n p) f -> p n f", p=128), reads=[C.mo_buf], writes=[mob])
        Cbf = [sb(es, nc, "p3_Cb%d" % i, [128, 2, 65], BF16) for i in range(2)]; Cbfb = [Buf(), Buf()]
        pvp = [sb(es, nc, "p3_pw%d" % i, [128, 130], BF16) for i in range(4)]; pvpb = [Buf() for _ in range(4)]
        scT = [sb(es, nc, "p3_sc%d" % i, [128, 128], BF16) for i in range(8)]; scTb = [Buf() for _ in range(8)]
        hbuf = [sb(es, nc, "p3_h%d" % i, [128, 256], F32) for i in range(2)]; hbufb = [Buf(), Buf()]
        xn = [sb(es, nc, "p3_xn%d" % i, [128, 256], F32) for i in range(2)]; xnb = [Buf(), Buf()]
        yml = [sb(es, nc, "p3_y%d" % i, [128, 256], BF16) for i in range(2)]; ymlb = [Buf(), Buf()]
        ysg = [sb(es, nc, "p3_ys%d" % i, [128, 2, 128], BF16) for i in range(2)]; ysgb = [Buf(), Buf()]
        sm = [sb(es, nc, "p3_sm%d" % i, [128, 16], F32) for i in range(2)]; smb = [Buf(), Buf()]
        st6 = [sb(es, nc, "p3_st%d" % i, [128, 4, 6], F32) for i in range(2)]; st6b = [Buf(), Buf()]
        mvs = [sb(es, nc, "p3_mv%d" % i, [128, 4, 2], F32) for i in range(2)]; mvsb = [Buf(), Buf()]
        rs = [sb(es, nc, "p3_rs%d" % i, [128, 8], F32) for i in range(2)]; rsb = [Buf(), Buf()]
        smh = [[Buf() for _ in range(4)] for _ in range(2)]
        hbh = [[Buf() for _ in range(4)] for _ in range(2)]
        sth = [[Buf() for _ in range(4)] for _ in range(2)]
        mvh = [[Buf() for _ in range(4)] for _ in range(2)]
        xnh = [[Buf() for _ in range(4)] for _ in range(2)]
        tpq = [Buf(), Buf()]

        def emit_A(ob):
            blk = NB // 2 + ob
            bi = ob % 2
            k.op("act", lambda e: e.copy(Cbf[bi][:, :, :], Call[:, ob, :, :]), reads=[Callb], writes=[Cbfb[bi]])
            for jc in range(2):
                pi = bi * 2 + jc
                for hp in range(2):
                    h = 2 * jc + hp
                    k.op("pool" if hp == 0 else "dve",
                         lambda e: e.tensor_scalar(pvp[pi][:, hp * 65:(hp + 1) * 65], vaug[:, blk, h, :],
                                                   apr[:, blk * 4 + h:blk * 4 + h + 1], None, ALU.mult),
                         reads=[vb, aprb], writes=[pvpb[pi]])
            for h in range(4):
                jc, hp = h // 2, h % 2
                ps_ = slice(hp * 64, (hp + 1) * 64)
                k.op("pe", lambda e: e.matmul(A[bi][:, h * 128:(h + 1) * 128], kT[ps_, jc, blk * 128:(blk + 1) * 128],
                                              qT[ps_, jc, ob * 128:(ob + 1) * 128], start=True, stop=True),
                     reads=[kTb, qTb], writes=[Apb[bi]], sig=True)
                k.op("dve", lambda e: e.tensor_tensor(out=scT[bi * 4 + h][:, :], in0=A[bi][:, h * 128:(h + 1) * 128], in1=utri[:, :],
                                                      op=ALU.mult),
                     reads=[Apb[bi], ub], writes=[scTb[bi * 4 + h]])

        def emit_B(ob):
            bi = ob % 2
            for h in range(4):
                jc, hp = h // 2, h % 2
                ps_ = slice(hp * 64, (hp + 1) * 64)
                pi = bi * 2 + jc
                k.op("pe", lambda e: e.matmul(Np[bi][:, h * 65:(h + 1) * 65], scT[bi * 4 + h][:, :], pvp[pi][:, hp * 65:(hp + 1) * 65],
                                              start=True, stop=False),
                     reads=[scTb[bi * 4 + h], pvpb[pi]], writes=[Npb[bi]], sig=False)
                k.op("pe", lambda e: e.matmul(Np[bi][:, h * 65:(h + 1) * 65], qT[ps_, jc, ob * 128:(ob + 1) * 128],
                                              Cbf[bi][ps_, jc, :], start=False, stop=True),
                     reads=[qTb, Cbfb[bi]], writes=[Npb[bi]], sig=True)

        def emit_C(ob):
            blk = NB // 2 + ob
            bi = ob % 2
            i = ob % 2
            for step in range(6):
                for h in range(4):
                    c = blk * 4 + h
                    nh = Np[bi][:, h * 65:(h + 1) * 65]
                    sb_ = smh[i][h]
                    if step == 0:
                        k.op("dve", lambda e: e.tensor_scalar(sm[i][:, h:h + 1], nh[:, 64:65], eb[:, c:c + 1], None, ALU.mult),
                             reads=[Npb[bi], ebb], writes=[sb_])
                    elif step == 1:
                        k.op("dve", lambda e: e.tensor_scalar(sm[i][:, 4 + h:5 + h], sm[i][:, h:h + 1], 1.0, None, ALU.max),
                             reads=[sb_], writes=[sb_])
                    elif step == 2:
                        k.op("dve", lambda e: e.scalar_tensor_tensor(out=sm[i][:, h:h + 1], in0=sm[i][:, h:h + 1], scalar=-1.0,
                                                                     in1=sm[i][:, 4 + h:5 + h], op0=ALU.mult, op1=ALU.max),
                             reads=[sb_], writes=[sb_])
                    elif step == 3:
                        k.op("dve", lambda e: e.reciprocal(sm[i][:, 4 + h:5 + h], sm[i][:, h:h + 1]), reads=[sb_], writes=[sb_])
                    elif step == 4:
                        k.op("dve", lambda e: e.tensor_tensor(out=sm[i][:, 8 + h:9 + h], in0=sm[i][:, 4 + h:5 + h], in1=eb[:, c:c + 1], op=ALU.mult),
                             reads=[sb_, ebb], writes=[sb_])
                    else:
                        k.op("dve", lambda e: e.scalar_tensor_tensor(out=hbuf[i][:, h * 64:(h + 1) * 64], in0=nh[:, 0:64], scalar=sm[i][:, 8 + h:9 + h],
                                                                     in1=mo[:, ob, h * 64:(h + 1) * 64], op0=ALU.mult, op1=ALU.mult),
                             reads=[Npb[bi], sb_, mob], writes=[hbh[i][h]])
            for h in range(4):
                k.op("dve", lambda e: e.bn_stats(st6[i][:, h, :], hbuf[i][:, h * 64:(h + 1) * 64]), reads=[hbh[i][h]], writes=[sth[i][h]])
            for h in range(4):
                k.op("dve", lambda e: e.bn_aggr(mvs[i][:, h, :], st6[i][:, h, :]), reads=[sth[i][h]], writes=[mvh[i][h]])
            k.op("dve", lambda e: e.tensor_scalar(rs[i][:, 0:4], mvs[i][:, :, 1], EPS, None, ALU.add), reads=mvh[i], writes=[rsb[i]])
            rstd_from(k, C, rs[i][:, 0:4], rs[i][:, 0:4], [rsb[i]], [rsb[i]], rs[i][:, 4:8], rsb[i])
            for h in range(4):
                k.op("dve" if h % 2 == 0 else "pool",
                     lambda e: e.tensor_scalar(xn[i][:, h * 64:(h + 1) * 64], hbuf[i][:, h * 64:(h + 1) * 64], mvs[i][:, h, 0:1],
                                               rs[i][:, h:h + 1], ALU.subtract, ALU.mult),
                     reads=[hbh[i][h], mvh[i][h], rsb[i]], writes=[xnh[i][h]])
            k.op("pool", lambda e: e.tensor_tensor(out=yml[i][:, :], in0=xn[i][:, :], in1=mlg[:, :], op=ALU.mult),
                 reads=xnh[i] + [mlgb], writes=[ymlb[i]])
            for jc in range(2):
                k.op("pe", lambda e: e.transpose(tpx[i][:, 512 + jc * 128:512 + (jc + 1) * 128],
                                                 yml[i][:, jc * 128:(jc + 1) * 128], C.identbf[:, :]),
                     reads=[ymlb[i], C.identbfb], writes=[tpq[i]], sig=(jc == 1))
            k.op("act", lambda e: e.copy(ysg[i][:, :, :], tpx[i][:, 512:768].rearrange("p (a b) -> p a b", a=2)),
                 reads=[tpq[i]], writes=[ysgb[i]])
            k.dma(C.ymixT[:, 2:4, ob * 128:(ob + 1) * 128], ysg[i][:, :, :], reads=[ysgb[i]], writes=[C.ymix_buf], q="pool")

        emit_A(0)
        for ob in range(NO):
            if ob + 1 < NO:
                emit_A(ob + 1)
            emit_B(ob)
            emit_C(ob)
    k.barrier()


def layer_norm_rows(k, C, z, zb, st, stb, mv, mvb, tmp, tmpb, grow, brow, rowb, outt, outb, eng2="pool"):
    for hf in range(2):
        k.op("dve", lambda e: e.bn_stats(st[:, hf, :], z[:, hf * 512:(hf + 1) * 512]), reads=[zb], writes=[stb])
    k.op("dve", lambda e: e.bn_aggr(mv[:, 0:2], st[:, :, :].rearrange("p a b -> p (a b)")), reads=[stb], writes=[mvb])
    k.op("dve", lambda e: e.tensor_scalar(mv[:, 2:3], mv[:, 1:2], EPS, None, ALU.add), reads=[mvb], writes=[mvb])
    rstd_from(k, C, mv[:, 3:4], mv[:, 2:3], [mvb], [mvb], tmp[:, 0:1], tmpb)
    k.op("dve", lambda e: e.tensor_scalar(z[:, :], z[:, :], mv[:, 0:1], mv[:, 3:4], ALU.subtract, ALU.mult),
         reads=[zb, mvb], writes=[zb])
    k.op(eng2, lambda e: e.tensor_tensor(out=outt[:, :], in0=z[:, :], in1=grow[:, :], op=ALU.mult),
         reads=[zb, rowb], writes=[outb])
    k.op(eng2, lambda e: e.tensor_tensor(out=outt[:, :], in0=outt[:, :], in1=brow[:, :], op=ALU.add),
         reads=[outb, rowb], writes=[outb])


def phase5(k, nc, C, L, xres_src):
    with ExitStack() as es:
        wout = sb(es, nc, "p5_w", [128, 8, D], BF16); woutb = Buf()
        load_cast(k, es, nc, "p5w", wout, woutb, L.w_out.rearrange("(kc p) n -> p kc n", p=128), D, 256)
        grow = sb(es, nc, "p5_g", [128, D], F32); brow = sb(es, nc, "p5_b", [128, D], F32); rowb = Buf()
        k.dma(grow[:, :], L.ln1_g.partition_broadcast(128), writes=[rowb])
        k.dma(brow[:, :], L.ln1_b.partition_broadcast(128), writes=[rowb])
        ymt = [sb(es, nc, "p5_y%d" % i, [128, 8, 128], BF16) for i in range(4)]; ymtb = [Buf() for _ in range(4)]
        xr = [sb(es, nc, "p5_x%d" % i, [128, D], F32) for i in range(4)]; xrb = [Buf() for _ in range(4)]
        z = [sb(es, nc, "p5_z%d" % i, [128, D], F32) for i in range(4)]; zb = [Buf() for _ in range(4)]
        xt = [sb(es, nc, "p5_t%d" % i, [128, 8, 128], BF16) for i in range(4)]; xtb = [Buf() for _ in range(4)]
        st = [sb(es, nc, "p5_st%d" % i, [128, 2, 6], F32) for i in range(4)]; stb = [Buf() for _ in range(4)]
        mvt = [sb(es, nc, "p5_mv%d" % i, [128, 2, 6], F32) for i in range(2)]; mvb = [Buf(), Buf()]
        ps = [psb(es, nc, "p5_ps%d" % i, [128, 512], F32) for i in range(8)]
        pb = [Buf() for _ in range(8)]
        NGR = TO // 256

        def emit_MM(g):
            for j in range(2):
                bi = (g % 2) * 2 + j
                r0 = (g * 2 + j) * 128
                k.dma(ymt[bi][:, :, :], C.ymixT[:, :, r0:r0 + 128], reads=[C.ymix_buf], writes=[ymtb[bi]])
                k.dma(xr[bi][:, :], xres_src[r0:r0 + 128, :], writes=[xrb[bi]])
                for hf in range(2):
                    pi = j * 2 + hf
                    for kc in range(8):
                        k.op("pe", lambda e: e.matmul(ps[pi][:, :], ymt[bi][:, kc, :], wout[:, kc, hf * 512:(hf + 1) * 512],
                                                      start=(kc == 0), stop=(kc == 7)),
                             reads=[ymtb[bi], woutb], writes=[pb[pi]], sig=(kc == 7))

        def emit_stt(g):
            gp = g % 2
            for j in range(2):
                bi = gp * 2 + j
                for hf in range(2):
                    pi = j * 2 + hf
                    k.op("dve", lambda e: e.scalar_tensor_tensor(out=z[bi][:, hf * 512:(hf + 1) * 512], in0=xr[bi][:, hf * 512:(hf + 1) * 512],
                                                                 scalar=ALPHA, in1=ps[pi][:, :], op0=ALU.mult, op1=ALU.add),
                         reads=[xrb[bi], pb[pi]], writes=[zb[bi]])
                    k.op("dve", lambda e: e.bn_stats(st[bi][:, hf, :], z[bi][:, hf * 512:(hf + 1) * 512]), reads=[zb[bi]], writes=[stb[bi]])
                k.op("dve", lambda e: e.bn_aggr(mvt[gp][:, j, 0:2], st[bi][:, :, :].rearrange("p a b -> p (a b)")),
                     reads=[stb[bi]], writes=[mvb[gp]])

        def emit_LN(g):
            gp = g % 2
            k.op("dve", lambda e: e.tensor_scalar(mvt[gp][:, :, 2], mvt[gp][:, :, 1], EPS, None, ALU.add), reads=[mvb[gp]], writes=[mvb[gp]])
            rstd_from(k, C, mvt[gp][:, :, 4], mvt[gp][:, :, 2], [mvb[gp]], [mvb[gp]], mvt[gp][:, :, 3], mvb[gp])
            for j in range(2):
                bi = gp * 2 + j
                r0 = (g * 2 + j) * 128
                k.op("dve", lambda e: e.tensor_scalar(z[bi][:, :], z[bi][:, :], mvt[gp][:, j, 0:1], mvt[gp][:, j, 4:5], ALU.subtract, ALU.mult),
                     reads=[zb[bi], mvb[gp]], writes=[zb[bi]])
                k.op("dve", lambda e: e.tensor_tensor(out=z[bi][:, :], in0=z[bi][:, :], in1=grow[:, :], op=ALU.mult),
                     reads=[zb[bi], rowb], writes=[zb[bi]])
                k.op("pool", lambda e: e.tensor_tensor(out=z[bi][:, :], in0=z[bi][:, :], in1=brow[:, :], op=ALU.add),
                     reads=[zb[bi], rowb], writes=[zb[bi]])
                k.dma(C.x1[r0:r0 + 128, :], z[bi][:, :], reads=[zb[bi]], writes=[C.x1_buf], q="pool")

        def emit_T(g):
            gp = g % 2
            for j in range(2):
                bi = gp * 2 + j
                r0 = (g * 2 + j) * 128
                for hf in range(2):
                    pi = 4 + j * 2 + hf
                    for jj in range(4):
                        kc = hf * 4 + jj
                        k.op("pe", lambda e: e.transpose(ps[pi][:, jj * 128:(jj + 1) * 128], z[bi][:, kc * 128:(kc + 1) * 128], C.ident[:, :]),
                             reads=[zb[bi], C.identb], writes=[pb[pi]], sig=(jj == 3))
                    k.op("act", lambda e: e.copy(xt[bi][:, hf * 4:(hf + 1) * 4, :], ps[pi][:, :].rearrange("p (a b) -> p a b", a=4)),
                         reads=[pb[pi]], writes=[xtb[bi]])
                k.dma(C.x1T[:, :, r0:r0 + 128], xt[bi][:, :, :], reads=[xtb[bi]], writes=[C.x1T_buf], q="pool")

        emit_MM(0)
        emit_stt(0)
        for g in range(NGR):
            if g + 1 < NGR:
                emit_MM(g + 1)
            emit_LN(g)
            emit_T(g)
            if g + 1 < NGR:
                emit_stt(g + 1)
    k.barrier()


def phase6(k, nc, C, L, xdst, xdst_buf, PU):
    with ExitStack() as es:
        PU.finish()
        wup = PU.dst
        wupb = PU.bufs
        wdn = sb(es, nc, "p6_wd", [128, 32, D], BF16)
        wdnb = [[Buf(), Buf()] for _ in range(8)]
        std = [sb(es, nc, "p6_sd%d" % i, [128, 16, 128], F32) for i in range(2)]; stdb = [Buf(), Buf()]
        wd3 = L.w_down.rearrange("(kc p) n -> p kc n", p=128)
        n = 0
        for cb in range(8):
            for fh in range(2):
                i = n % 2
                n += 1
                k.dma(std[i][:, :, :], wd3[:, fh * 16:(fh + 1) * 16, cb * 128:(cb + 1) * 128], writes=[stdb[i]])
                k.op("dve" if n % 2 == 0 else "pool",
                     lambda e: e.tensor_copy(wdn[:, fh * 16:(fh + 1) * 16, cb * 128:(cb + 1) * 128], std[i][:, :, :]),
                     reads=[stdb[i]], writes=[wdnb[cb][fh]])
        grow = sb(es, nc, "p6_g", [128, D], F32); brow = sb(es, nc, "p6_b", [128, D], F32); rowb = Buf()
        k.dma(grow[:, :], L.ln2_g.partition_broadcast(128), writes=[rowb])
        k.dma(brow[:, :], L.ln2_b.partition_broadcast(128), writes=[rowb])
        xt = [sb(es, nc, "p6_t%d" % i, [128, 8, 256], BF16) for i in range(2)]; xtb = [Buf(), Buf()]
        hT = sb(es, nc, "p6_h", [128, 32, 256], BF16); hTb = [Buf() for _ in range(32)]
        rl = [sb(es, nc, "p6_r%d" % i, [128, 256], BF16) for i in range(4)]; rlb = [Buf() for _ in range(4)]
        xr = [sb(es, nc, "p6_x%d" % i, [128, D], F32) for i in range(2)]; xrb = [Buf(), Buf()]
        z = [sb(es, nc, "p6_z%d" % i, [128, D], F32) for i in range(2)]; zb = [Buf(), Buf()]
        st = sb(es, nc, "p6_st", [128, 2, 6], F32); stb = Buf()
        mv = sb(es, nc, "p6_mv", [128, 4], F32); mvb = Buf()
        tmp = sb(es, nc, "p6_tmp", [128, 1], F32); tmpb = Buf()
        ps = [psb(es, nc, "p6_ps%d" % i, [128, 512], F32) for i in range(8)]
        pb = [Buf() for _ in range(8)]
        nb = 0
        for ti in range(TO // 256):
            t0 = ti * 256
            x_t = xt[ti % 2]
            k.dma(x_t[:, :, :], C.x1T[:, :, t0:t0 + 256], reads=[C.x1T_buf], writes=[xtb[ti % 2]])
            for fc in range(32):
                pi = fc % 4
                for kc in range(8):
                    k.op("pe", lambda e: e.matmul(ps[pi][:, 0:256], wup[:, kc, fc * 128:(fc + 1) * 128], x_t[:, kc, :],
                                                  start=(kc == 0), stop=(kc == 7)),
                         reads=[wupb[fc], xtb[ti % 2]], writes=[pb[pi]], sig=(kc == 7))
                ri = fc % 4
                k.op("act", lambda e: e.activation(out=rl[ri][:, :], in_=ps[pi][:, 0:256], func=AF.Relu),
                     reads=[pb[pi]], writes=[rlb[ri]])
                k.op("dve" if fc % 2 == 0 else "pool",
                     lambda e: e.tensor_tensor(out=hT[:, fc, :], in0=rl[ri][:, :], in1=rl[ri][:, :], op=ALU.mult),
                     reads=[rlb[ri]], writes=[hTb[fc]])
            for blk in range(2):
                i = nb % 2
                nb += 1
                r0 = t0 + blk * 128
                k.dma(xr[i][:, :], C.x1[r0:r0 + 128, :], reads=[C.x1_buf], writes=[xrb[i]])
                for hf in range(2):
                    pi = 4 + i * 2 + hf
                    for fc in range(32):
                        wr = [wdnb[cb][fc // 16] for cb in range(hf * 4, hf * 4 + 4)]
                        k.op("pe", lambda e: e.matmul(ps[pi][:, :], hT[:, fc, blk * 128:(blk + 1) * 128],
                                                      wdn[:, fc, hf * 512:(hf + 1) * 512], start=(fc == 0), stop=(fc == 31)),
                             reads=[hTb[fc]] + wr, writes=[pb[pi]], sig=(fc == 31))
                    k.op("dve", lambda e: e.scalar_tensor_tensor(out=z[i][:, hf * 512:(hf + 1) * 512],
                                                                 in0=xr[i][:, hf * 512:(hf + 1) * 512], scalar=ALPHA,
                                                                 in1=ps[pi][:, :], op0=ALU.mult, op1=ALU.add),
                         reads=[xrb[i], pb[pi]], writes=[zb[i]])
                layer_norm_rows(k, C, z[i], zb[i], st, stb, mv, mvb, tmp, tmpb, grow, brow, rowb, z[i], zb[i])
                k.dma(xdst[r0:r0 + 128, :], z[i][:, :], reads=[zb[i]], writes=[xdst_buf], q="pool")
    k.barrier()


LAYER_W = [("w_in", [D, DIN]), ("b_igate", [4]), ("b_fgate", [4]), ("conv_pw_w", [256, 256]),
           ("colpack", [128, NCP]), ("ml_norm_g", [256]), ("lam_q1", [64]), ("lam_k1", [64]),
           ("lam_q2", [64]), ("lam_k2", [64]), ("da_norm_g", [128]), ("w_out", [D, D]), ("ln1_g", [D]), ("ln1_b", [D]),
           ("w_up", [D, DFF]), ("w_down", [DFF, D]), ("ln2_g", [D]), ("ln2_b", [D])]


def build(layers, phases=None, debug=False, inject=()):
    nc = bass.Bass("TRN2", target_bir_lowering=False)
    C = Ctx()
    skind = "ExternalOutput" if debug else "Internal"
    x_own = nc.dram_tensor("x_own", [TO, D], F32, kind="ExternalInput").ap()
    x_prev = nc.dram_tensor("x_prev", [TO, D], F32, kind="ExternalInput").ap()
    flag_d = nc.dram_tensor("flag", [128, 2], F32, kind="ExternalInput").ap()
    ident_d = nc.dram_tensor("ident", [128, 128], F32, kind="ExternalInput").ap()
    cmask_d = nc.dram_tensor("cmask", [128, 4, 512], F32, kind="ExternalInput").ap()
    C.utri_d = nc.dram_tensor("utri", [128, 128], F32, kind="ExternalInput").ap()
    Ls = []
    for l in layers:
        L = Ctx()
        L.idx = l
        for (nm, shp) in LAYER_W:
            setattr(L, nm, nc.dram_tensor("%s_%d" % (nm, l), shp, F32, kind="ExternalInput").ap())
        Ls.append(L)
    out = nc.dram_tensor("out", [TO, D], F32, kind="ExternalOutput").ap()

    def scratch(name, shape, dt):
        return nc.dram_tensor(name, shape, dt, kind=("ExternalInput" if name in inject else skind)).ap()
    C.xT_all = scratch("xT_all", [128, 8, TA], BF16); C.xT_buf = Buf()
    C.glu = scratch("glu", [128, 2, 512 + TO], BF16); C.glu_buf = Buf()
    C.mqk = scratch("mqk", [128, 4, TA], BF16); C.mqk_buf = Buf()
    C.mv = scratch("mv", [TA, 256], BF16); C.mv_buf = Buf()
    C.mo = scratch("mo", [TO, 256], BF16); C.mo_buf = Buf()
    C.gates = scratch("gates", [TA, 8], F32); C.gates_buf = Buf()
    C.dq = scratch("dq", [128, 4, TO], BF16); C.dq_buf = Buf()
    C.dk = scratch("dk", [128, 4, TA], BF16); C.dk_buf = Buf()
    C.dv = scratch("dv", [TA, 512], BF16); C.dv_buf = Buf()
    C.ymixT = scratch("ymixT", [128, 8, TO], BF16); C.ymix_buf = Buf()
    C.x1 = scratch("x1", [TO, D], F32); C.x1_buf = Buf()
    C.x1T = scratch("x1T", [128, 8, TO], BF16); C.x1T_buf = Buf()
    NCH, CH = 8, 512
    bnc = [nc.dram_tensor("xbnc%d" % j, [CH, D], F32) for j in range(NCH)]
    gth = [nc.dram_tensor("xgth%d" % j, [2 * CH, D], F32) for j in range(NCH)]
    bncb = Buf()
    C.xmid = RowChunks(bnc, CH); C.xmid_buf = Buf()
    C.out = out; C.out_buf = Buf()
    with ExitStack() as es:
        k = KB(nc, es)
        cc_sem = es.enter_context(nc.semaphore("cc_sem"))
        C.ident = sb(es, nc, "ident_sb", [128, 128], F32); C.identb = Buf()
        C.identbf = sb(es, nc, "identbf", [128, 128], BF16); C.identbfb = Buf()
        C.flag = sb(es, nc, "flagsb", [128, 2], F32); C.flagb = Buf()
        k.dma(C.ident[:, :], ident_d[:, :], writes=[C.identb])
        k.dma(C.flag[:, :], flag_d[:, :], writes=[C.flagb])
        k.op("dve", lambda e: e.tensor_copy(C.identbf[:, :], C.ident[:, :]), reads=[C.identb], writes=[C.identbfb])
        C.cmask_d = cmask_d
        ncc = 0
        for li, L in enumerate(Ls):
            if li == 0:
                xo = x_own
                src_prev = [(x_prev, TO, 0)]
            else:
                k.barrier()
                for j in range(NCH):
                    nc.gpsimd.collective_compute("AllGather", ALU.bypass, replica_groups=[[0, 1], [2, 3], [4, 5], [6, 7]],
                                                 ins=[bnc[j].ap().opt()], outs=[gth[j].ap().opt()]).then_inc(cc_sem)
                    ncc += 1
                for e in ("pe", "act", "dve", "pool", "sp"):
                    k.E[e].wait_ge(cc_sem, ncc)
                xo = C.xmid
                src_prev = [(gth[j].ap()[0:CH, :], CH, j * CH) for j in range(NCH)]
            last = li == len(Ls) - 1
            xdst = out if last else C.xmid
            xdst_buf = C.out_buf if last else C.xmid_buf
            ph = phases if phases is not None else [0, 1, 2, 3, 4, 5, 6]
            with ExitStack() as esw:
                PW = Prefetch(k, esw, nc, "pw_in", L.w_in.rearrange("(kc p) n -> p kc n", p=128), 8, DIN, 280, ("dve", "pool"))
                if 0 in ph:
                    phase0(k, nc, C, src_prev + [(xo, TO, TO)], hook=lambda it: PW.step(1) if it % 5 == 0 else None)
                if 1 in ph:
                    phase1(k, nc, C, L, PW)
                k.barrier()
            if 2 in ph:
                phase2(k, nc, C, L)
            if 3 in ph:
                phase3(k, nc, C, L)
            with ExitStack() as esw:
                PU = Prefetch(k, esw, nc, "pw_up", L.w_up.rearrange("(kc p) n -> p kc n", p=128), 8, DFF, 128, ("pool",))
                if 4 in ph:
                    phase4(k, nc, C, L, hook=lambda: PU.step(1))
                if 5 in ph:
                    phase5(k, nc, C, L, xo)
                if 6 in ph:
                    phase6(k, nc, C, L, xdst, xdst_buf, PU)
                k.barrier()
        k.barrier()
        C.ninst = k.ninst
    return nc, C


def make_cmask():
    kk = np.arange(128)[:, None, None]
    jj = np.arange(4)[None, :, None]
    qq = np.arange(512)[None, None, :]
    return np.where(qq >= jj * 128 + kk, 0.0, NEG).astype(np.float32)


def make_colpack(inp, l):
    cp = np.zeros((128, NCP), np.float32)
    def cols(v, n):
        return np.asarray(v, np.float32).reshape(n, 128).T
    cp[:, CP_DWB:CP_DWB + 2] = cols(inp["conv_dw_b"][l], 2)
    cp[:, CP_LNG:CP_LNG + 2] = cols(inp["conv_ln_g"][l], 2)
    cp[:, CP_LNB:CP_LNB + 2] = cols(inp["conv_ln_b"][l], 2)
    cp[:, CP_PWB:CP_PWB + 2] = cols(inp["conv_pw_b"][l], 2)
    cp[:, CP_MLB:CP_MLB + 4] = cols(inp["ml_conv_b"][l], 4)
    w = np.asarray(inp["conv_dw_w"][l], np.float32)
    for j in range(2):
        cp[:, CP_DWW + j * 31:CP_DWW + (j + 1) * 31] = w[:, j * 128:(j + 1) * 128].T
    w = np.asarray(inp["ml_conv_w"][l], np.float32)
    for j in range(4):
        cp[:, CP_MLW + j * 4:CP_MLW + (j + 1) * 4] = w[:, j * 128:(j + 1) * 128].T
    return cp


def core_inputs(inp, x_full, c, layers):
    b, p = c // 2, c % 2
    m = {"x_own": np.ascontiguousarray(x_full[b, p * TO:(p + 1) * TO]),
         "x_prev": np.ascontiguousarray(x_full[b, 0:TO]),
         "flag": np.tile(np.array([[float(p), 0.0 if p == 1 else NEG]], np.float32), (128, 1)),
         "ident": np.eye(128, dtype=np.float32), "cmask": make_cmask(),
         "utri": np.triu(np.ones((128, 128), np.float32))}
    for l in layers:
        for (nm, shp) in LAYER_W:
            if nm == "colpack":
                m["colpack_%d" % l] = make_colpack(inp, l)
            else:
                m["%s_%d" % (nm, l)] = np.ascontiguousarray(inp[nm][l])
    return m


def _run(inp, x_full, layers):
    nc, _ = build(layers)
    in_maps = [core_inputs(inp, x_full, c, layers) for c in range(8)]
    res = run_bass_kernel_spmd(nc, in_maps, core_ids=list(range(8)))
    out = np.empty((4, S, D), np.float32)
    for c in range(8):
        b, p = c // 2, c % 2
        out[b, p * TO:(p + 1) * TO] = np.asarray(res.results[c]["out"], dtype=np.float32)
    return out


def kernel(**inputs):
    inp = {k_: np.asarray(v) for k_, v in inputs.items()}
    x = np.asarray(inp["x"], np.float32)
    return _run(inp, x, list(range(DEPTH)))
```

```python
import math
from contextlib import ExitStack
import numpy as np
import concourse.bass as bass
import concourse.mybir as mybir
from concourse.bass_utils import run_bass_kernel_spmd

F32 = mybir.dt.float32
BF16 = mybir.dt.bfloat16
AF = mybir.ActivationFunctionType
ALU = mybir.AluOpType

D = 1024
S = 8192
TO = 4096
TA = 8192
DIN = 3080
DFF = 4096
DEPTH = 2
ALPHA = (2 * DEPTH) ** 0.25
EPS = 1e-5
NEG = -30000.0
NDS = 48
FORCE_Q = None


class Buf:
    __slots__ = ("w", "r")

    def __init__(self):
        self.w = None
        self.r = {}


class KB:
    def __init__(self, nc, es):
        self.nc = nc
        self.E = {"pe": nc.tensor, "act": nc.scalar, "dve": nc.vector, "pool": nc.gpsimd, "sp": nc.sync}
        self.sem = {}
        self.cnt = {}
        for e in ("pe", "act", "dve", "pool"):
            self.sem[e] = es.enter_context(nc.semaphore("s_" + e))
            self.cnt[e] = 0
        self.dsem = [es.enter_context(nc.semaphore("d%d" % i)) for i in range(NDS)]
        self.dcnt = [0] * NDS
        self.dnext = 0
        self.seen = {}
        self.ninst = 0

    def _toks(self, reads, writes):
        toks = []
        for b in reads:
            if b.w is not None:
                toks.append(b.w)
        for b in writes:
            if b.w is not None:
                toks.append(b.w)
            toks.extend(b.r.items())
        return toks

    def _wait(self, e, toks):
        need = {}
        for k, v in toks:
            if k == "pe" and e == "pe":
                continue
            if self.seen.get((e, k), 0) >= v:
                continue
            if need.get(k, 0) < v:
                need[k] = v
        for k, v in need.items():
            sem = self.sem[k] if isinstance(k, str) else self.dsem[k]
            self.E[e].wait_ge(sem, v)
            self.seen[(e, k)] = v
            self.ninst += 1

    def _commit(self, tok, reads, writes):
        k, v = tok
        for b in reads:
            if b.r.get(k, 0) < v:
                b.r[k] = v
        for b in writes:
            b.w = tok
            b.r = {}

    def op(self, e, fn, reads=(), writes=(), sig=True):
        self._wait(e, self._toks(reads, writes))
        inst = fn(self.E[e])
        self.ninst += 1
        if sig:
            self.cnt[e] += 1
            inst.then_inc(self.sem[e], 1)
            tok = (e, self.cnt[e])
        else:
            tok = (e, self.cnt[e] + 1)
        self._commit(tok, reads, writes)
        return tok

    def dma(self, out, in_, reads=(), writes=(), q="sp"):
        if FORCE_Q is not None:
            q = FORCE_Q
        self._wait(q, self._toks(reads, writes))
        j = self.dnext
        self.dnext = (j + 1) % NDS
        self.dcnt[j] += 16
        self.E[q].dma_start(out=out, in_=in_).then_inc(self.dsem[j], 16)
        self.ninst += 1
        tok = (j, self.dcnt[j])
        self._commit(tok, reads, writes)
        return tok

    def barrier(self):
        toks = [(e, self.cnt[e]) for e in ("pe", "act", "dve", "pool") if self.cnt[e] > 0]
        toks += [(j, self.dcnt[j]) for j in range(NDS) if self.dcnt[j] > 0]
        for e in ("pe", "act", "dve", "pool", "sp"):
            need = []
            for k, v in toks:
                if self.seen.get((e, k), 0) < v:
                    need.append((k, v))
            for k, v in need:
                sem = self.sem[k] if isinstance(k, str) else self.dsem[k]
                self.E[e].wait_ge(sem, v)
                self.seen[(e, k)] = v


class Ctx:
    pass


class RowChunks:
    def __init__(self, handles, ch):
        self.h = handles
        self.ch = ch

    def __getitem__(self, idx):
        rs, cs = idx
        j = rs.start // self.ch
        a = rs.start - j * self.ch
        b = rs.stop - j * self.ch
        assert 0 <= a < b <= self.ch
        return self.h[j].ap()[a:b, cs]


_UID = [0]


def sb(es, nc, name, shape, dt):
    _UID[0] += 1
    return es.enter_context(nc.sbuf_tensor("%s_u%d" % (name, _UID[0]), shape, dt))


def psb(es, nc, name, shape, dt):
    _UID[0] += 1
    return es.enter_context(nc.psum_tensor("%s_u%d" % (name, _UID[0]), shape, dt))


def load_cast(k, es, nc, name, dst, dstbuf, src3, ncols, piece, engines=("dve", "pool")):
    KC = dst.shape[1]
    st = [sb(es, nc, "%s_st%d" % (name, i), [128, KC, piece], F32) for i in range(2)]
    stb = [Buf(), Buf()]
    i = 0
    c0 = 0
    while c0 < ncols:
        w = min(piece, ncols - c0)
        s = st[i % 2]
        k.dma(s[:, :, 0:w], src3[:, :, c0:c0 + w], writes=[stb[i % 2]])
        e = engines[i % len(engines)]
        k.op(e, lambda eng, s=s, c0=c0, w=w: eng.tensor_copy(dst[:, :, c0:c0 + w], s[:, :, 0:w]),
             reads=[stb[i % 2]], writes=[dstbuf])
        c0 += w
        i += 1


class Prefetch:
    def __init__(self, k, es, nc, name, src3, KC, ncols, piece, engines):
        self.k = k
        self.src3 = src3
        self.piece = piece
        self.ncols = ncols
        self.engines = engines
        self.dst = sb(es, nc, name, [128, KC, ncols], BF16)
        self.st = [sb(es, nc, "%s_st%d" % (name, i), [128, KC, piece], F32) for i in range(2)]
        self.stb = [Buf(), Buf()]
        self.npieces = (ncols + piece - 1) // piece
        self.bufs = [Buf() for _ in range(self.npieces)]
        self.i = 0

    def step(self, n=1):
        k = self.k
        for _ in range(n):
            if self.i >= self.npieces:
                return
            i = self.i
            self.i += 1
            c0 = i * self.piece
            w = min(self.piece, self.ncols - c0)
            s_ = self.st[i % 2]
            k.dma(s_[:, :, 0:w], self.src3[:, :, c0:c0 + w], writes=[self.stb[i % 2]])
            e = self.engines[i % len(self.engines)]
            k.op(e, lambda eng: eng.tensor_copy(self.dst[:, :, c0:c0 + w], s_[:, :, 0:w]),
                 reads=[self.stb[i % 2]], writes=[self.bufs[i]])

    def finish(self):
        self.step(self.npieces)


def phase0(k, nc, C, src_list, hook=None):
    with ExitStack() as es:
        NBF = 4
        xin = [sb(es, nc, "p0_x%d" % i, [128, D], F32) for i in range(NBF)]
        xinb = [Buf() for _ in range(NBF)]
        xo = [sb(es, nc, "p0_o%d" % i, [128, 8, 512], BF16) for i in range(2)]
        xob = [Buf(), Buf()]
        ps = [psb(es, nc, "p0_ps%d" % i, [128, 512], F32) for i in range(8)]
        psbuf = [Buf() for _ in range(8)]
        it = 0
        for (src, nrows, toff) in src_list:
            for blk in range(nrows // 128):
                xi = xin[it % NBF]
                g = (it // 4) % 2
                b4 = it % 4
                k.dma(xi[:, :], src[blk * 128:(blk + 1) * 128, :], writes=[xinb[it % NBF]])
                for hf in range(2):
                    pi = (it * 2 + hf) % 8
                    for j in range(4):
                        kc = hf * 4 + j
                        k.op("pe", lambda e: e.transpose(ps[pi][:, j * 128:(j + 1) * 128], xi[:, kc * 128:(kc + 1) * 128], C.ident[:, :]),
                             reads=[xinb[it % NBF], C.identb], writes=[psbuf[pi]], sig=(j == 3))
                    dst = xo[g][:, hf * 4:(hf + 1) * 4, b4 * 128:(b4 + 1) * 128]
                    srcp = ps[pi][:, :].rearrange("p (a b) -> p a b", a=4)
                    if hf == 0:
                        k.op("act", lambda e: e.copy(dst, srcp), reads=[psbuf[pi]], writes=[xob[g]])
                    else:
                        k.op("dve", lambda e: e.tensor_copy(dst, srcp), reads=[psbuf[pi]], writes=[xob[g]])
                if b4 == 3:
                    t0 = toff + (blk - 3) * 128
                    k.dma(C.xT_all[:, :, t0:t0 + 512], xo[g][:, :, :], reads=[xob[g]], writes=[C.xT_buf], q="pool")
                it += 1
                if hook is not None:
                    hook(it)
    k.barrier()


def phase1(k, nc, C, L, PW):
    with ExitStack() as es:
        PW.finish()
        win = PW.dst
        k._wait("pe", [b.w for b in PW.bufs if b.w is not None])
        winb = Buf()
        gb = sb(es, nc, "p1_gb", [128, 8], F32)
        gbb = Buf()
        k.dma(gb[:, 0:4], L.b_igate.partition_broadcast(128), writes=[gbb])
        k.dma(gb[:, 4:8], L.b_fgate.partition_broadcast(128), writes=[gbb])
        xt = [sb(es, nc, "p1_xt%d" % i, [128, 8, 512], BF16) for i in range(2)]
        xtb = [Buf(), Buf()]
        NST = 4
        stg = [sb(es, nc, "p1_stg%d" % i, [128, 512], BF16) for i in range(NST)]
        stgb = [Buf() for _ in range(NST)]
        sg = [sb(es, nc, "p1_sg%d" % i, [128, 512], BF16) for i in range(2)]
        sgb = [Buf(), Buf()]
        tst = [sb(es, nc, "p1_tst%d" % i, [128, 512], BF16) for i in range(NST)]
        tstb = [Buf() for _ in range(NST)]
        gst = [sb(es, nc, "p1_gst%d" % i, [128, 8], F32) for i in range(2)]
        gstb = [Buf(), Buf()]
        ps = [psb(es, nc, "p1_ps%d" % i, [128, 512], F32) for i in range(8)]
        psbuf = [Buf() for _ in range(8)]
        st = {"ps": 0, "stg": 0, "tst": 0, "ev": 0, "sg": 0, "gst": 0}

        def mm_feat(xtile, xbuf, col0):
            pi = st["ps"] % 8
            st["ps"] += 1
            for kc in range(8):
                k.op("pe", lambda e, kc=kc, pi=pi: e.matmul(ps[pi][:, :], win[:, kc, col0:col0 + 128],
                                                           xtile[:, kc, :], start=(kc == 0), stop=(kc == 7)),
                     reads=[winb, xbuf], writes=[psbuf[pi]], sig=(kc == 7))
            return pi

        def evac_copy(pi, dst_sb, dst_buf):
            eng = "act" if st["ev"] % 2 == 0 else "dve"
            st["ev"] += 1
            if eng == "act":
                k.op("act", lambda e: e.copy(dst_sb, ps[pi][:, 0:dst_sb.shape[1]]), reads=[psbuf[pi]], writes=[dst_buf])
            else:
                k.op("dve", lambda e: e.tensor_copy(dst_sb, ps[pi][:, 0:dst_sb.shape[1]]), reads=[psbuf[pi]],
                     writes=[dst_buf])

        for ti in range(16):
            own = ti >= 8
            lastprev = ti == 7
            t0 = ti * 512
            xtile = xt[ti % 2]
            xbuf = xtb[ti % 2]
            k.dma(xtile[:, :, :], C.xT_all[:, :, t0:t0 + 512], reads=[C.xT_buf], writes=[xbuf])
            if own or lastprev:
                gcol = t0 - 7 * 512
                for j in range(2):
                    pa = mm_feat(xtile, xbuf, j * 128)
                    pg = mm_feat(xtile, xbuf, 256 + j * 128)
                    si = st["sg"] % 2
                    st["sg"] += 1
                    k.op("act", lambda e, si=si, pg=pg: e.activation(out=sg[si][:, :], in_=ps[pg][:, :], func=AF.Sigmoid),
                         reads=[psbuf[pg]], writes=[sgb[si]])
                    oi = st["stg"] % NST
                    st["stg"] += 1
                    k.op("dve", lambda e, si=si, pa=pa, oi=oi: e.tensor_tensor(out=stg[oi][:, :], in0=ps[pa][:, :],
                                                                             in1=sg[si][:, :], op=ALU.mult),
                         reads=[psbuf[pa], sgb[si]], writes=[stgb[oi]])
                    k.dma(C.glu[:, j, gcol:gcol + 512], stg[oi][:, :], reads=[stgb[oi]], writes=[C.glu_buf], q="pool")
            feats = []
            for j in range(4):
                if own or lastprev or j >= 2:
                    feats.append((512 + j * 128, C.mqk[:, j, t0:t0 + 512], C.mqk_buf))
            if own:
                for j in range(4):
                    feats.append((1544 + j * 128, C.dq[:, j, t0 - TO:t0 - TO + 512], C.dq_buf))
            for j in range(4):
                feats.append((2056 + j * 128, C.dk[:, j, t0:t0 + 512], C.dk_buf))
            for (col0, dst, dbuf) in feats:
                pi = mm_feat(xtile, xbuf, col0)
                oi = st["stg"] % NST
                st["stg"] += 1
                evac_copy(pi, stg[oi][:, :], stgb[oi])
                k.dma(dst, stg[oi][:, :], reads=[stgb[oi]], writes=[dbuf], q="pool")
            for blk in range(4):
                tb = t0 + blk * 128
                pi = st["ps"] % 8
                st["ps"] += 1
                for kc in range(8):
                    k.op("pe", lambda e, kc=kc, pi=pi: e.matmul(ps[pi][:, :], xtile[:, kc, blk * 128:(blk + 1) * 128],
                                                               win[:, kc, 1024:1536], start=(kc == 0), stop=(kc == 7)),
                         reads=[winb, xbuf], writes=[psbuf[pi]], sig=(kc == 7))
                oi = st["tst"] % NST
                st["tst"] += 1
                k.op("dve", lambda e, pi=pi, oi=oi: e.tensor_copy(tst[oi][:, 0:256], ps[pi][:, 0:256]),
                     reads=[psbuf[pi]], writes=[tstb[oi]])
                if own:
                    k.op("act", lambda e, pi=pi, oi=oi: e.activation(out=tst[oi][:, 256:512], in_=ps[pi][:, 256:512],
                                                                    func=AF.Sigmoid),
                         reads=[psbuf[pi]], writes=[tstb[oi]])
                k.dma(C.mv[tb:tb + 128, :], tst[oi][:, 0:256], reads=[tstb[oi]], writes=[C.mv_buf], q="pool")
                if own:
                    k.dma(C.mo[tb - TO:tb - TO + 128, :], tst[oi][:, 256:512], reads=[tstb[oi]], writes=[C.mo_buf],
                          q="pool")
                pi = st["ps"] % 8
                st["ps"] += 1
                for kc in range(8):
                    k.op("pe", lambda e, kc=kc, pi=pi: e.matmul(ps[pi][:, 0:8], xtile[:, kc, blk * 128:(blk + 1) * 128],
                                                               win[:, kc, 1536:1544], start=(kc == 0), stop=(kc == 7)),
                         reads=[winb, xbuf], writes=[psbuf[pi]], sig=(kc == 7))
                gi = st["gst"] % 2
                st["gst"] += 1
                k.op("dve", lambda e, pi=pi, gi=gi: e.tensor_tensor(out=gst[gi][:, :], in0=ps[pi][:, 0:8], in1=gb[:, :],
                                                                   op=ALU.add),
                     reads=[psbuf[pi], gbb], writes=[gstb[gi]])
                k.dma(C.gates[tb:tb + 128, :], gst[gi][:, :], reads=[gstb[gi]], writes=[C.gates_buf], q="pool")
                pi = st["ps"] % 8
                st["ps"] += 1
                for kc in range(8):
                    k.op("pe", lambda e, kc=kc, pi=pi: e.matmul(ps[pi][:, :], xtile[:, kc, blk * 128:(blk + 1) * 128],
                                                               win[:, kc, 2568:3080], start=(kc == 0), stop=(kc == 7)),
                         reads=[winb, xbuf], writes=[psbuf[pi]], sig=(kc == 7))
                oi = st["tst"] % NST
                st["tst"] += 1
                evac_copy(pi, tst[oi][:, :], tstb[oi])
                k.dma(C.dv[tb:tb + 128, :], tst[oi][:, :], reads=[tstb[oi]], writes=[C.dv_buf], q="pool")
    k.barrier()


def rstd_from(k, C, out_ap, in_ap, bufs_r, bufs_w, tmp_ap, tmpbuf):
    k.op("act", lambda e: e.activation(out=tmp_ap, in_=in_ap, func=AF.Ln), reads=bufs_r, writes=[tmpbuf])
    k.op("act", lambda e: e.activation(out=out_ap, in_=tmp_ap, func=AF.Exp, scale=-0.5), reads=[tmpbuf], writes=bufs_w)


CP_DWB, CP_LNG, CP_LNB, CP_PWB, CP_MLB, CP_DWW, CP_MLW = 0, 2, 4, 6, 8, 12, 74
NCP = 90


def phase2(k, nc, C, L):
    with ExitStack() as es:
        glu = sb(es, nc, "p2_glu", [128, 2, 512 + TO], BF16)
        glub = Buf()
        for j in range(2):
            k.dma(glu[:, j, :], C.glu[:, j, :], reads=[C.glu_buf], writes=[glub])
        k.op("dve", lambda e: e.tensor_scalar(glu[:, :, 482:512], glu[:, :, 482:512], C.flag[:, 0:1], None, ALU.mult),
             reads=[glub, C.flagb], writes=[glub])
        cp = sb(es, nc, "p2_cp", [128, NCP], F32)
        cpb = Buf()
        k.dma(cp[:, :], L.colpack[:, :], writes=[cpb])
        dg = sb(es, nc, "p2_dg", [128, 62, 128], BF16)
        dgb = Buf()
        for i in range(62):
            k.op("dve" if i % 2 == 0 else "pool",
                 lambda e: e.tensor_scalar(dg[:, i, :], C.identbf[:, :], cp[:, CP_DWW + i:CP_DWW + i + 1], None, ALU.mult),
                 reads=[C.identbfb, cpb], writes=[dgb])
        pw = sb(es, nc, "p2_pw", [128, 2, 256], BF16)
        pwb = Buf()
        load_cast(k, es, nc, "p2pw", pw, pwb, L.conv_pw_w.rearrange("(kc p) n -> p kc n", p=128), 256, 256)
        ones = sb(es, nc, "p2_ones", [128, 128], F32)
        onesb = Buf()
        k.op("dve", lambda e: e.memset(ones[:, :], 1.0), writes=[onesb])
        yb = [[sb(es, nc, "p2_y%d_%d" % (p, j), [128, 512], F32) for j in range(2)] for p in range(2)]
        ybb = [[Buf(), Buf()] for _ in range(2)]
        sq = [[sb(es, nc, "p2_sq%d_%d" % (p, j), [128, 512], F32) for j in range(2)] for p in range(2)]
        sqb = [[Buf(), Buf()] for _ in range(2)]
        msb = sb(es, nc, "p2_ms", [128, 512], F32); msbb = Buf()
        tt = sb(es, nc, "p2_tt", [128, 512], F32); ttb = Buf()
        var = sb(es, nc, "p2_var", [128, 512], F32); varb = Buf()
        lnt = sb(es, nc, "p2_lnt", [128, 512], F32); lntb = Buf()
        rstd = sb(es, nc, "p2_rstd", [128, 512], F32); rstdb = Buf()
        dd = [sb(es, nc, "p2_d%d" % j, [128, 512], F32) for j in range(2)]
        ddb = [Buf(), Buf()]
        act = [sb(es, nc, "p2_a%d" % j, [128, 512], BF16) for j in range(2)]
        actb = [Buf(), Buf()]
        og = [sb(es, nc, "p2_o%d" % j, [128, 512], BF16) for j in range(2)]
        ogb = [Buf(), Buf()]
        ps = [psb(es, nc, "p2_ps%d" % i, [128, 512], F32) for i in range(6)]
        pb = [Buf() for _ in range(6)]

        def emit_conv(ti):
            p = ti % 2
            c0 = 512 + ti * 512
            for j in range(2):
                for t in range(31):
                    k.op("pe", lambda e: e.matmul(ps[j][:, :], dg[:, j * 31 + t, :], glu[:, j, c0 - 30 + t:c0 - 30 + t + 512],
                                                  start=(t == 0), stop=(t == 30)),
                         reads=[dgb, glub], writes=[pb[j]], sig=(t == 30))
                k.op("act", lambda e: e.activation(out=yb[p][j][:, :], in_=ps[j][:, :], func=AF.Identity,
                                                   bias=cp[:, CP_DWB + j:CP_DWB + j + 1]),
                     reads=[pb[j], cpb], writes=[ybb[p][j]])
                k.op("act", lambda e: e.activation(out=sq[p][j][:, :], in_=ps[j][:, :], func=AF.Square,
                                                   bias=cp[:, CP_DWB + j:CP_DWB + j + 1]),
                     reads=[pb[j], cpb], writes=[sqb[p][j]])

        def emit_stats_mm(ti):
            p = ti % 2
            for j in range(2):
                k.op("pe", lambda e: e.matmul(ps[2][:, :], ones[:, :], yb[p][j][:, :], start=(j == 0), stop=(j == 1)),
                     reads=[onesb, ybb[p][j]], writes=[pb[2]], sig=(j == 1))
            for j in range(2):
                k.op("pe", lambda e: e.matmul(ps[3][:, :], ones[:, :], sq[p][j][:, :], start=(j == 0), stop=(j == 1)),
                     reads=[onesb, sqb[p][j]], writes=[pb[3]], sig=(j == 1))

        def emit_ln(ti):
            p = ti % 2
            k.op("act", lambda e: e.activation(out=msb[:, :], in_=ps[2][:, :], func=AF.Copy, scale=1.0 / 256),
                 reads=[pb[2]], writes=[msbb])
            k.op("dve", lambda e: e.tensor_tensor(out=tt[:, :], in0=msb[:, :], in1=msb[:, :], op=ALU.mult),
                 reads=[msbb], writes=[ttb])
            k.op("dve", lambda e: e.scalar_tensor_tensor(out=var[:, :], in0=ps[3][:, :], scalar=1.0 / 256, in1=tt[:, :],
                                                         op0=ALU.mult, op1=ALU.subtract),
                 reads=[pb[3], ttb], writes=[varb])
            k.op("dve", lambda e: e.tensor_scalar(var[:, :], var[:, :], EPS, None, ALU.add), reads=[varb], writes=[varb])
            rstd_from(k, C, rstd[:, :], var[:, :], [varb], [rstdb], lnt[:, :], lntb)
            for j in range(2):
                k.op("dve", lambda e: e.tensor_tensor(out=dd[j][:, :], in0=yb[p][j][:, :], in1=msb[:, :], op=ALU.subtract),
                     reads=[ybb[p][j], msbb], writes=[ddb[j]])
                k.op("pool", lambda e: e.tensor_tensor(out=dd[j][:, :], in0=dd[j][:, :], in1=rstd[:, :], op=ALU.mult),
                     reads=[ddb[j], rstdb], writes=[ddb[j]])
                k.op("act", lambda e: e.activation(out=act[j][:, :], in_=dd[j][:, :], func=AF.Silu,
                                                   scale=cp[:, CP_LNG + j:CP_LNG + j + 1],
                                                   bias=cp[:, CP_LNB + j:CP_LNB + j + 1]),
                     reads=[ddb[j], cpb], writes=[actb[j]])

        def emit_pw(ti):
            for co in range(2):
                for ci in range(2):
                    k.op("pe", lambda e: e.matmul(ps[4 + co][:, :], pw[:, ci, co * 128:(co + 1) * 128], act[ci][:, :],
                                                  start=(ci == 0), stop=(ci == 1)),
                         reads=[pwb, actb[ci]], writes=[pb[4 + co]], sig=(ci == 1))
                k.op("act", lambda e: e.activation(out=og[co][:, :], in_=ps[4 + co][:, :], func=AF.Identity,
                                                   bias=cp[:, CP_PWB + co:CP_PWB + co + 1]),
                     reads=[pb[4 + co], cpb], writes=[ogb[co]])
                k.dma(C.ymixT[:, co, ti * 512:(ti + 1) * 512], og[co][:, :], reads=[ogb[co]], writes=[C.ymix_buf], q="pool")

        emit_conv(0)
        for ti in range(8):
            emit_stats_mm(ti)
            if ti + 1 < 8:
                emit_conv(ti + 1)
            emit_ln(ti)
            emit_pw(ti)
    k.barrier()


def phase4(k, nc, C, L, hook=None):
    lam_init = 0.8 - 0.6 * math.exp(-0.3 * L.idx)
    X = mybir.AxisListType.X
    with ExitStack() as es:
        lamv = sb(es, nc, "p4_lamv", [128, 4, 64], F32); lamb = Buf()
        for i, a in enumerate((L.lam_q1, L.lam_k1, L.lam_q2, L.lam_k2)):
            k.dma(lamv[:, i, :], a.partition_broadcast(128), writes=[lamb])
        prod = sb(es, nc, "p4_prod", [128, 2, 64], F32); prodb = Buf()
        dots = sb(es, nc, "p4_dots", [128, 4], F32); dotsb = Buf()
        for j in range(2):
            k.op("dve", lambda e: e.tensor_tensor(out=prod[:, j, :], in0=lamv[:, 2 * j, :], in1=lamv[:, 2 * j + 1, :], op=ALU.mult),
                 reads=[lamb], writes=[prodb])
            k.op("dve", lambda e: e.reduce_sum(dots[:, j:j + 1], prod[:, j, :], X), reads=[prodb], writes=[dotsb])
        k.op("act", lambda e: e.activation(out=dots[:, 2:4], in_=dots[:, 0:2], func=AF.Exp), reads=[dotsb], writes=[dotsb])
        neglam = sb(es, nc, "p4_nl", [128, 1], F32); nlb = Buf()
        k.op("dve", lambda e: e.tensor_tensor(out=neglam[:, :], in0=dots[:, 3:4], in1=dots[:, 2:3], op=ALU.subtract),
             reads=[dotsb], writes=[nlb])
        k.op("dve", lambda e: e.tensor_scalar(neglam[:, :], neglam[:, :], -lam_init, None, ALU.add), reads=[nlb], writes=[nlb])
        grow = sb(es, nc, "p4_grow", [128, 128], F32); growb = Buf()
        k.dma(grow[:, :], L.da_norm_g.partition_broadcast(128), writes=[growb])
        k.op("dve", lambda e: e.tensor_scalar(grow[:, :], grow[:, :], 1.0 - lam_init, None, ALU.mult), reads=[growb], writes=[growb])
        cmf = sb(es, nc, "p4_cmf", [128, 4, 512], F32); cmfb = Buf()
        cmask = sb(es, nc, "p4_cm", [128, 4, 512], BF16); cmb = Buf()
        k.dma(cmf[:, :, :], C.cmask_d[:, :, :], writes=[cmfb])
        k.op("dve", lambda e: e.tensor_copy(cmask[:, :, :], cmf[:, :, :]), reads=[cmfb], writes=[cmb])
        KT = [sb(es, nc, "p4_k%d" % i, [128, TA], BF16) for i in range(2)]; KTb = [Buf(), Buf()]
        QT = [sb(es, nc, "p4_q%d" % i, [128, TO], BF16) for i in range(2)]; QTb = [Buf(), Buf()]
        V = [sb(es, nc, "p4_v%d" % i, [128, 64, 129], BF16) for i in range(2)]; Vb = [Buf(), Buf()]
        for i in range(2):
            k.op("pool", lambda e: e.memset(V[i][:, :, 128:129], 1.0), writes=[Vb[i]])
        PT = [sb(es, nc, "p4_p%d" % i, [128, 1024], BF16) for i in range(2)]; PTb = [Buf() for _ in range(2)]
        o0 = [sb(es, nc, "p4_o0%d" % i, [128, 128], F32) for i in range(4)]; o0b = [Buf() for _ in range(4)]
        oo = [sb(es, nc, "p4_oo%d" % i, [128, 128], F32) for i in range(4)]; oob = [Buf() for _ in range(4)]
        yy = [sb(es, nc, "p4_yy%d" % i, [128, 128], BF16) for i in range(4)]; yyb = [Buf() for _ in range(4)]
        sm = [sb(es, nc, "p4_sm%d" % i, [128, 8], F32) for i in range(4)]; smb = [Buf() for _ in range(4)]
        st6 = [sb(es, nc, "p4_st%d" % i, [128, 6], F32) for i in range(4)]; st6b = [Buf() for _ in range(4)]
        ys = [sb(es, nc, "p4_ys%d" % i, [128, 512], BF16) for i in range(2)]; ysb = [Buf(), Buf()]
        Sps = [psb(es, nc, "p4_S%d" % i, [128, 1024], F32) for i in range(2)]; Sb = [Buf() for _ in range(2)]
        Aps = [psb(es, nc, "p4_A%d" % i, [128, 512], F32) for i in range(3)]; Ab = [Buf() for _ in range(3)]
        tp = psb(es, nc, "p4_tp", [128, 1024], BF16); tpb = Buf()

        def acc(a):
            return Aps[a // 3][:, (a % 3) * 129:(a % 3) * 129 + 129], Ab[a // 3]

        pending = []

        def flush_pending():
            while pending:
                ph_, pqt = pending.pop(0)
                for qb in range(4):
                    k.op("pe", lambda e: e.transpose(tp[:, qb * 128:(qb + 1) * 128], yy[qb][:, :], C.identbf[:, :]),
                         reads=[yyb[qb], C.identbfb], writes=[tpb], sig=True)
                yi = (ph_ * 8 + pqt) % 2
                k.op("dve", lambda e: e.tensor_copy(ys[yi][:, :], tp[:, 0:512]), reads=[tpb], writes=[ysb[yi]])
                k.dma(C.ymixT[:, 4 + ph_, pqt * 512:(pqt + 1) * 512], ys[yi][:, :], reads=[ysb[yi]], writes=[C.ymix_buf], q="pool")

        for h in range(4):
            hi = h % 2
            k.dma(KT[hi][:, :], C.dk[:, h, :], reads=[C.dk_buf], writes=[KTb[hi]])
            k.dma(QT[hi][:, :], C.dq[:, h, :], reads=[C.dq_buf], writes=[QTb[hi]])
            k.dma(V[hi][:, :, 0:128], C.dv[:, h * 128:(h + 1) * 128].rearrange("(n p) f -> p n f", p=128),
                  reads=[C.dv_buf], writes=[Vb[hi]])
            for qt in range(8):
                if hook is not None:
                    hook()
                nkb = 32 + 4 * qt + 4
                d0 = 32 + 4 * qt

                def emit_S(kb):
                    dj = kb - d0
                    si = kb % 2
                    c0 = 128 * dj if dj > 0 else 0
                    for m in range(2):
                        k.op("pe", lambda e: e.matmul(Sps[si][:, m * 512 + c0:(m + 1) * 512], KT[hi][m * 64:(m + 1) * 64, kb * 128:(kb + 1) * 128],
                                                      QT[hi][m * 64:(m + 1) * 64, qt * 512 + c0:(qt + 1) * 512],
                                                      start=True, stop=(dj < 0)),
                             reads=[KTb[hi], QTb[hi]], writes=[Sb[si]], sig=(dj < 0 and m == 1))
                        if dj >= 0:
                            k.op("pe", lambda e: e.matmul(Sps[si][:, m * 512 + c0:(m + 1) * 512], C.identbf[:, :], cmask[:, dj, c0:512],
                                                          start=False, stop=True),
                                 reads=[C.identbfb, cmb], writes=[Sb[si]], sig=(m == 1))
                    bias = C.flag[:, 1:2] if kb < 32 else 0.0
                    if c0 == 0:
                        k.op("act", lambda e: e.activation(out=PT[si][:, :], in_=Sps[si][:, :], func=AF.Exp, bias=bias, scale=0.125),
                             reads=[Sb[si], C.flagb], writes=[PTb[si]])
                    else:
                        k.op("act", lambda e: e.activation(out=PT[si][:, :].rearrange("p (m c) -> p m c", m=2)[:, :, c0:512],
                                                           in_=Sps[si][:, :].rearrange("p (m c) -> p m c", m=2)[:, :, c0:512],
                                                           func=AF.Exp, bias=bias, scale=0.125),
                             reads=[Sb[si], C.flagb], writes=[PTb[si]])

                def emit_PV(kb):
                    dj = kb - d0
                    si = kb % 2
                    lst = []
                    for qb in range(4):
                        if dj >= 0 and qb < dj:
                            continue
                        for m in range(2):
                            lst.append((qb, m))
                    for idx, (qb, m) in enumerate(lst):
                        ap, ab = acc(qb * 2 + m)
                        k.op("pe", lambda e: e.matmul(ap, PT[si][:, m * 512 + qb * 128:m * 512 + (qb + 1) * 128], V[hi][:, kb, :],
                                                      start=(kb == 0 and (qb * 2 + m) % 3 == 0), stop=(kb == d0 + qb), skip_group_check=True),
                             reads=[PTb[si], Vb[hi]], writes=[ab], sig=(idx == len(lst) - 1))

                emit_S(0)
                for kb in range(nkb):
                    if kb + 1 < nkb:
                        emit_S(kb + 1)
                    emit_PV(kb)
                    if kb == 12:
                        flush_pending()
                for step in range(9):
                    for qb in range(4):
                        i = qb
                        a0, ab0 = acc(qb * 2)
                        a1, ab1 = acc(qb * 2 + 1)
                        if step == 0:
                            k.op("dve", lambda e: e.reciprocal(sm[i][:, 0:1], a0[:, 128:129]), reads=[ab0], writes=[smb[i]])
                            k.op("dve", lambda e: e.reciprocal(sm[i][:, 1:2], a1[:, 128:129]), reads=[ab1], writes=[smb[i]])
                        elif step == 1:
                            k.op("dve", lambda e: e.tensor_tensor(out=sm[i][:, 2:3], in0=sm[i][:, 1:2], in1=neglam[:, 0:1], op=ALU.mult),
                                 reads=[smb[i], nlb], writes=[smb[i]])
                            k.op("dve", lambda e: e.tensor_scalar(o0[i][:, :], a0[:, 0:128], sm[i][:, 0:1], None, ALU.mult),
                                 reads=[ab0, smb[i]], writes=[o0b[i]])
                        elif step == 2:
                            k.op("dve", lambda e: e.scalar_tensor_tensor(out=oo[i][:, :], in0=a1[:, 0:128], scalar=sm[i][:, 2:3],
                                                                         in1=o0[i][:, :], op0=ALU.mult, op1=ALU.add),
                                 reads=[ab1, smb[i], o0b[i]], writes=[oob[i]])
                        elif step == 3:
                            k.op("dve", lambda e: e.bn_stats(st6[i][:, :], oo[i][:, :]), reads=[oob[i]], writes=[st6b[i]])
                        elif step == 4:
                            k.op("dve", lambda e: e.bn_aggr(sm[i][:, 3:5], st6[i][:, :]), reads=[st6b[i]], writes=[smb[i]])
                        elif step == 5:
                            k.op("dve", lambda e: e.scalar_tensor_tensor(out=sm[i][:, 5:6], in0=sm[i][:, 3:4], scalar=sm[i][:, 3:4],
                                                                         in1=sm[i][:, 4:5], op0=ALU.mult, op1=ALU.add),
                                 reads=[smb[i]], writes=[smb[i]])
                        elif step == 6:
                            k.op("dve", lambda e: e.tensor_scalar(sm[i][:, 5:6], sm[i][:, 5:6], EPS, None, ALU.add), reads=[smb[i]], writes=[smb[i]])
                        elif step == 7:
                            rstd_from(k, C, sm[i][:, 7:8], sm[i][:, 5:6], [smb[i]], [smb[i]], sm[i][:, 6:7], smb[i])
                        else:
                            k.op("dve", lambda e: e.scalar_tensor_tensor(out=yy[i][:, :], in0=oo[i][:, :], scalar=sm[i][:, 7:8],
                                                                         in1=grow[:, :], op0=ALU.mult, op1=ALU.mult),
                                 reads=[oob[i], smb[i], growb], writes=[yyb[i]])
                pending.append((h, qt))
        flush_pending()
    k.barrier()


def phase3(k, nc, C, L):
    NB = TA // 128
    NO = TO // 128
    NG = NB * 4
    with ExitStack() as es:
        cp = sb(es, nc, "p3_cp", [128, NCP], F32); cpb = Buf()
        k.dma(cp[:, :], L.colpack[:, :], writes=[cpb])
        dgm = sb(es, nc, "p3_dg", [128, 16, 128], BF16); dgb = Buf()
        for i in range(16):
            k.op("dve", lambda e: e.tensor_scalar(dgm[:, i, :], C.identbf[:, :], cp[:, CP_MLW + i:CP_MLW + i + 1], None, ALU.mult),
                 reads=[C.identbfb, cpb], writes=[dgb])
        utri = sb(es, nc, "p3_u", [128, 128], F32); ub = Buf()
        k.dma(utri[:, :], C.utri_d[:, :], writes=[ub])
        ones = sb(es, nc, "p3_ones", [128, 128], F32); onesb = Buf()
        k.op("dve", lambda e: e.memset(ones[:, :], 1.0), writes=[onesb])
        mlg = sb(es, nc, "p3_mlg", [128, 256], F32); mlgb = Buf()
        k.dma(mlg[:, :], L.ml_norm_g.partition_broadcast(128), writes=[mlgb])
        qT = sb(es, nc, "p3_qT", [128, 2, TO], BF16); qTb = Buf()
        kT = sb(es, nc, "p3_kT", [128, 2, TA], BF16); kTb = Buf()
        gts = sb(es, nc, "p3_g", [128, NB, 8], F32); gtb = Buf()
        k.dma(gts[:, :, :], C.gates.rearrange("(n p) g -> p n g", p=128), reads=[C.gates_buf], writes=[gtb])
        Call = sb(es, nc, "p3_Call", [128, NO, 2, 65], BF16); Callb = Buf()
        vaug = sb(es, nc, "p3_v", [128, NB, 4, 65], BF16); vb = Buf()
        A = [psb(es, nc, "p3_A%d" % i, [128, 512], F32) for i in range(2)]; Apb = [Buf(), Buf()]
        Np = [psb(es, nc, "p3_N%d" % i, [128, 512], F32) for i in range(2)]; Npb = [Buf(), Buf()]
        Dp = [psb(es, nc, "p3_D%d" % i, [128, 512], F32) for i in range(2)]; Dpb = [Buf(), Buf()]
        tp = psb(es, nc, "p3_tp", [128, 1024], BF16); tpb = Buf()
        tp2 = psb(es, nc, "p3_tp2", [128, 1024], BF16)
        tpx = [tp, tp2]
        with ExitStack() as esA:
            inp = [sb(esA, nc, "p3_in%d" % i, [128, 3 + TO], BF16) for i in range(2)]; inpb = [Buf(), Buf()]
            na = 0
            for j in range(4):
                k.op("pool", lambda e: e.memset(inp[0][:, 0:3], 0.0), writes=[inpb[0]])
                k.dma(inp[0][:, 3:], C.mqk[:, j, 0:TO], reads=[C.mqk_buf], writes=[inpb[0]])
                k.dma(inp[1][:, 3:], C.mqk[:, j, TO:TA], reads=[C.mqk_buf], writes=[inpb[1]])
                k.op("dve", lambda e: e.tensor_scalar(inp[1][:, 0:3], inp[0][:, TO:TO + 3], C.flag[:, 0:1], None, ALU.mult),
                     reads=[inpb[0], C.flagb], writes=[inpb[1]])
                for part in range(2):
                    if part == 0 and j < 2:
                        continue
                    for ti in range(8):
                        ai = na % 2
                        na += 1
                        for t in range(4):
                            k.op("pe", lambda e: e.matmul(A[ai][:, :], dgm[:, j * 4 + t, :], inp[part][:, ti * 512 + t:ti * 512 + t + 512],
                                                          start=(t == 0), stop=(t == 3)),
                                 reads=[dgb, inpb[part]], writes=[Apb[ai]], sig=(t == 3))
                        if j < 2:
                            dst, dbuf = qT[:, j, ti * 512:(ti + 1) * 512], qTb
                        else:
                            c0 = part * TO + ti * 512
                            dst, dbuf = kT[:, j - 2, c0:c0 + 512], kTb
                        k.op("act", lambda e: e.activation(out=dst, in_=A[ai][:, :], func=AF.Silu, bias=cp[:, CP_MLB + j:CP_MLB + j + 1]),
                             reads=[Apb[ai], cpb], writes=[dbuf])
            k.barrier()
        lf = sb(es, nc, "p3_lf", [128, NB, 4], F32); lfb = Buf()
        li = sb(es, nc, "p3_li", [128, NB, 4], F32); lib = Buf()
        bcol = sb(es, nc, "p3_bc", [128, NG], F32); bcb = Buf()
        apr = sb(es, nc, "p3_ap", [128, NG], F32); aprb = Buf()
        eb = sb(es, nc, "p3_eb", [128, NG], F32); ebb = Buf()
        eg = sb(es, nc, "p3_eg", [128, NB, 4], F32); egb = Buf()
        egc = sb(es, nc, "p3_egc", [128, NB, 2], F32); egcb = Buf()
        k.op("act", lambda e: e.activation(out=lf[:, :, :], in_=gts[:, :, 4:8], func=AF.Exp, scale=-1.0), reads=[gtb], writes=[lfb])
        k.op("act", lambda e: e.activation(out=lf[:, :, :], in_=lf[:, :, :], func=AF.Ln, bias=1.0), reads=[lfb], writes=[lfb])
        k.op("dve", lambda e: e.tensor_scalar(lf[:, :, :], lf[:, :, :], -1.0, None, ALU.mult), reads=[lfb], writes=[lfb])
        k.op("dve", lambda e: e.tensor_copy(li[:, :, :], gts[:, :, 0:4]), reads=[gtb], writes=[lib])
        lf2 = lf[:, :, :].rearrange("p a b -> p (a b)")
        li2 = li[:, :, :].rearrange("p a b -> p (a b)")
        eg2 = eg[:, :, :].rearrange("p a b -> p (a b)")
        k.op("pe", lambda e: e.matmul(A[0][:, 0:NG], utri[:, :], lf2, start=True, stop=True), reads=[ub, lfb], writes=[Apb[0]])
        k.op("pe", lambda e: e.matmul(A[1][:, 0:NG], ones[:, :], lf2, start=True, stop=True), reads=[onesb, lfb], writes=[Apb[1]])
        k.op("dve", lambda e: e.tensor_copy(bcol[:, :], A[0][:, 0:NG]), reads=[Apb[0]], writes=[bcb])
        k.op("dve", lambda e: e.tensor_tensor(out=apr[:, :], in0=li2, in1=bcol[:, :], op=ALU.subtract), reads=[lib, bcb], writes=[aprb])
        k.op("act", lambda e: e.activation(out=apr[:, :], in_=apr[:, :], func=AF.Exp), reads=[aprb], writes=[aprb])
        k.op("act", lambda e: e.activation(out=eb[:, :], in_=bcol[:, :], func=AF.Exp), reads=[bcb], writes=[ebb])
        k.op("dve", lambda e: e.tensor_scalar(eb[:, :], eb[:, :], 0.125, None, ALU.mult), reads=[ebb], writes=[ebb])
        k.op("act", lambda e: e.activation(out=eg2, in_=A[1][:, 0:NG], func=AF.Exp), reads=[Apb[1]], writes=[egb])
        for jc in range(2):
            for hp in range(2):
                ps_ = slice(hp * 64, (hp + 1) * 64)
                k.op("dve", lambda e: e.tensor_copy(egc[ps_, :, jc], eg[ps_, :, 2 * jc + hp]), reads=[egb], writes=[egcb])
        with ExitStack() as esB:
            ktok = sb(esB, nc, "p3_ktok", [128, NB, 256], BF16); ktokb = [Buf() for _ in range(NB)]
            dC = sb(esB, nc, "p3_dC", [128, NB, 2, 65], F32); dCb = [[[Buf(), Buf()] for _ in range(2)] for _ in range(NB)]
            tps = [Buf(), Buf()]
            for h in range(4):
                k.dma(vaug[:, :, h, 0:64], C.mv[:, h * 64:(h + 1) * 64].rearrange("(n p) d -> p n d", p=128), reads=[C.mv_buf], writes=[vb])
            k.op("pool", lambda e: e.memset(vaug[:, :, :, 64:65], 1.0), writes=[vb])
            for blk in range(NB):
                for jc in range(2):
                    k.op("pe", lambda e: e.transpose(tpx[blk % 2][:, jc * 128:(jc + 1) * 128],
                                                     kT[:, jc, blk * 128:(blk + 1) * 128], C.identbf[:, :]),
                         reads=[kTb, C.identbfb], writes=[tps[blk % 2]], sig=(jc == 1))
                if blk % 2 == 0:
                    k.op("dve", lambda e: e.tensor_copy(ktok[:, blk, :], tp[:, 0:256]), reads=[tps[0]], writes=[ktokb[blk]])
                else:
                    k.op("act", lambda e: e.copy(ktok[:, blk, :], tp2[:, 0:256]), reads=[tps[1]], writes=[ktokb[blk]])
            pvp = [sb(esB, nc, "p3_pv%d" % i, [128, 130], BF16) for i in range(8)]; pvpb = [Buf() for _ in range(8)]
            nv = 0
            for blk in range(NB):
                own = blk >= NB // 2
                ob = blk - NB // 2
                for jc in range(2):
                    pv, pvb_ = pvp[nv % 8][:, :], pvpb[nv % 8]
                    for hp in range(2):
                        h = 2 * jc + hp
                        k.op("pool" if hp == 0 else "dve",
                             lambda e: e.tensor_scalar(pv[:, hp * 65:(hp + 1) * 65], vaug[:, blk, h, :],
                                                       apr[:, blk * 4 + h:blk * 4 + h + 1], None, ALU.mult),
                             reads=[vb, aprb], writes=[pvb_])
                    di = nv % 2
                    nv += 1
                    k.op("pe", lambda e: e.matmul(Dp[di][:, 0:130], ktok[:, blk, jc * 128:(jc + 1) * 128], pv, start=True, stop=True),
                         reads=[ktokb[blk], pvb_], writes=[Dpb[di]], sig=True)
                    for hp in range(2):
                        h = 2 * jc + hp
                        ps_ = slice(hp * 64, (hp + 1) * 64)
                        if hp == 0:
                            k.op("dve", lambda e: e.tensor_scalar(dC[ps_, blk, jc, :], Dp[di][ps_, hp * 65:(hp + 1) * 65],
                                                                  eg[ps_, blk, h:h + 1], None, ALU.mult),
                                 reads=[Dpb[di], egb], writes=[dCb[blk][jc][hp]])
                        else:
                            k.op("act", lambda e: e.activation(out=dC[ps_, blk, jc, :], in_=Dp[di][ps_, hp * 65:(hp + 1) * 65],
                                                               func=AF.Copy, scale=eg[ps_, blk, h:h + 1]),
                                 reads=[Dpb[di], egb], writes=[dCb[blk][jc][hp]])
            Cst = sb(esB, nc, "p3_C", [128, 2, 65], F32)
            Ch = sb(esB, nc, "p3_Ch", [128, NO, 2, 65], F32)
            Cb2 = [Buf(), Buf()]
            k.op("dve", lambda e: e.memset(Cst[:, :, :], 0.0), writes=Cb2)
            for blk in range(NB - 1):
                own_next = blk + 1 >= NB // 2
                for jc in range(2):
                    if blk + 1 == NB // 2:
                        k.op("dve", lambda e: e.scalar_tensor_tensor(out=Cst[:, jc, :], in0=Cst[:, jc, :], scalar=egc[:, blk, jc:jc + 1],
                                                                     in1=dC[:, blk, jc, :], op0=ALU.mult, op1=ALU.add),
                             reads=[Cb2[jc], egcb] + dCb[blk][jc], writes=[Cb2[jc]])
                        k.op("dve", lambda e: e.tensor_scalar(Ch[:, 0, jc, :], Cst[:, jc, :], C.flag[:, 0:1], None, ALU.mult),
                             reads=[C.flagb, Cb2[jc]], writes=[Cb2[jc]])
                        continue
                    if blk + 1 < NB // 2:
                        src, dst = Cst[:, jc, :], Cst[:, jc, :]
                    else:
                        ob = blk - NB // 2
                        src, dst = Ch[:, ob, jc, :], Ch[:, ob + 1, jc, :]
                    k.op("dve", lambda e: e.scalar_tensor_tensor(out=dst, in0=src, scalar=egc[:, blk, jc:jc + 1],
                                                                 in1=dC[:, blk, jc, :], op0=ALU.mult, op1=ALU.add),
                         reads=[Cb2[jc], egcb] + dCb[blk][jc], writes=[Cb2[jc]])
            for q in range(4):
                k.op("act" if q % 2 == 0 else "pool",
                     (lambda e: e.copy(Call[:, q * 8:(q + 1) * 8, :, :], Ch[:, q * 8:(q + 1) * 8, :, :])) if q % 2 == 0 else
                     (lambda e: e.tensor_copy(Call[:, q * 8:(q + 1) * 8, :, :], Ch[:, q * 8:(q + 1) * 8, :, :])),
                     reads=Cb2, writes=[Callb])
            k.barrier()
        mo = sb(es, nc, "p3_mo", [128, NO, 256], BF16); mob = Buf()
        k.dma(mo[:, :, :], C.mo.rearrange("(n p) f -> p n f", p=128), reads=[C.mo_buf], writes=[mob])
        Cbf = [sb(es, nc, "p3_Cb%d" % i, [128, 2, 65], BF16) for i in range(2)]; Cbfb = [Buf(), Buf()]
        pvp = [sb(es, nc, "p3_pw%d" % i, [128, 130], BF16) for i in range(4)]; pvpb = [Buf() for _ in range(4)]
        scT = [sb(es, nc, "p3_sc%d" % i, [128, 128], BF16) for i in range(8)]; scTb = [Buf() for _ in range(8)]
        hbuf = [sb(es, nc, "p3_h%d" % i, [128, 256], F32) for i in range(2)]; hbufb = [Buf(), Buf()]
        xn = [sb(es, nc, "p3_xn%d" % i, [128, 256], F32) for i in range(2)]; xnb = [Buf(), Buf()]
        yml = [sb(es, nc, "p3_y%d" % i, [128, 256], BF16) for i in range(2)]; ymlb = [Buf(), Buf()]
        ysg = [sb(es, nc, "p3_ys%d" % i, [128, 2, 128], BF16) for i in range(2)]; ysgb = [Buf(), Buf()]
        sm = [sb(es, nc, "p3_sm%d" % i, [128, 16], F32) for i in range(2)]; smb = [Buf(), Buf()]
        st6 = [sb(es, nc, "p3_st%d" % i, [128, 4, 6], F32) for i in range(2)]; st6b = [Buf(), Buf()]
        mvs = [sb(es, nc, "p3_mv%d" % i, [128, 4, 2], F32) for i in range(2)]; mvsb = [Buf(), Buf()]
        rs = [sb(es, nc, "p3_rs%d" % i, [128, 8], F32) for i in range(2)]; rsb = [Buf(), Buf()]
        smh = [[Buf() for _ in range(4)] for _ in range(2)]
        hbh = [[Buf() for _ in range(4)] for _ in range(2)]
        sth = [[Buf() for _ in range(4)] for _ in range(2)]
        mvh = [[Buf() for _ in range(4)] for _ in range(2)]
        xnh = [[Buf() for _ in range(4)] for _ in range(2)]
        tpq = [Buf(), Buf()]

        def emit_A(ob):
            blk = NB // 2 + ob
            bi = ob % 2
            k.op("act", lambda e: e.copy(Cbf[bi][:, :, :], Call[:, ob, :, :]), reads=[Callb], writes=[Cbfb[bi]])
            for jc in range(2):
                pi = bi * 2 + jc
                for hp in range(2):
                    h = 2 * jc + hp
                    k.op("pool" if hp == 0 else "dve",
                         lambda e: e.tensor_scalar(pvp[pi][:, hp * 65:(hp + 1) * 65], vaug[:, blk, h, :],
                                                   apr[:, blk * 4 + h:blk * 4 + h + 1], None, ALU.mult),
                         reads=[vb, aprb], writes=[pvpb[pi]])
            for h in range(4):
                jc, hp = h // 2, h % 2
                ps_ = slice(hp * 64, (hp + 1) * 64)
                k.op("pe", lambda e: e.matmul(A[bi][:, h * 128:(h + 1) * 128], kT[ps_, jc, blk * 128:(blk + 1) * 128],
                                              qT[ps_, jc, ob * 128:(ob + 1) * 128], start=True, stop=True),
                     reads=[kTb, qTb], writes=[Apb[bi]], sig=True)
                k.op("dve", lambda e: e.tensor_tensor(out=scT[bi * 4 + h][:, :], in0=A[bi][:, h * 128:(h + 1) * 128], in1=utri[:, :],
                                                      op=ALU.mult),
                     reads=[Apb[bi], ub], writes=[scTb[bi * 4 + h]])

        def emit_B(ob):
            bi = ob % 2
            for h in range(4):
                jc, hp = h // 2, h % 2
                ps_ = slice(hp * 64, (hp + 1) * 64)
                pi = bi * 2 + jc
                k.op("pe", lambda e: e.matmul(Np[bi][:, h * 65:(h + 1) * 65], scT[bi * 4 + h][:, :], pvp[pi][:, hp * 65:(hp + 1) * 65],
                                              start=True, stop=False),
                     reads=[scTb[bi * 4 + h], pvpb[pi]], writes=[Npb[bi]], sig=False)
                k.op("pe", lambda e: e.matmul(Np[bi][:, h * 65:(h + 1) * 65], qT[ps_, jc, ob * 128:(ob + 1) * 128],
                                              Cbf[bi][ps_, jc, :], start=False, stop=True),
                     reads=[qTb, Cbfb[bi]], writes=[Npb[bi]], sig=True)

        def emit_C(ob):
            blk = NB // 2 + ob
            bi = ob % 2
            i = ob % 2
            for step in range(6):
                for h in range(4):
                    c = blk * 4 + h
                    nh = Np[bi][:, h * 65:(h + 1) * 65]
                    sb_ = smh[i][h]
                    if step == 0:
                        k.op("dve", lambda e: e.tensor_scalar(sm[i][:, h:h + 1], nh[:, 64:65], eb[:, c:c + 1], None, ALU.mult),
                             reads=[Npb[bi], ebb], writes=[sb_])
                    elif step == 1:
                        k.op("dve", lambda e: e.tensor_scalar(sm[i][:, 4 + h:5 + h], sm[i][:, h:h + 1], 1.0, None, ALU.max),
                             reads=[sb_], writes=[sb_])
                    elif step == 2:
                        k.op("dve", lambda e: e.scalar_tensor_tensor(out=sm[i][:, h:h + 1], in0=sm[i][:, h:h + 1], scalar=-1.0,
                                                                     in1=sm[i][:, 4 + h:5 + h], op0=ALU.mult, op1=ALU.max),
                             reads=[sb_], writes=[sb_])
                    elif step == 3:
                        k.op("dve", lambda e: e.reciprocal(sm[i][:, 4 + h:5 + h], sm[i][:, h:h + 1]), reads=[sb_], writes=[sb_])
                    elif step == 4:
                        k.op("dve", lambda e: e.tensor_tensor(out=sm[i][:, 8 + h:9 + h], in0=sm[i][:, 4 + h:5 + h], in1=eb[:, c:c + 1], op=ALU.mult),
                             reads=[sb_, ebb], writes=[sb_])
                    else:
                        k.op("dve", lambda e: e.scalar_tensor_tensor(out=hbuf[i][:, h * 64:(h + 1) * 64], in0=nh[:, 0:64], scalar=sm[i][:, 8 + h:9 + h],
                                                                     in1=mo[:, ob, h * 64:(h + 1) * 64], op0=ALU.mult, op1=ALU.mult),
                             reads=[Npb[bi], sb_, mob], writes=[hbh[i][h]])
            for h in range(4):
                k.op("dve", lambda e: e.bn_stats(st6[i][:, h, :], hbuf[i][:, h * 64:(h + 1) * 64]), reads=[hbh[i][h]], writes=[sth[i][h]])
            for h in range(4):
                k.op("dve", lambda e: e.bn_aggr(mvs[i][:, h, :], st6[i][:, h, :]), reads=[sth[i][h]], writes=[mvh[i][h]])
            k.op("dve", lambda e: e.tensor_scalar(rs[i][:, 0:4], mvs[i][:, :, 1], EPS, None, ALU.add), reads=mvh[i], writes=[rsb[i]])
            rstd_from(k, C, rs[i][:, 0:4], rs[i][:, 0:4], [rsb[i]], [rsb[i]], rs[i][:, 4:8], rsb[i])
            for h in range(4):
                k.op("dve" if h % 2 == 0 else "pool",
                     lambda e: e.tensor_scalar(xn[i][:, h * 64:(h + 1) * 64], hbuf[i][:, h * 64:(h + 1) * 64], mvs[i][:, h, 0:1],
                                               rs[i][:, h:h + 1], ALU.subtract, ALU.mult),
                     reads=[hbh[i][h], mvh[i][h], rsb[i]], writes=[xnh[i][h]])
            k.op("pool", lambda e: e.tensor_tensor(out=yml[i][:, :], in0=xn[i][:, :], in1=mlg[:, :], op=ALU.mult),
                 reads=xnh[i] + [mlgb], writes=[ymlb[i]])
            for jc in range(2):
                k.op("pe", lambda e: e.transpose(tpx[i][:, 512 + jc * 128:512 + (jc + 1) * 128],
                                                 yml[i][:, jc * 128:(jc + 1) * 128], C.identbf[:, :]),
                     reads=[ymlb[i], C.identbfb], writes=[tpq[i]], sig=(jc == 1))
            k.op("act", lambda e: e.copy(ysg[i][:, :, :], tpx[i][:, 512:768].rearrange("p (a b) -> p a b", a=2)),
                 reads=[tpq[i]], writes=[ysgb[i]])
            k.dma(C.ymixT[:, 2:4, ob * 128:(ob + 1) * 128], ysg[i][:, :, :], reads=[ysgb[i]], writes=[C.ymix_buf], q="pool")

        emit_A(0)
        for ob in range(NO):
            if ob + 1 < NO:
                emit_A(ob + 1)
            emit_B(ob)
            emit_C(ob)
    k.barrier()


def layer_norm_rows(k, C, z, zb, st, stb, mv, mvb, tmp, tmpb, grow, brow, rowb, outt, outb, eng2="pool"):
    for hf in range(2):
        k.op("dve", lambda e: e.bn_stats(st[:, hf, :], z[:, hf * 512:(hf + 1) * 512]), reads=[zb], writes=[stb])
    k.op("dve", lambda e: e.bn_aggr(mv[:, 0:2], st[:, :, :].rearrange("p a b -> p (a b)")), reads=[stb], writes=[mvb])
    k.op("dve", lambda e: e.tensor_scalar(mv[:, 2:3], mv[:, 1:2], EPS, None, ALU.add), reads=[mvb], writes=[mvb])
    rstd_from(k, C, mv[:, 3:4], mv[:, 2:3], [mvb], [mvb], tmp[:, 0:1], tmpb)
    k.op("dve", lambda e: e.tensor_scalar(z[:, :], z[:, :], mv[:, 0:1], mv[:, 3:4], ALU.subtract, ALU.mult),
         reads=[zb, mvb], writes=[zb])
    k.op(eng2, lambda e: e.tensor_tensor(out=outt[:, :], in0=z[:, :], in1=grow[:, :], op=ALU.mult),
         reads=[zb, rowb], writes=[outb])
    k.op(eng2, lambda e: e.tensor_tensor(out=outt[:, :], in0=outt[:, :], in1=brow[:, :], op=ALU.add),
         reads=[outb, rowb], writes=[outb])


def phase5(k, nc, C, L, xres_src):
    with ExitStack() as es:
        wout = sb(es, nc, "p5_w", [128, 8, D], BF16); woutb = Buf()
        load_cast(k, es, nc, "p5w", wout, woutb, L.w_out.rearrange("(kc p) n -> p kc n", p=128), D, 256)
        grow = sb(es, nc, "p5_g", [128, D], F32); brow = sb(es, nc, "p5_b", [128, D], F32); rowb = Buf()
        k.dma(grow[:, :], L.ln1_g.partition_broadcast(128), writes=[rowb])
        k.dma(brow[:, :], L.ln1_b.partition_broadcast(128), writes=[rowb])
        ymt = [sb(es, nc, "p5_y%d" % i, [128, 8, 128], BF16) for i in range(4)]; ymtb = [Buf() for _ in range(4)]
        xr = [sb(es, nc, "p5_x%d" % i, [128, D], F32) for i in range(4)]; xrb = [Buf() for _ in range(4)]
        z = [sb(es, nc, "p5_z%d" % i, [128, D], F32) for i in range(4)]; zb = [Buf() for _ in range(4)]
        xt = [sb(es, nc, "p5_t%d" % i, [128, 8, 128], BF16) for i in range(4)]; xtb = [Buf() for _ in range(4)]
        st = [sb(es, nc, "p5_st%d" % i, [128, 2, 6], F32) for i in range(4)]; stb = [Buf() for _ in range(4)]
        mvt = [sb(es, nc, "p5_mv%d" % i, [128, 2, 6], F32) for i in range(2)]; mvb = [Buf(), Buf()]
        ps = [psb(es, nc, "p5_ps%d" % i, [128, 512], F32) for i in range(8)]
        pb = [Buf() for _ in range(8)]
        NGR = TO // 256

        def emit_MM(g):
            for j in range(2):
                bi = (g % 2) * 2 + j
                r0 = (g * 2 + j) * 128
                k.dma(ymt[bi][:, :, :], C.ymixT[:, :, r0:r0 + 128], reads=[C.ymix_buf], writes=[ymtb[bi]])
                k.dma(xr[bi][:, :], xres_src[r0:r0 + 128, :], writes=[xrb[bi]])
                for hf in range(2):
                    pi = j * 2 + hf
                    for kc in range(8):
                        k.op("pe", lambda e: e.matmul(ps[pi][:, :], ymt[bi][:, kc, :], wout[:, kc, hf * 512:(hf + 1) * 512],
                                                      start=(kc == 0), stop=(kc == 7)),
                             reads=[ymtb[bi], woutb], writes=[pb[pi]], sig=(kc == 7))

        def emit_stt(g):
            gp = g % 2
            for j in range(2):
                bi = gp * 2 + j
                for hf in range(2):
                    pi = j * 2 + hf
                    k.op("dve", lambda e: e.scalar_tensor_tensor(out=z[bi][:, hf * 512:(hf + 1) * 512], in0=xr[bi][:, hf * 512:(hf + 1) * 512],
                                                                 scalar=ALPHA, in1=ps[pi][:, :], op0=ALU.mult, op1=ALU.add),
                         reads=[xrb[bi], pb[pi]], writes=[zb[bi]])
                    k.op("dve", lambda e: e.bn_stats(st[bi][:, hf, :], z[bi][:, hf * 512:(hf + 1) * 512]), reads=[zb[bi]], writes=[stb[bi]])
                k.op("dve", lambda e: e.bn_aggr(mvt[gp][:, j, 0:2], st[bi][:, :, :].rearrange("p a b -> p (a b)")),
                     reads=[stb[bi]], writes=[mvb[gp]])

        def emit_LN(g):
            gp = g % 2
            k.op("dve", lambda e: e.tensor_scalar(mvt[gp][:, :, 2], mvt[gp][:, :, 1], EPS, None, ALU.add), reads=[mvb[gp]], writes=[mvb[gp]])
            rstd_from(k, C, mvt[gp][:, :, 4], mvt[gp][:, :, 2], [mvb[gp]], [mvb[gp]], mvt[gp][:, :, 3], mvb[gp])
            for j in range(2):
                bi = gp * 2 + j
                r0 = (g * 2 + j) * 128
                k.op("dve", lambda e: e.tensor_scalar(z[bi][:, :], z[bi][:, :], mvt[gp][:, j, 0:1], mvt[gp][:, j, 4:5], ALU.subtract, ALU.mult),
                     reads=[zb[bi], mvb[gp]], writes=[zb[bi]])
                k.op("dve", lambda e: e.tensor_tensor(out=z[bi][:, :], in0=z[bi][:, :], in1=grow[:, :], op=ALU.mult),
                     reads=[zb[bi], rowb], writes=[zb[bi]])
                k.op("pool", lambda e: e.tensor_tensor(out=z[bi][:, :], in0=z[bi][:, :], in1=brow[:, :], op=ALU.add),
                     reads=[zb[bi], rowb], writes=[zb[bi]])
                k.dma(C.x1[r0:r0 + 128, :], z[bi][:, :], reads=[zb[bi]], writes=[C.x1_buf], q="pool")

        def emit_T(g):
            gp = g % 2
            for j in range(2):
                bi = gp * 2 + j
                r0 = (g * 2 + j) * 128
                for hf in range(2):
                    pi = 4 + j * 2 + hf
                    for jj in range(4):
                        kc = hf * 4 + jj
                        k.op("pe", lambda e: e.transpose(ps[pi][:, jj * 128:(jj + 1) * 128], z[bi][:, kc * 128:(kc + 1) * 128], C.ident[:, :]),
                             reads=[zb[bi], C.identb], writes=[pb[pi]], sig=(jj == 3))
                    k.op("act", lambda e: e.copy(xt[bi][:, hf * 4:(hf + 1) * 4, :], ps[pi][:, :].rearrange("p (a b) -> p a b", a=4)),
                         reads=[pb[pi]], writes=[xtb[bi]])
                k.dma(C.x1T[:, :, r0:r0 + 128], xt[bi][:, :, :], reads=[xtb[bi]], writes=[C.x1T_buf], q="pool")

        emit_MM(0)
        emit_stt(0)
        for g in range(NGR):
            if g + 1 < NGR:
                emit_MM(g + 1)
            emit_LN(g)
            emit_T(g)
            if g + 1 < NGR:
                emit_stt(g + 1)
    k.barrier()


def phase6(k, nc, C, L, xdst, xdst_buf, PU):
    with ExitStack() as es:
        PU.finish()
        wup = PU.dst
        wupb = PU.bufs
        wdn = sb(es, nc, "p6_wd", [128, 32, D], BF16)
        wdnb = [[Buf(), Buf()] for _ in range(8)]
        std = [sb(es, nc, "p6_sd%d" % i, [128, 16, 128], F32) for i in range(2)]; stdb = [Buf(), Buf()]
        wd3 = L.w_down.rearrange("(kc p) n -> p kc n", p=128)
        n = 0
        for cb in range(8):
            for fh in range(2):
                i = n % 2
                n += 1
                k.dma(std[i][:, :, :], wd3[:, fh * 16:(fh + 1) * 16, cb * 128:(cb + 1) * 128], writes=[stdb[i]])
                k.op("dve" if n % 2 == 0 else "pool",
                     lambda e: e.tensor_copy(wdn[:, fh * 16:(fh + 1) * 16, cb * 128:(cb + 1) * 128], std[i][:, :, :]),
                     reads=[stdb[i]], writes=[wdnb[cb][fh]])
        grow = sb(es, nc, "p6_g", [128, D], F32); brow = sb(es, nc, "p6_b", [128, D], F32); rowb = Buf()
        k.dma(grow[:, :], L.ln2_g.partition_broadcast(128), writes=[rowb])
        k.dma(brow[:, :], L.ln2_b.partition_broadcast(128), writes=[rowb])
        xt = [sb(es, nc, "p6_t%d" % i, [128, 8, 256], BF16) for i in range(2)]; xtb = [Buf(), Buf()]
        hT = sb(es, nc, "p6_h", [128, 32, 256], BF16); hTb = [Buf() for _ in range(32)]
        rl = [sb(es, nc, "p6_r%d" % i, [128, 256], BF16) for i in range(4)]; rlb = [Buf() for _ in range(4)]
        xr = [sb(es, nc, "p6_x%d" % i, [128, D], F32) for i in range(2)]; xrb = [Buf(), Buf()]
        z = [sb(es, nc, "p6_z%d" % i, [128, D], F32) for i in range(2)]; zb = [Buf(), Buf()]
        st = sb(es, nc, "p6_st", [128, 2, 6], F32); stb = Buf()
        mv = sb(es, nc, "p6_mv", [128, 4], F32); mvb = Buf()
        tmp = sb(es, nc, "p6_tmp", [128, 1], F32); tmpb = Buf()
        ps = [psb(es, nc, "p6_ps%d" % i, [128, 512], F32) for i in range(8)]
        pb = [Buf() for _ in range(8)]
        nb = 0
        for ti in range(TO // 256):
            t0 = ti * 256
            x_t = xt[ti % 2]
            k.dma(x_t[:, :, :], C.x1T[:, :, t0:t0 + 256], reads=[C.x1T_buf], writes=[xtb[ti % 2]])
            for fc in range(32):
                pi = fc % 4
                for kc in range(8):
                    k.op("pe", lambda e: e.matmul(ps[pi][:, 0:256], wup[:, kc, fc * 128:(fc + 1) * 128], x_t[:, kc, :],
                                                  start=(kc == 0), stop=(kc == 7)),
                         reads=[wupb[fc], xtb[ti % 2]], writes=[pb[pi]], sig=(kc == 7))
                ri = fc % 4
                k.op("act", lambda e: e.activation(out=rl[ri][:, :], in_=ps[pi][:, 0:256], func=AF.Relu),
                     reads=[pb[pi]], writes=[rlb[ri]])
                k.op("dve" if fc % 2 == 0 else "pool",
                     lambda e: e.tensor_tensor(out=hT[:, fc, :], in0=rl[ri][:, :], in1=rl[ri][:, :], op=ALU.mult),
                     reads=[rlb[ri]], writes=[hTb[fc]])
            for blk in range(2):
                i = nb % 2
                nb += 1
                r0 = t0 + blk * 128
                k.dma(xr[i][:, :], C.x1[r0:r0 + 128, :], reads=[C.x1_buf], writes=[xrb[i]])
                for hf in range(2):
                    pi = 4 + i * 2 + hf
                    for fc in range(32):
                        wr = [wdnb[cb][fc // 16] for cb in range(hf * 4, hf * 4 + 4)]
                        k.op("pe", lambda e: e.matmul(ps[pi][:, :], hT[:, fc, blk * 128:(blk + 1) * 128],
                                                      wdn[:, fc, hf * 512:(hf + 1) * 512], start=(fc == 0), stop=(fc == 31)),
                             reads=[hTb[fc]] + wr, writes=[pb[pi]], sig=(fc == 31))
                    k.op("dve", lambda e: e.scalar_tensor_tensor(out=z[i][:, hf * 512:(hf + 1) * 512],
                                                                 in0=xr[i][:, hf * 512:(hf + 1) * 512], scalar=ALPHA,
                                                                 in1=ps[pi][:, :], op0=ALU.mult, op1=ALU.add),
                         reads=[xrb[i], pb[pi]], writes=[zb[i]])
                layer_norm_rows(k, C, z[i], zb[i], st, stb, mv, mvb, tmp, tmpb, grow, brow, rowb, z[i], zb[i])
                k.dma(xdst[r0:r0 + 128, :], z[i][:, :], reads=[zb[i]], writes=[xdst_buf], q="pool")
    k.barrier()


LAYER_W = [("w_in", [D, DIN]), ("b_igate", [4]), ("b_fgate", [4]), ("conv_pw_w", [256, 256]),
           ("colpack", [128, NCP]), ("ml_norm_g", [256]), ("lam_q1", [64]), ("lam_k1", [64]),
           ("lam_q2", [64]), ("lam_k2", [64]), ("da_norm_g", [128]), ("w_out", [D, D]), ("ln1_g", [D]), ("ln1_b", [D]),
           ("w_up", [D, DFF]), ("w_down", [DFF, D]), ("ln2_g", [D]), ("ln2_b", [D])]


def build(layers, phases=None, debug=False, inject=()):
    nc = bass.Bass("TRN2", target_bir_lowering=False)
    C = Ctx()
    skind = "ExternalOutput" if debug else "Internal"
    x_own = nc.dram_tensor("x_own", [TO, D], F32, kind="ExternalInput").ap()
    x_prev = nc.dram_tensor("x_prev", [TO, D], F32, kind="ExternalInput").ap()
    flag_d = nc.dram_tensor("flag", [128, 2], F32, kind="ExternalInput").ap()
    ident_d = nc.dram_tensor("ident", [128, 128], F32, kind="ExternalInput").ap()
    cmask_d = nc.dram_tensor("cmask", [128, 4, 512], F32, kind="ExternalInput").ap()
    C.utri_d = nc.dram_tensor("utri", [128, 128], F32, kind="ExternalInput").ap()
    Ls = []
    for l in layers:
        L = Ctx()
        L.idx = l
        for (nm, shp) in LAYER_W:
            setattr(L, nm, nc.dram_tensor("%s_%d" % (nm, l), shp, F32, kind="ExternalInput").ap())
        Ls.append(L)
    out = nc.dram_tensor("out", [TO, D], F32, kind="ExternalOutput").ap()

    def scratch(name, shape, dt):
        return nc.dram_tensor(name, shape, dt, kind=("ExternalInput" if name in inject else skind)).ap()
    C.xT_all = scratch("xT_all", [128, 8, TA], BF16); C.xT_buf = Buf()
    C.glu = scratch("glu", [128, 2, 512 + TO], BF16); C.glu_buf = Buf()
    C.mqk = scratch("mqk", [128, 4, TA], BF16); C.mqk_buf = Buf()
    C.mv = scratch("mv", [TA, 256], BF16); C.mv_buf = Buf()
    C.mo = scratch("mo", [TO, 256], BF16); C.mo_buf = Buf()
    C.gates = scratch("gates", [TA, 8], F32); C.gates_buf = Buf()
    C.dq = scratch("dq", [128, 4, TO], BF16); C.dq_buf = Buf()
    C.dk = scratch("dk", [128, 4, TA], BF16); C.dk_buf = Buf()
    C.dv = scratch("dv", [TA, 512], BF16); C.dv_buf = Buf()
    C.ymixT = scratch("ymixT", [128, 8, TO], BF16); C.ymix_buf = Buf()
    C.x1 = scratch("x1", [TO, D], F32); C.x1_buf = Buf()
    C.x1T = scratch("x1T", [128, 8, TO], BF16); C.x1T_buf = Buf()
    NCH, CH = 8, 512
    bnc = [nc.dram_tensor("xbnc%d" % j, [CH, D], F32) for j in range(NCH)]
    gth = [nc.dram_tensor("xgth%d" % j, [2 * CH, D], F32) for j in range(NCH)]
    bncb = Buf()
    C.xmid = RowChunks(bnc, CH); C.xmid_buf = Buf()
    C.out = out; C.out_buf = Buf()
    with ExitStack() as es:
        k = KB(nc, es)
        cc_sem = es.enter_context(nc.semaphore("cc_sem"))
        C.ident = sb(es, nc, "ident_sb", [128, 128], F32); C.identb = Buf()
        C.identbf = sb(es, nc, "identbf", [128, 128], BF16); C.identbfb = Buf()
        C.flag = sb(es, nc, "flagsb", [128, 2], F32); C.flagb = Buf()
        k.dma(C.ident[:, :], ident_d[:, :], writes=[C.identb])
        k.dma(C.flag[:, :], flag_d[:, :], writes=[C.flagb])
        k.op("dve", lambda e: e.tensor_copy(C.identbf[:, :], C.ident[:, :]), reads=[C.identb], writes=[C.identbfb])
        C.cmask_d = cmask_d
        ncc = 0
        for li, L in enumerate(Ls):
            if li == 0:
                xo = x_own
                src_prev = [(x_prev, TO, 0)]
            else:
                k.barrier()
                for j in range(NCH):
                    nc.gpsimd.collective_compute("AllGather", ALU.bypass, replica_groups=[[0, 1], [2, 3], [4, 5], [6, 7]],
                                                 ins=[bnc[j].ap().opt()], outs=[gth[j].ap().opt()]).then_inc(cc_sem)
                    ncc += 1
                for e in ("pe", "act", "dve", "pool", "sp"):
                    k.E[e].wait_ge(cc_sem, ncc)
                xo = C.xmid
                src_prev = [(gth[j].ap()[0:CH, :], CH, j * CH) for j in range(NCH)]
            last = li == len(Ls) - 1
            xdst = out if last else C.xmid
            xdst_buf = C.out_buf if last else C.xmid_buf
            ph = phases if phases is not None else [0, 1, 2, 3, 4, 5, 6]
            with ExitStack() as esw:
                PW = Prefetch(k, esw, nc, "pw_in", L.w_in.rearrange("(kc p) n -> p kc n", p=128), 8, DIN, 280, ("dve", "pool"))
                if 0 in ph:
                    phase0(k, nc, C, src_prev + [(xo, TO, TO)], hook=lambda it: PW.step(1) if it % 5 == 0 else None)
                if 1 in ph:
                    phase1(k, nc, C, L, PW)
                k.barrier()
            if 2 in ph:
                phase2(k, nc, C, L)
            if 3 in ph:
                phase3(k, nc, C, L)
            with ExitStack() as esw:
                PU = Prefetch(k, esw, nc, "pw_up", L.w_up.rearrange("(kc p) n -> p kc n", p=128), 8, DFF, 128, ("pool",))
                if 4 in ph:
                    phase4(k, nc, C, L, hook=lambda: PU.step(1))
                if 5 in ph:
                    phase5(k, nc, C, L, xo)
                if 6 in ph:
                    phase6(k, nc, C, L, xdst, xdst_buf, PU)
                k.barrier()
        k.barrier()
        C.ninst = k.ninst
    return nc, C


def make_cmask():
    kk = np.arange(128)[:, None, None]
    jj = np.arange(4)[None, :, None]
    qq = np.arange(512)[None, None, :]
    return np.where(qq >= jj * 128 + kk, 0.0, NEG).astype(np.float32)


def make_colpack(inp, l):
    cp = np.zeros((128, NCP), np.float32)
    def cols(v, n):
        return np.asarray(v, np.float32).reshape(n, 128).T
    cp[:, CP_DWB:CP_DWB + 2] = cols(inp["conv_dw_b"][l], 2)
    cp[:, CP_LNG:CP_LNG + 2] = cols(inp["conv_ln_g"][l], 2)
    cp[:, CP_LNB:CP_LNB + 2] = cols(inp["conv_ln_b"][l], 2)
    cp[:, CP_PWB:CP_PWB + 2] = cols(inp["conv_pw_b"][l], 2)
    cp[:, CP_MLB:CP_MLB + 4] = cols(inp["ml_conv_b"][l], 4)
    w = np.asarray(inp["conv_dw_w"][l], np.float32)
    for j in range(2):
        cp[:, CP_DWW + j * 31:CP_DWW + (j + 1) * 31] = w[:, j * 128:(j + 1) * 128].T
    w = np.asarray(inp["ml_conv_w"][l], np.float32)
    for j in range(4):
        cp[:, CP_MLW + j * 4:CP_MLW + (j + 1) * 4] = w[:, j * 128:(j + 1) * 128].T
    return cp


def core_inputs(inp, x_full, c, layers):
    b, p = c // 2, c % 2
    m = {"x_own": np.ascontiguousarray(x_full[b, p * TO:(p + 1) * TO]),
         "x_prev": np.ascontiguousarray(x_full[b, 0:TO]),
         "flag": np.tile(np.array([[float(p), 0.0 if p == 1 else NEG]], np.float32), (128, 1)),
         "ident": np.eye(128, dtype=np.float32), "cmask": make_cmask(),
         "utri": np.triu(np.ones((128, 128), np.float32))}
    for l in layers:
        for (nm, shp) in LAYER_W:
            if nm == "colpack":
                m["colpack_%d" % l] = make_colpack(inp, l)
            else:
                m["%s_%d" % (nm, l)] = np.ascontiguousarray(inp[nm][l])
    return m


def _run(inp, x_full, layers):
    nc, _ = build(layers)
    in_maps = [core_inputs(inp, x_full, c, layers) for c in range(8)]
    res = run_bass_kernel_spmd(nc, in_maps, core_ids=list(range(8)))
    out = np.empty((4, S, D), np.float32)
    for c in range(8):
        b, p = c // 2, c % 2
        out[b, p * TO:(p + 1) * TO] = np.asarray(res.results[c]["out"], dtype=np.float32)
    return out


def kernel(**inputs):
    inp = {k_: np.asarray(v) for k_, v in inputs.items()}
    x = np.asarray(inp["x"], np.float32)
    return _run(inp, x, list(range(DEPTH)))
```
